# Optimizing a Trainium2 kernel written in Bass

```python
import math
import jax, jax.numpy as jnp
from jax import lax
import numpy as np

D_MODEL = 1024
BATCH = 4
SEQ = 8192
DEPTH = 4

GRID_W = 64
CTX_LEN = 256
N_MIXERS = 2
EPS = 1e-6

RET_HEADS = 4
RET_DK = 256
RET_DV = 512
RET_QK = RET_HEADS * RET_DK
RET_V = RET_HEADS * RET_DV
RET_IN = 2 * RET_QK + 2 * RET_V
RET_CHUNK = 128
ROPE_BASE = 10000.0

LRU_WIDTH = 1280
LRU_BLOCKS = 10
LRU_BW = LRU_WIDTH // LRU_BLOCKS
CONV_W = 4
LRU_C = 8.0

N_RET = (DEPTH + 1) // 2
N_LRU = DEPTH // 2

kernel_name = "hybrid_retention_rglru_prefix_dit"


def rms_norm(t, g):
    tf = t.astype(jnp.float32)
    y = tf * lax.rsqrt(jnp.mean(tf * tf, axis=-1, keepdims=True) + EPS)
    return (y * g.astype(jnp.float32)).astype(t.dtype)


def axial_rope_tables(n_tok):
    n_rows = n_tok // GRID_W
    row = jnp.repeat(jnp.arange(n_rows, dtype=jnp.float32), GRID_W)
    col = jnp.tile(jnp.arange(GRID_W, dtype=jnp.float32), n_rows)
    n_freq = RET_DK // 4
    inv = ROPE_BASE ** (-jnp.arange(n_freq, dtype=jnp.float32) / n_freq)
    ang = jnp.concatenate([row[:, None] * inv, col[:, None] * inv], axis=-1)
    return jnp.cos(ang), jnp.sin(ang)


def apply_rope(t, cos, sin):
    te, to = t[..., 0::2], t[..., 1::2]
    return jnp.stack([te * cos - to * sin, te * sin + to * cos], axis=-1).reshape(t.shape)


def split_heads(t, dh):
    b, n, _ = t.shape
    return t.reshape(b, n, -1, dh).transpose(0, 2, 1, 3).astype(jnp.float32)


def retention_scan(q, k, v, log_g, state0):
    b, h, n_tok, dk = q.shape
    dv = v.shape[-1]
    n_chunk = n_tok // RET_CHUNK
    idx = jnp.arange(RET_CHUNK, dtype=jnp.float32)
    rel = idx[:, None] - idx[None, :]
    intra = jnp.where(rel >= 0, jnp.exp(log_g[:, None, None] * jnp.maximum(rel, 0.0)), 0.0)
    q_dec = jnp.exp(log_g[:, None] * (idx + 1.0))[:, :, None]
    k_dec = jnp.exp(log_g[:, None] * (RET_CHUNK - 1.0 - idx))[:, :, None]
    chunk_dec = jnp.exp(log_g * RET_CHUNK)[:, None, None]

    def chunks(a):
        return jnp.moveaxis(a.reshape(b, h, n_chunk, RET_CHUNK, a.shape[-1]), 2, 0)

    def step(state, qkv):
        qc, kc, vc = qkv
        s = jnp.einsum('bhcd,bhsd->bhcs', qc, kc) * intra
        o = (jnp.einsum('bhcs,bhse->bhce', s, vc)
             + jnp.einsum('bhcd,bhde->bhce', qc * q_dec, state))
        state = state * chunk_dec + jnp.einsum('bhsd,bhse->bhde', kc * k_dec, vc)
        return state, o

    state, o = lax.scan(step, state0, (chunks(q), chunks(k), chunks(v)))
    o = jnp.moveaxis(o, 0, 2).reshape(b, h, n_tok, dv)
    return o, state


def retention_output(o, g, gn_g, w_out):
    mu = jnp.mean(o, axis=-1, keepdims=True)
    var = jnp.mean(jnp.square(o - mu), axis=-1, keepdims=True)
    o = (o - mu) * lax.rsqrt(var + EPS)
    b, h, n_tok, dv = o.shape
    o = o.transpose(0, 2, 1, 3).reshape(b, n_tok, h * dv) * gn_g.astype(jnp.float32)
    return (o * jax.nn.silu(g.astype(jnp.float32))).astype(w_out.dtype) @ w_out


def retention_mixer(h_lat, h_ctx, cos, sin, w_in, log_decay, gn_g, w_out, need_ctx):
    log_g = -jnp.abs(log_decay.astype(jnp.float32))
    scale = RET_DK ** -0.5

    def project(hh):
        p = hh @ w_in
        q = split_heads(p[..., :RET_QK], RET_DK)
        k = split_heads(p[..., RET_QK:2 * RET_QK], RET_DK) * scale
        v = split_heads(p[..., 2 * RET_QK:2 * RET_QK + RET_V], RET_DV)
        g = p[..., 2 * RET_QK + RET_V:]
        return q, k, v, g

    q_l, k_l, v_l, g_l = project(h_lat)
    q_l, k_l = apply_rope(q_l, cos, sin), apply_rope(k_l, cos, sin)
    q_c, k_c, v_c, g_c = project(h_ctx)

    b = h_lat.shape[0]
    zero = jnp.zeros((b, RET_HEADS, RET_DK, RET_DV), jnp.float32)
    flip = lambda a: jnp.flip(a, axis=2)
    oc_f, sc_f = retention_scan(q_c, k_c, v_c, log_g[0], zero)
    oc_b, sc_b = retention_scan(flip(q_c), flip(k_c), flip(v_c), log_g[1], zero)
    ol_f, _ = retention_scan(q_l, k_l, v_l, log_g[0], sc_f)
    ol_b, _ = retention_scan(flip(q_l), flip(k_l), flip(v_l), log_g[1], sc_b)
    y_lat = retention_output(ol_f + flip(ol_b), g_l, gn_g, w_out)
    y_ctx = retention_output(oc_f + flip(oc_b), g_c, gn_g, w_out) if need_ctx else None
    return y_lat, y_ctx


def conv_centred(t, w, bias):
    n_tok = t.shape[1]
    left = CONV_W // 2
    tp = jnp.pad(t, ((0, 0), (left, CONV_W - 1 - left), (0, 0)))
    out = bias.astype(jnp.float32)
    for j in range(CONV_W):
        out = out + tp[:, j:j + n_tok] * w[j].astype(jnp.float32)
    return out


def rglru_coeffs(xc, w_a, b_a, w_x, b_x, lam):
    b, n_tok, wd = xc.shape
    xb = xc.reshape(b, n_tok, LRU_BLOCKS, LRU_BW)
    r = jax.nn.sigmoid(jnp.einsum('btni,nij->btnj', xb, w_a.astype(jnp.float32)).reshape(b, n_tok, wd)
                       + b_a.astype(jnp.float32))
    gi = jax.nn.sigmoid(jnp.einsum('btni,nij->btnj', xb, w_x.astype(jnp.float32)).reshape(b, n_tok, wd)
                        + b_x.astype(jnp.float32))
    log_a = -LRU_C * r * jax.nn.softplus(-lam.astype(jnp.float32))
    a = jnp.exp(log_a)
    u = jnp.sqrt(-jnp.expm1(2.0 * log_a)) * gi * xc
    return a, u


def linear_scan(a, u, h0):
    def comb(e1, e2):
        a1, u1 = e1
        a2, u2 = e2
        return a1 * a2, a2 * u1 + u2
    a_cum, hs = lax.associative_scan(comb, (a, u), axis=1)
    return hs + a_cum * h0[:, None, :]


def lru_mixer(h_lat, h_ctx, w_in, conv_w, conv_b, w_a, b_a, w_x, b_x, lam, w_out, need_ctx):
    def branch(hh):
        p = hh @ w_in
        xr = p[..., :LRU_WIDTH].astype(jnp.float32)
        return conv_centred(xr, conv_w, conv_b), p[..., LRU_WIDTH:]

    xc_l, g_l = branch(h_lat)
    xc_c, g_c = branch(h_ctx)
    b = h_lat.shape[0]
    zero = jnp.zeros((b, LRU_WIDTH), jnp.float32)
    flip = lambda t: jnp.flip(t, axis=1)
    hl_sum = 0.0
    hc_sum = 0.0
    for d in range(2):
        a_c, u_c = rglru_coeffs(xc_c, w_a[d], b_a[d], w_x[d], b_x[d], lam[d])
        a_l, u_l = rglru_coeffs(xc_l, w_a[d], b_a[d], w_x[d], b_x[d], lam[d])
        if d == 1:
            a_c, u_c, a_l, u_l = flip(a_c), flip(u_c), flip(a_l), flip(u_l)
        hc = linear_scan(a_c, u_c, zero)
        hl = linear_scan(a_l, u_l, hc[:, -1])
        if d == 1:
            hc, hl = flip(hc), flip(hl)
        hl_sum = hl_sum + hl
        hc_sum = hc_sum + hc
    y_lat = (hl_sum * jax.nn.silu(g_l.astype(jnp.float32))).astype(w_out.dtype) @ w_out
    y_ctx = ((hc_sum * jax.nn.silu(g_c.astype(jnp.float32))).astype(w_out.dtype) @ w_out) if need_ctx else None
    return y_lat, y_ctx


def setup_inputs(seed: int = 0) -> dict:
    key = jax.random.key(seed)
    ks = jax.random.split(key, 24)
    f32 = jnp.float32
    nrm = lambda k, shape, s: jax.random.normal(k, shape, f32) * s
    x = nrm(ks[0], (BATCH, SEQ, D_MODEL), 1.0)
    c = nrm(ks[1], (BATCH, D_MODEL), 1.0)
    ctx = nrm(ks[2], (BATCH, CTX_LEN, D_MODEL), 1.0)
    c_ctx = nrm(ks[3], (D_MODEL,), 1.0)
    mod_w = nrm(ks[4], (DEPTH, D_MODEL, 3 * D_MODEL), D_MODEL ** -0.5)
    mod_b = nrm(ks[5], (DEPTH, 3 * D_MODEL), 0.02)
    norm_pre = 1.0 + nrm(ks[6], (DEPTH, D_MODEL), 0.02)
    norm_post = 1.0 + nrm(ks[7], (DEPTH, D_MODEL), 0.02)
    ret_w_in = nrm(ks[8], (N_RET, D_MODEL, RET_IN), D_MODEL ** -0.5)
    base = jnp.log1p(-(2.0 ** (-5.0 - jnp.arange(RET_HEADS, dtype=f32))))
    ret_log_decay = base * (1.0 + nrm(ks[9], (N_RET, 2, RET_HEADS), 0.1))
    ret_gn = 1.0 + nrm(ks[10], (N_RET, RET_V), 0.02)
    ret_w_out = nrm(ks[11], (N_RET, RET_V, D_MODEL), RET_V ** -0.5)
    lru_w_in = nrm(ks[12], (N_LRU, D_MODEL, 2 * LRU_WIDTH), D_MODEL ** -0.5)
    lru_conv_w = nrm(ks[13], (N_LRU, CONV_W, LRU_WIDTH), CONV_W ** -0.5)
    lru_conv_b = nrm(ks[14], (N_LRU, LRU_WIDTH), 0.01)
    lru_w_a = nrm(ks[15], (N_LRU, 2, LRU_BLOCKS, LRU_BW, LRU_BW), LRU_BW ** -0.5)
    lru_b_a = nrm(ks[16], (N_LRU, 2, LRU_WIDTH), 0.01)
    lru_w_x = nrm(ks[17], (N_LRU, 2, LRU_BLOCKS, LRU_BW, LRU_BW), LRU_BW ** -0.5)
    lru_b_x = nrm(ks[18], (N_LRU, 2, LRU_WIDTH), 0.01)
    a_pow_c = jax.random.uniform(ks[19], (N_LRU, 2, LRU_WIDTH), f32, 0.9, 0.999)
    a_base = a_pow_c ** (1.0 / LRU_C)
    lru_lambda = jnp.log(a_base) - jnp.log1p(-a_base)
    lru_w_out = nrm(ks[20], (N_LRU, LRU_WIDTH, D_MODEL), LRU_WIDTH ** -0.5)
    return {"x": x, "c": c, "ctx": ctx, "c_ctx": c_ctx,
            "mod_w": mod_w, "mod_b": mod_b, "norm_pre": norm_pre, "norm_post": norm_post,
            "ret_w_in": ret_w_in, "ret_log_decay": ret_log_decay, "ret_gn": ret_gn, "ret_w_out": ret_w_out,
            "lru_w_in": lru_w_in, "lru_conv_w": lru_conv_w, "lru_conv_b": lru_conv_b,
            "lru_w_a": lru_w_a, "lru_b_a": lru_b_a, "lru_w_x": lru_w_x, "lru_b_x": lru_b_x,
            "lru_lambda": lru_lambda, "lru_w_out": lru_w_out}


def reference(x, c, ctx, c_ctx, mod_w, mod_b, norm_pre, norm_post,
              ret_w_in, ret_log_decay, ret_gn, ret_w_out,
              lru_w_in, lru_conv_w, lru_conv_b, lru_w_a, lru_b_a, lru_w_x, lru_b_x,
              lru_lambda, lru_w_out):
    n_tok = x.shape[1]
    cos, sin = axial_rope_tables(n_tok)
    s_ctx = ctx
    act_l = jax.nn.silu(c)
    act_c = jax.nn.silu(c_ctx)
    for i in range(DEPTH):
        need_ctx = i < DEPTH - 1
        shift_l, scale_l, gate_l = jnp.split(act_l @ mod_w[i] + mod_b[i], 3, axis=-1)
        shift_c, scale_c, gate_c = jnp.split(act_c @ mod_w[i] + mod_b[i], 3, axis=-1)
        h_l = rms_norm(x, norm_pre[i]) * (1.0 + scale_l[:, None, :]) + shift_l[:, None, :]
        h_c = rms_norm(s_ctx, norm_pre[i]) * (1.0 + scale_c) + shift_c
        j = i // N_MIXERS
        if i % N_MIXERS == 0:
            y_l, y_c = retention_mixer(h_l, h_c, cos, sin, ret_w_in[j], ret_log_decay[j],
                                       ret_gn[j], ret_w_out[j], need_ctx)
        else:
            y_l, y_c = lru_mixer(h_l, h_c, lru_w_in[j], lru_conv_w[j], lru_conv_b[j],
                                 lru_w_a[j], lru_b_a[j], lru_w_x[j], lru_b_x[j],
                                 lru_lambda[j], lru_w_out[j], need_ctx)
        x = x + gate_l[:, None, :] * rms_norm(y_l, norm_post[i])
        if need_ctx:
            s_ctx = s_ctx + gate_c * rms_norm(y_c, norm_post[i])
    return x
```

```python
import contextlib
import numpy as np
import concourse.bass as bass
import concourse.mybir as mybir
from concourse.bass_utils import run_bass_kernel_spmd

F32 = mybir.dt.float32
BF16 = mybir.dt.bfloat16
AF = mybir.ActivationFunctionType
ALU = mybir.AluOpType
AX = mybir.AxisListType

D = 1024
DEPTH = 4
SEQ = 8192
BATCH = 4
CTX = 256
GRID_W = 64
EPS = 1e-6
RET_H = 4
LRU_W = 1280
NCH = 10
PAIR = True
NCORES = 8 if PAIR else 4
NL = SEQ // 2 if PAIR else SEQ
NTOK = CTX + NL
NCHUNK = NTOK // 128
NCC = CTX // 128


class Sem:
    def __init__(self, handle, stream, cls, step):
        self.h = handle
        self.stream = stream
        self.cls = cls
        self.step = step
        self.count = 0


class Buf:
    __slots__ = ("name", "w", "r")

    def __init__(self, name=""):
        self.name = name
        self.w = None
        self.r = {}


class Stream:
    def __init__(self, name):
        self.name = name
        self.items = []
        self.waited = {}


class Sched:
    NDMA = 14

    def __init__(self, nc, es):
        self.nc = nc
        self.es = es
        self.streams = {n: Stream(n) for n in ("pe", "act", "dve", "pool", "sp")}
        self.sems = {}
        self.all_sems = []
        self.nsem = 0
        self.dma_sems = {}
        self.dma_rr = {}
        for st in ("pool", "sp"):
            self.dma_sems[st] = [self._new_sem(st, "d", 16) for _ in range(self.NDMA)]
            self.dma_rr[st] = 0
        self.rotate()

    def _new_sem(self, st, cls, step):
        h = self.es.enter_context(self.nc.semaphore(f"s{self.nsem}_{st}_{cls}"))
        self.nsem += 1
        sm = Sem(h, st, cls, step)
        self.all_sems.append(sm)
        return sm

    def rotate(self):
        for st in ("pe", "act", "dve", "pool"):
            self.sems[(st, "c")] = self._new_sem(st, "c", 1)

    def add(self, stream, cls, fns, reads=(), writes=()):
        st = self.streams[stream]
        need = {}
        if cls == "d":
            i = self.dma_rr[stream]
            self.dma_rr[stream] = (i + 1) % self.NDMA
            sm = self.dma_sems[stream][i]
            if sm.count > 0 and st.waited.get(sm, 0) < sm.count:
                need[sm] = sm.count
        elif cls == "cc":
            sm = self._new_sem(stream, "cc", 1)
        else:
            sm = self.sems[(stream, cls)]
        raw = set()
        oth = set()
        for b in reads:
            if b.w is not None:
                raw.add(b.w)
        for b in writes:
            if b.w is not None:
                oth.add(b.w)
            for k, v in b.r.items():
                oth.add((k, v))
        for (dsm, v) in raw | oth:
            if dsm.stream == stream:
                if stream == "pe":
                    continue
                if dsm.cls == "c" and cls == "c" and (dsm, v) not in raw:
                    continue
            if st.waited.get(dsm, 0) >= v:
                continue
            if need.get(dsm, 0) < v:
                need[dsm] = v
        for k, v in need.items():
            st.waited[k] = v
        sm.count += sm.step
        tok = (sm, sm.count)
        st.items.append((list(need.items()), fns, sm))
        for b in reads:
            if b.r.get(sm, 0) < sm.count:
                b.r[sm] = sm.count
        for b in writes:
            b.w = tok
            b.r = {}
        return tok

    def barrier(self):
        for st in self.streams.values():
            need = []
            for sm in self.all_sems:
                if sm.count > 0 and st.waited.get(sm, 0) < sm.count:
                    if sm.stream == st.name and st.name == "pe":
                        continue
                    need.append((sm, sm.count))
                    st.waited[sm] = sm.count
            if need:
                st.items.append((need, [], None))

    def final_wait(self, stream="sp"):
        st = self.streams[stream]
        need = [(sm, sm.count) for sm in self.all_sems if sm.count > 0 and st.waited.get(sm, 0) < sm.count]
        st.items.append((need, [], None))

    def emit(self, stream, eng):
        for waits, fns, sm in self.streams[stream].items:
            for (wsm, v) in waits:
                eng.wait_ge(wsm.h, v)
            ins = None
            for f in fns:
                ins = f(eng)
            if ins is not None and sm is not None:
                ins.then_inc(sm.h, sm.step)


class Ring:
    _uid = [0]

    def __init__(self, nc, es, name, shape, dtype, n):
        Ring._uid[0] += 1
        name = f"{name}u{Ring._uid[0]}_"
        self.t = [es.enter_context(nc.sbuf_tensor(f"{name}{i}", shape, dtype)) for i in range(n)]
        self.b = [Buf(f"{name}{i}") for i in range(n)]
        self.i = 0

    def next(self):
        i = self.i
        self.i = (i + 1) % len(self.t)
        return self.t[i], self.b[i]


def build_program(depth=DEPTH, ncores=NCORES, dbg=False):
    nc = bass.Bass("TRN2", target_bir_lowering=False)
    dt_in = lambda n, s: nc.dram_tensor(n, s, F32, kind="ExternalInput").ap()
    xin = dt_in("xin", [NTOK, D])
    ropec = dt_in("ropec", [NTOK, 128])
    ropes = dt_in("ropes", [NTOK, 128])
    c_col = dt_in("c_col", [128, 16])
    mod_w = dt_in("mod_w", [DEPTH, D, 3 * D])
    mod_b_col = dt_in("mod_b_col", [128, DEPTH * 24])
    npre_col = dt_in("npre_col", [128, DEPTH * 8])
    npost_col = dt_in("npost_col", [128, DEPTH * 8])
    ret_w_in = dt_in("ret_w_in", [2, D, 6144])
    ret_lg = dt_in("ret_lg", [128, 16])
    ret_gn_col = dt_in("ret_gn_col", [128, 32])
    ret_w_out = dt_in("ret_w_out", [2, 2048, D])
    lru_w_in = dt_in("lru_w_in", [2, D, 2 * LRU_W])
    lru_cw = dt_in("lru_cw", [128, 2 * NCH * 5])
    lru_cb = dt_in("lru_cb", [128, 2 * NCH])
    lru_wa = dt_in("lru_wa", [2, 2, NCH, 128, 128])
    lru_wx = dt_in("lru_wx", [2, 2, NCH, 128, 128])
    lru_ba = dt_in("lru_ba", [128, 2 * 2 * NCH])
    lru_bx = dt_in("lru_bx", [128, 2 * 2 * NCH])
    lru_lam = dt_in("lru_lam", [128, 2 * 2 * NCH])
    lru_w_out = dt_in("lru_w_out", [2, LRU_W, D])
    consts = dt_in("consts", [128, 128 * 5 + 16])
    out = nc.dram_tensor("out", [NL, D], F32, kind="ExternalOutput").ap()
    dbg_ctx = nc.dram_tensor("dbg_ctx", [CTX, D], F32, kind="ExternalOutput").ap() if dbg else None
    rgroups = [[2 * i, 2 * i + 1] for i in range(ncores // 2)]
    dumps = {}
    DUMP_B = Buf("dump")

    dram = lambda n, s, d=F32: nc.dram_tensor(n, s, d)
    Xs = [dram("Xs0", [NTOK, D]), dram("Xs1", [NTOK, D])]
    QTd = dram("QTd", [NCHUNK, 128, 1024], BF16)
    KTd = dram("KTd", [NCHUNK, 128, 1024], BF16)
    Krd = dram("Krd", [NCHUNK, 128, 1024], BF16)
    Vd = dram("Vd", [NCHUNK, 128, 2048], BF16)
    O1d = dram("O1d", [NCHUNK, 128, 2048])
    XRc = dram("XRc", [NCH, 128, CTX + 4])
    XRl = dram("XRl", [NCH, 128, NL + 4])
    XCc = dram("XCc", [NCH, 128, CTX])
    XCl = dram("XCl", [NCH, 128, NL])
    H1c = dram("H1c", [NCH, 128, CTX])
    H1l = dram("H1l", [NCH, 128, NL])
    SGc = dram("SGc", [NCH, 128, CTX], BF16)
    SGl = dram("SGl", [NCH, 128, NL], BF16)
    cc_state_in = [dram(f"ccsi{j}", [128, 4096]) for j in range(2)]
    cc_state_out = [dram(f"ccso{j}", [256, 4096]) for j in range(2)]
    cc_small_in = [dram(f"ccmi{j}", [128, 32]) for j in range(4)]
    cc_small_out = [dram(f"ccmo{j}", [256, 32]) for j in range(4)]

    es = contextlib.ExitStack()
    with es:
        S = Sched(nc, es)
        sb = lambda n, s, d=F32: es.enter_context(nc.sbuf_tensor(n, s, d))
        W = sb("W", [128, 8 * 4096], BF16)
        Wb = [Buf(f"W{i}") for i in range(8)]
        wslot = lambda s: W[:, s * 4096:(s + 1) * 4096]
        cst = sb("cst", [128, 128 * 5 + 16])
        cst_b = Buf("cst")
        ident_f = cst[:, 0:128]
        ones_f = cst[:, 128:256]
        tri1 = cst[:, 256:384]
        tri2 = cst[:, 384:512]
        zeros_f = cst[:, 512:640]
        coef = cst[:, 640:647]
        eps_col = cst[:, 647:648]
        one_col = cst[:, 648:649]
        sel0 = cst[:, 649:650]
        sel1 = cst[:, 650:651]
        ident_b = sb("ident_b", [128, 128], BF16)
        ident_bb = Buf("ident_b")
        ccol = sb("ccol", [128, 16])
        act_bf = sb("act_bf", [128, 16], BF16)
        act_b = Buf("act")
        small = sb("small", [128, DEPTH * 24 + DEPTH * 16 + 16 + 32 + 100 + 120])
        small_b = Buf("small")
        o = 0
        modb_t = small[:, o:o + DEPTH * 24]; o += DEPTH * 24
        npre_t = small[:, o:o + DEPTH * 8]; o += DEPTH * 8
        npost_t = small[:, o:o + DEPTH * 8]; o += DEPTH * 8
        lg_t = small[:, o:o + 16]; o += 16
        gn_t = small[:, o:o + 32]; o += 32
        cw_t = small[:, o:o + 100]; o += 100
        cb_t = small[:, o:o + 20]; o += 20
        ba_t = small[:, o:o + 40]; o += 40
        bx_t = small[:, o:o + 40]; o += 40
        lam_t = sb("lam_t", [128, 40])
        modT = sb("modT", [128, 48])
        modT_b = Buf("modT")
        A_col = sb("A_col", [128, 16])
        G2col = sb("G2col", [128, 16])
        col_b = Buf("cols")
        G2bc = [sb("G2bc0", [128, D]), sb("G2bc1", [128, D])]
        G2bc_b = Buf("G2bc")
        ps_proj = [es.enter_context(nc.psum_tensor(f"ps_proj{i}", [128, 512], F32)) for i in range(2)]
        ps_proj_b = [Buf("pp0"), Buf("pp1")]
        ps_tr = es.enter_context(nc.psum_tensor("ps_tr", [128, 1024], BF16))
        ps_tr_b = Buf("ptr")
        ps_st = es.enter_context(nc.psum_tensor("ps_st", [128, 512], F32))
        ps_st_b = Buf("pst")
        ps_o = [es.enter_context(nc.psum_tensor(f"ps_o{i}", [128, 512], F32)) for i in range(2)]
        ps_o_b = [Buf("po0"), Buf("po1")]
        ps_su = [es.enter_context(nc.psum_tensor(f"ps_su{i}", [128, 512], F32)) for i in range(2)]
        ps_su_b = [Buf("psu0"), Buf("psu1")]

        def dma_sp(out_ap, in_ap, reads, writes):
            S.add("sp", "d", [lambda e: e.dma_start(out=out_ap, in_=in_ap)], reads, writes)

        def dma_pool(out_ap, in_ap, reads, writes, slow=False):
            if slow:
                S.add("pool", "d", [lambda e: e.dma_start(out=out_ap, in_=in_ap, allow_slow_non_contiguous=True)], reads, writes)
            else:
                S.add("pool", "d", [lambda e: e.dma_start(out=out_ap, in_=in_ap)], reads, writes)

        def act(out_ap, in_ap, func, reads, writes, scale=None, bias=None):
            kw = {}
            if scale is not None:
                kw["scale"] = scale
            if bias is not None:
                kw["bias"] = bias
            S.add("act", "c", [lambda e: e.activation(out=out_ap, in_=in_ap, func=func, **kw)], reads, writes)

        def ts(stream, out_ap, in_ap, s1, s2, op0, op1, reads, writes):
            if op1 is None:
                S.add(stream, "c", [lambda e: e.tensor_scalar(out=out_ap, in0=in_ap, scalar1=s1, scalar2=None, op0=op0)], reads, writes)
            else:
                S.add(stream, "c", [lambda e: e.tensor_scalar(out=out_ap, in0=in_ap, scalar1=s1, scalar2=s2, op0=op0, op1=op1)], reads, writes)

        def tt(stream, out_ap, a, b, op, reads, writes):
            S.add(stream, "c", [lambda e: e.tensor_tensor(out=out_ap, in0=a, in1=b, op=op)], reads, writes)

        def stt(out_ap, in0, scalar, in1, op0, op1, reads, writes):
            S.add("dve", "c", [lambda e: e.scalar_tensor_tensor(out=out_ap, in0=in0, scalar=scalar, in1=in1, op0=op0, op1=op1)], reads, writes)

        def copy(stream, out_ap, in_ap, reads, writes):
            if stream == "act":
                S.add("act", "c", [lambda e: e.copy(out=out_ap, in_=in_ap)], reads, writes)
            else:
                S.add(stream, "c", [lambda e: e.tensor_copy(out=out_ap, in_=in_ap)], reads, writes)

        def mm_group(specs, reads, writes):
            fns = []
            for (o_, l_, r_, st_, sp_) in specs:
                fns.append(lambda e, o_=o_, l_=l_, r_=r_, st_=st_, sp_=sp_: e.matmul(o_, lhsT=l_, rhs=r_, start=st_, stop=sp_))
            S.add("pe", "c", fns, reads, writes)

        def tr_group(specs, reads, writes):
            fns = []
            for (o_, i_) in specs:
                fns.append(lambda e, o_=o_, i_=i_: e.transpose(o_, i_, ident_b[:]))
            S.add("pe", "c", fns, list(reads) + [ident_bb], writes)

        def dump(name, ap, buf, F):
            if not dbg:
                return
            t = nc.dram_tensor("dmp_" + name, [128, F], F32, kind="ExternalOutput").ap()
            dumps[name] = t
            dma_pool(t[:, :], ap, [buf], [DUMP_B])

        dma_sp(cst[:], consts[:, :], [], [cst_b])
        dma_pool(ident_b[:], consts[:, 0:128], [], [ident_bb])
        dma_sp(ccol[:], c_col[:, :], [], [act_b])
        dma_sp(modb_t, mod_b_col[:, :], [], [small_b])
        dma_sp(npre_t, npre_col[:, :], [], [small_b])
        dma_sp(npost_t, npost_col[:, :], [], [small_b])
        dma_sp(lg_t, ret_lg[:, :], [], [small_b])
        dma_sp(gn_t, ret_gn_col[:, :], [], [small_b])
        dma_sp(cw_t, lru_cw[:, :], [], [small_b])
        dma_sp(cb_t, lru_cb[:, :], [], [small_b])
        dma_sp(ba_t, lru_ba[:, :], [], [small_b])
        dma_sp(bx_t, lru_bx[:, :], [], [small_b])
        dma_sp(lam_t[:], lru_lam[:, :], [], [small_b])
        act(act_bf[:], ccol[:], AF.Silu, [act_b], [act_b])

        def prep_layer(l, les):
            lsb = lambda n, s, d=F32: les.enter_context(nc.sbuf_tensor(n, s, d))
            dg = Ring(nc, les, f"dg{l}_", [128, 128], F32, 2)
            actv = act_bf[:].rearrange("p (k v) -> p k v", v=2)
            for cg in range(6):
                s = cg % 2
                wv = wslot(s).rearrange("p (k c) -> p k c", k=8)
                dma_pool(wv, mod_w[l][:, cg * 512:(cg + 1) * 512].rearrange("(k p) c -> p k c", p=128), [], [Wb[s]])
                specs = []
                for cc in range(4):
                    ck = cg * 4 + cc
                    for k in range(8):
                        specs.append((ps_st[:, ck * 2:ck * 2 + 2], wv[:, k, cc * 128:(cc + 1) * 128], actv[:, k, :], k == 0, k == 7))
                mm_group(specs, [Wb[s], act_b], [ps_st_b])
            tt("dve", modT[:].rearrange("p (c v) -> p c v", v=2), ps_st[:, 0:48].rearrange("p (c v) -> p c v", v=2),
               modb_t[:, l * 24:(l + 1) * 24].unsqueeze(2).to_broadcast([128, 24, 2]), ALU.add, [ps_st_b, small_b], [modT_b])
            npre_bc = npre_t[:, l * 8:(l + 1) * 8].unsqueeze(2).to_broadcast([128, 8, 2])
            npost_bc = npost_t[:, l * 8:(l + 1) * 8].unsqueeze(2).to_broadcast([128, 8, 2])
            v3 = lambda t: t.rearrange("p (c v) -> p c v", v=2)
            stt(v3(A_col[:]), v3(modT[:, 16:32]), 1.0, npre_bc, ALU.add, ALU.mult, [modT_b, small_b], [col_b])
            tt("dve", v3(G2col[:]), v3(modT[:, 32:48]), npost_bc, ALU.mult, [modT_b, small_b, col_b], [col_b])
            for v in range(2):
                for half in range(2):
                    specs = []
                    dbs = []
                    for kk in range(4):
                        k = half * 4 + kk
                        dt_, db_ = dg.next()
                        ts("dve", dt_[:], ident_f, G2col[:, k * 2 + v:k * 2 + v + 1], None, ALU.mult, None, [cst_b, col_b], [db_])
                        mm_group([(ps_o[half][:, kk * 128:(kk + 1) * 128], ones_f, dt_[:], True, True)], [db_, cst_b], [ps_o_b[half]])
                    copy("act", G2bc[v][:, half * 512:(half + 1) * 512], ps_o[half][:], [ps_o_b[half]], [G2bc_b])

        def norm_chunk(R, src_ap, src_buf, row0, v, hT_ap_fn, hT_buf, keep_x=False):
            xt, xb = R["x"].next()
            dma_sp(xt[:], src_ap[row0:row0 + 128, :], [src_buf], [xb])
            jt, jb = R["junk"].next()
            st_t, st_b = R["stat"].next()
            act(jt[:], xt[:], AF.Square, [xb], [jb])
            S.add("dve", "c", [lambda e: e.tensor_reduce(out=st_t[:, 0:1], in_=jt[:], axis=AX.X, op=ALU.add)], [jb], [st_b])
            act(st_t[:, 1:2], st_t[:, 0:1], AF.Sqrt, [st_b, cst_b], [st_b], scale=1.0 / D, bias=eps_col)
            S.add("dve", "c", [lambda e: e.reciprocal(out=st_t[:, 2:3], in_=st_t[:, 1:2])], [st_b], [st_b])
            xh, xhb = R["xhat"].next()
            ts("dve", xh[:], xt[:], st_t[:, 2:3], None, ALU.mult, None, [xb, st_b], [xhb])
            tr_group([(ps_tr[:, k * 128:(k + 1) * 128], xh[:, k * 128:(k + 1) * 128]) for k in range(8)], [xhb], [ps_tr_b])
            for k in range(8):
                act(hT_ap_fn(k), ps_tr[:, k * 128:(k + 1) * 128], AF.Identity, [ps_tr_b, col_b, modT_b], [hT_buf],
                    scale=A_col[:, k * 2 + v:k * 2 + v + 1], bias=modT[:, k * 2 + v:k * 2 + v + 1])
            return xt, xb

        def post_chunk(R, xt, xb, v, dst_ap, dst_buf, dst_row0):
            jt, jb = R["junk"].next()
            st_t, st_b = R["stat"].next()
            for g in range(2):
                act(jt[:, g * 512:(g + 1) * 512], ps_proj[g][:], AF.Square, [ps_proj_b[g]], [jb])
            S.add("dve", "c", [lambda e: e.tensor_reduce(out=st_t[:, 0:1], in_=jt[:], axis=AX.X, op=ALU.add)], [jb], [st_b])
            act(st_t[:, 1:2], st_t[:, 0:1], AF.Sqrt, [st_b, cst_b], [st_b], scale=1.0 / D, bias=eps_col)
            S.add("dve", "c", [lambda e: e.reciprocal(out=st_t[:, 2:3], in_=st_t[:, 1:2])], [st_b], [st_b])
            tm, tmb = R["tmp"].next()
            xn, xnb = R["xn"].next()
            for g in range(2):
                stt(tm[:, g * 512:(g + 1) * 512], ps_proj[g][:], st_t[:, 2:3], G2bc[v][:, g * 512:(g + 1) * 512],
                    ALU.mult, ALU.mult, [ps_proj_b[g], st_b, G2bc_b], [tmb])
            tt("pool", xn[:], tm[:], xt[:], ALU.add, [tmb, xb], [xnb])
            if dst_row0 == CTX and v == 0 and "tm" not in dumps:
                dump("tm", tm[:], tmb, 1024); dump("ysq", jt[:], jb, 1024); dump("stat", st_t[:], st_b, 4)
            dma_pool(dst_ap[dst_row0:dst_row0 + 128, :], xn[:], [xnb], [dst_buf])

        def exchange_start(idx_big, src_ap, src_buf, F):
            if F > 32:
                cin, cout = cc_state_in[idx_big], cc_state_out[idx_big]
            else:
                cin, cout = cc_small_in[idx_big], cc_small_out[idx_big]
            cb = Buf("ccin")
            cob = Buf("ccout")
            dma_pool(cin[:, 0:F], src_ap, [src_buf], [cb])
            S.add("pool", "cc", [lambda e: e.collective_compute(
                "AllGather", ALU.bypass, replica_groups=rgroups,
                ins=[cin[:, :]], outs=[cout[:, :]])], [cb], [cob])
            return cout, cob

        def exchange_finish(R, cout, cob, dst_ap, dst_buf, F):
            step = min(F, 1024)
            for p0 in range(0, F, step):
                g0, g0b = R["xg"].next()
                g1, g1b = R["xg"].next()
                dma_sp(g0[:, 0:step], cout[0:128, p0:p0 + step], [cob], [g0b])
                dma_sp(g1[:, 0:step], cout[128:256, p0:p0 + step], [cob], [g1b])
                ts("dve", g0[:, 0:step], g0[:, 0:step], sel0, None, ALU.mult, None, [g0b, cst_b], [g0b])
                stt(dst_ap[:, p0:p0 + step], g1[:, 0:step], sel1, g0[:, 0:step], ALU.mult, ALU.add, [g0b, g1b, cst_b], [dst_buf])

        def retention_layer(l, src_ap, src_buf, dst_ap, dst_buf, last):
            j = l // 2
            les = contextlib.ExitStack()
            with les:
                lsb = lambda n, s, d=F32: les.enter_context(nc.sbuf_tensor(f"{n}_L{l}", s, d))
                prep_layer(l, les)
                state = lsb("state", [128, 4096])
                state_bf = lsb("state_bf", [128, 4096], BF16)
                state_b = Buf("state")
                statebf_b = Buf("state_bf")
                mask = [lsb("mask1", [128, 512]), lsb("mask2", [128, 512])]
                tab = lsb("tab", [128, 2 * 28])
                lgn = lsb("lgn", [128, 8])
                tab_b = Buf("tab")
                mask_b = Buf("mask")
                ts("dve", tab[:, 0:8], lg_t[:, j * 8:(j + 1) * 8], -1.0, None, ALU.mult, None, [small_b], [tab_b])
                tt("dve", lgn[:], lg_t[:, j * 8:(j + 1) * 8], tab[:, 0:8], ALU.min, [small_b, tab_b], [tab_b])
                for d in range(2):
                    for i in range(7):
                        ts("dve", tab[:, d * 28 + i * 4:d * 28 + i * 4 + 4], lgn[:, d * 4:(d + 1) * 4], coef[:, i:i + 1], None,
                           ALU.mult, None, [tab_b, cst_b], [tab_b])
                act(tab[:], tab[:], AF.Exp, [tab_b], [tab_b])
                for d in range(2):
                    for i in ((0, 2) if d == 0 else (3, 5)):
                        ts("dve", tab[:, d * 28 + i * 4:d * 28 + i * 4 + 4], tab[:, d * 28 + i * 4:d * 28 + i * 4 + 4], 0.0625, None,
                           ALU.mult, None, [tab_b], [tab_b])
                T = lambda d, i, h: tab[:, d * 28 + i * 4 + h:d * 28 + i * 4 + h + 1]
                for d in range(2):
                    for h in range(4):
                        ts("dve", mask[d][:, h * 128:(h + 1) * 128], tri1 if d == 0 else tri2, T(d, 2 if d == 0 else 5, h), None,
                           ALU.mult, None, [tab_b, cst_b], [mask_b])
                KD = (0, 3)
                QD = (1, 4)

                def chunk_rows(n):
                    return n * 128, (1 if n < NCC else 0)

                def scan_part(R, d, n, QT, QTb, KT, KTb, Kr, Krb, V, Vb, o_t, o_b, o1_t, o1_b, need_o):
                    if need_o:
                        specs = []
                        for h in range(4):
                            for dc in range(2):
                                specs.append((ps_st[:, h * 128:(h + 1) * 128], KT[:, (2 * h + dc) * 128:(2 * h + dc + 1) * 128],
                                              QT[:, (2 * h + dc) * 128:(2 * h + dc + 1) * 128], dc == 0, dc == 1))
                        mm_group(specs, [KTb, QTb], [ps_st_b])
                        P, Pb = R["P"].next()
                        tt("dve", P[:], ps_st[:], mask[d][:], ALU.mult, [ps_st_b, mask_b], [Pb])
                    Kd, Kdb = R["Kdec"].next()
                    for h in range(4):
                        ts("pool", Kd[:, h * 256:(h + 1) * 256], Kr[:, h * 256:(h + 1) * 256], T(d, KD[d], h), None, ALU.mult, None,
                           [Krb, tab_b], [Kdb])
                    for h in range(4):
                        if need_o:
                            po, pob = ps_o[h % 2], ps_o_b[h % 2]
                            specs = [(po[:], P[:, h * 128:(h + 1) * 128], V[:, h * 512:(h + 1) * 512], True, False)]
                            for dc in range(2):
                                specs.append((po[:], QT[:, (2 * h + dc) * 128:(2 * h + dc + 1) * 128],
                                              state_bf[:, (h * 2 + dc) * 512:(h * 2 + dc + 1) * 512], False, dc == 1))
                            mm_group(specs, [Pb, Vb, QTb, statebf_b], [pob])
                            if d == 0:
                                act(o_t[:, h * 512:(h + 1) * 512], po[:], AF.Identity, [pob, tab_b], [o_b], scale=T(d, QD[d], h))
                            else:
                                stt(o_t[:, h * 512:(h + 1) * 512], po[:], T(d, QD[d], h), o1_t[:, h * 512:(h + 1) * 512],
                                    ALU.mult, ALU.add, [pob, tab_b, o1_b], [o_b])
                        for dc in range(2):
                            mm_group([(ps_su[dc][:], Kd[:, h * 256 + dc * 128:h * 256 + (dc + 1) * 128], V[:, h * 512:(h + 1) * 512], True, True)],
                                     [Kdb, Vb], [ps_su_b[dc]])
                            sl = slice((h * 2 + dc) * 512, (h * 2 + dc + 1) * 512)
                            stt(state[:, sl], state[:, sl], T(d, 6, h), ps_su[dc][:], ALU.mult, ALU.add,
                                [ps_su_b[dc], tab_b, state_b], [state_b])
                        sl = slice(h * 1024, (h + 1) * 1024)
                        copy("pool", state_bf[:, sl], state[:, sl], [state_b], [statebf_b])

                def zero_state():
                    S.add("dve", "c", [lambda e: e.memset(state[:], 0.0)], [], [state_b])
                    S.add("pool", "c", [lambda e: e.memset(state_bf[:], 0.0)], [], [statebf_b])

                Qb_d = [Buf(f"QTd{n}") for n in range(NCHUNK)]
                Kb_d = [Buf(f"KTd{n}") for n in range(NCHUNK)]
                Krb_d = [Buf(f"Krd{n}") for n in range(NCHUNK)]
                Vb_d = [Buf(f"Vd{n}") for n in range(NCHUNK)]
                O1b_d = [Buf(f"O1d{n}") for n in range(NCHUNK)]

                pes = contextlib.ExitStack()
                with pes:
                    R = {
                        "x": Ring(nc, pes, "r1x", [128, D], F32, 2), "junk": Ring(nc, pes, "r1j", [128, D], F32, 1),
                        "stat": Ring(nc, pes, "r1s", [128, 4], F32, 2), "xhat": Ring(nc, pes, "r1xh", [128, D], BF16, 2),
                        "hT": Ring(nc, pes, "r1hT", [128, D], BF16, 2), "cs": Ring(nc, pes, "r1cs", [128, 256], F32, 2),
                        "qkf": Ring(nc, pes, "r1qkf", [128, 512], F32, 2), "rt": Ring(nc, pes, "r1rt", [128, 1024], F32, 2),
                        "Qr": Ring(nc, pes, "r1Qr", [128, D], BF16, 2), "Kr": Ring(nc, pes, "r1Kr", [128, D], BF16, 2),
                        "QT": Ring(nc, pes, "r1QT", [128, D], BF16, 2), "KT": Ring(nc, pes, "r1KT", [128, D], BF16, 2),
                        "V": Ring(nc, pes, "r1V", [128, 2048], BF16, 2), "P": Ring(nc, pes, "r1P", [128, 512], BF16, 2),
                        "Kdec": Ring(nc, pes, "r1Kd", [128, D], BF16, 2), "o1": Ring(nc, pes, "r1o1", [128, 2048], F32, 2),
                    }
                    for cg in range(8):
                        dma_pool(wslot(cg).rearrange("p (k c) -> p k c", k=8),
                                 ret_w_in[j][:, cg * 512:(cg + 1) * 512].rearrange("(k p) c -> p k c", p=128), [], [Wb[cg]])
                    zero_state()
                    for n in range(NCHUNK):
                        row0, v = chunk_rows(n)
                        hT, hTb = R["hT"].next()
                        norm_chunk(R, src_ap, src_buf, row0, v, lambda k, hT=hT: hT[:, k * 128:(k + 1) * 128], hTb)
                        cs, csb = R["cs"].next()
                        dma_sp(cs[:, 0:128], ropec[row0:row0 + 128, :], [], [csb])
                        dma_sp(cs[:, 128:256], ropes[row0:row0 + 128, :], [], [csb])
                        cosb = cs[:, 0:128].unsqueeze(1).to_broadcast([128, 2, 128])
                        sinb = cs[:, 128:256].unsqueeze(1).to_broadcast([128, 2, 128])
                        Qr, Qrb = R["Qr"].next()
                        Kr, Krb = R["Kr"].next()
                        V, Vb = R["V"].next()
                        for cg in range(8):
                            pp, ppb = ps_proj[cg % 2], ps_proj_b[cg % 2]
                            wv = wslot(cg).rearrange("p (k c) -> p k c", k=8)
                            mm_group([(pp[:], hT[:, k * 128:(k + 1) * 128], wv[:, k, :], k == 0, k == 7) for k in range(8)],
                                     [hTb, Wb[cg]], [ppb])
                            if cg < 4:
                                dst, dstb = (Qr, Qrb) if cg < 2 else (Kr, Krb)
                                eng = "dve" if cg < 2 else "pool"
                                qf, qfb = R["qkf"].next()
                                copy("act", qf[:], pp[:], [ppb], [qfb])
                                rt, rtb = R["rt"].next()
                                q4 = qf[:].rearrange("p (h e j) -> p h e j", h=2, e=2)
                                te, to = q4[:, :, 0, :], q4[:, :, 1, :]
                                r4 = rt[:].rearrange("p (a h j) -> p a h j", a=4, h=2)
                                d4 = dst[:, (cg % 2) * 512:(cg % 2 + 1) * 512].rearrange("p (h e j) -> p h e j", h=2, e=2)
                                tt(eng, r4[:, 0], te, cosb, ALU.mult, [qfb, csb], [rtb])
                                tt(eng, r4[:, 1], to, sinb, ALU.mult, [qfb, csb], [rtb])
                                tt(eng, r4[:, 2], te, sinb, ALU.mult, [qfb, csb], [rtb])
                                tt(eng, r4[:, 3], to, cosb, ALU.mult, [qfb, csb], [rtb])
                                tt(eng, d4[:, :, 0, :], r4[:, 0], r4[:, 1], ALU.subtract, [rtb], [dstb])
                                tt(eng, d4[:, :, 1, :], r4[:, 2], r4[:, 3], ALU.add, [rtb], [dstb])
                            else:
                                copy("act", V[:, (cg - 4) * 512:(cg - 3) * 512], pp[:], [ppb], [Vb])
                        QT, QTb = R["QT"].next()
                        KT, KTb = R["KT"].next()
                        tr_group([(ps_tr[:, k * 128:(k + 1) * 128], Qr[:, k * 128:(k + 1) * 128]) for k in range(8)], [Qrb], [ps_tr_b])
                        copy("dve", QT[:], ps_tr[:], [ps_tr_b], [QTb])
                        tr_group([(ps_tr[:, k * 128:(k + 1) * 128], Kr[:, k * 128:(k + 1) * 128]) for k in range(8)], [Krb], [ps_tr_b])
                        copy("act", KT[:], ps_tr[:], [ps_tr_b], [KTb])
                        o1, o1b = R["o1"].next()
                        scan_part(R, 0, n, QT, QTb, KT, KTb, Kr, Krb, V, Vb, o1, o1b, None, None, True)
                        if n == NCC and l == 0:
                            dump("hT", hT[:], hTb, 1024); dump("Qr", Qr[:], Qrb, 1024); dump("Kr", Kr[:], Krb, 1024)
                            dump("V", V[:], Vb, 2048); dump("o1p1", o1[:], o1b, 2048); dump("QT", QT[:], QTb, 1024)
                            dump("modT", modT[:], modT_b, 48); dump("Acol", A_col[:], col_b, 16); dump("G2bc0", G2bc[0][:], G2bc_b, 1024)
                            dump("tab", tab[:], tab_b, 56); dump("mask1", mask[0][:], mask_b, 512); dump("state", state[:], state_b, 4096)
                        dma_pool(QTd[n], QT[:], [QTb], [Qb_d[n]])
                        dma_pool(KTd[n], KT[:], [KTb], [Kb_d[n]])
                        dma_pool(Krd[n], Kr[:], [Krb], [Krb_d[n]])
                        dma_pool(Vd[n], V[:], [Vb], [Vb_d[n]])
                        dma_pool(O1d[n], o1[:], [o1b], [O1b_d[n]])
                    S.barrier()
                pes = contextlib.ExitStack()
                with pes:
                    R = {
                        "x": Ring(nc, pes, "r2x", [128, D], F32, 2), "junk": Ring(nc, pes, "r2j", [128, D], F32, 1),
                        "stat": Ring(nc, pes, "r2s", [128, 4], F32, 2), "xhat": Ring(nc, pes, "r2xh", [128, D], BF16, 2),
                        "hT": Ring(nc, pes, "r2hT", [128, D], BF16, 2),
                        "QT": Ring(nc, pes, "r2QT", [128, D], BF16, 2), "KT": Ring(nc, pes, "r2KT", [128, D], BF16, 2),
                        "Kr": Ring(nc, pes, "r2Kr", [128, D], BF16, 2),
                        "V": Ring(nc, pes, "r2V", [128, 2048], BF16, 2), "P": Ring(nc, pes, "r2P", [128, 512], BF16, 2),
                        "Kdec": Ring(nc, pes, "r2Kd", [128, D], BF16, 2), "o1": Ring(nc, pes, "r2o1", [128, 2048], F32, 1),
                        "SG": Ring(nc, pes, "r2SG", [128, 2048], BF16, 1), "Z": Ring(nc, pes, "r2Z", [128, 2048], BF16, 1),
                        "ZT": Ring(nc, pes, "r2ZT", [128, 2048], BF16, 1), "tmp": Ring(nc, pes, "r2tmp", [128, D], F32, 1),
                        "xn": Ring(nc, pes, "r2xn", [128, D], F32, 2), "wst": Ring(nc, pes, "r2wst", [128, D], F32, 1),
                        "bn": Ring(nc, pes, "r2bn", [128, 48], F32, 2),
                    }
                    R["xg"] = R["xn"]
                    if PAIR:
                        ex_cout, ex_cob = exchange_start(j, state[:], state_b, 4096)
                    for cg in range(4):
                        dma_pool(wslot(cg).rearrange("p (k c) -> p k c", k=8),
                                 ret_w_in[j][:, (12 - 4 + cg) * 512:(12 - 3 + cg) * 512].rearrange("(k p) c -> p k c", p=128), [], [Wb[cg]])
                    Wo = W[:, 4 * 4096:8 * 4096].rearrange("p (k c) -> p k c", k=16)
                    for k in range(16):
                        wt_, wtb = R["wst"].next()
                        dma_sp(wt_[:], ret_w_out[j][k * 128:(k + 1) * 128, :], [], [wtb])
                        ts("dve", Wo[:, k, :], wt_[:], gn_t[:, j * 16 + k:j * 16 + k + 1], None, ALU.mult, None, [wtb, small_b], [Wb[4 + k // 4]])
                    zero_state()
                    order = list(range(NCC - 1, -1, -1)) + list(range(NCHUNK - 1, NCC - 1, -1))
                    for n in order:
                        row0, v = chunk_rows(n)
                        if n == NCHUNK - 1 and PAIR:
                            exchange_finish(R, ex_cout, ex_cob, state, state_b, 4096)
                            copy("pool", state_bf[:], state[:], [state_b], [statebf_b])
                        skip_out = last and v == 1 and not dbg
                        QT, QTb = R["QT"].next()
                        KT, KTb = R["KT"].next()
                        Kr, Krb = R["Kr"].next()
                        V, Vb = R["V"].next()
                        dma_sp(Kr[:], Krd[n], [Krb_d[n]], [Krb])
                        dma_sp(V[:], Vd[n], [Vb_d[n]], [Vb])
                        if skip_out:
                            scan_part(R, 1, n, None, None, None, None, Kr, Krb, V, Vb, None, None, None, None, False)
                            continue
                        dma_sp(QT[:], QTd[n], [Qb_d[n]], [QTb])
                        dma_sp(KT[:], KTd[n], [Kb_d[n]], [KTb])
                        o1, o1b = R["o1"].next()
                        dma_sp(o1[:], O1d[n], [O1b_d[n]], [o1b])
                        hT, hTb = R["hT"].next()
                        xt, xb = norm_chunk(R, src_ap, src_buf, row0, v, lambda k, hT=hT: hT[:, k * 128:(k + 1) * 128], hTb)
                        SG, SGb = R["SG"].next()
                        for cg in range(4):
                            pp, ppb = ps_proj[cg % 2], ps_proj_b[cg % 2]
                            wv = wslot(cg).rearrange("p (k c) -> p k c", k=8)
                            mm_group([(pp[:], hT[:, k * 128:(k + 1) * 128], wv[:, k, :], k == 0, k == 7) for k in range(8)],
                                     [hTb, Wb[cg]], [ppb])
                            act(SG[:, cg * 512:(cg + 1) * 512], pp[:], AF.Silu, [ppb], [SGb])
                        scan_part(R, 1, n, QT, QTb, KT, KTb, Kr, Krb, V, Vb, o1, o1b, o1, o1b, True)
                        if n == NCC and l == 0:
                            dump("osum", o1[:], o1b, 2048)
                        bn, bnb = R["bn"].next()
                        for h in range(4):
                            S.add("dve", "c", [lambda e, h=h, bn=bn, o1=o1: e.bn_stats(out=bn[:, h * 6:(h + 1) * 6], in_=o1[:, h * 512:(h + 1) * 512])], [o1b], [bnb])
                        for h in range(4):
                            S.add("dve", "c", [lambda e, h=h, bn=bn: e.bn_aggr(out=bn[:, 24 + h * 2:24 + h * 2 + 2], in_=bn[:, h * 6:(h + 1) * 6])], [bnb], [bnb])
                        mv = bn[:, 24:32].rearrange("p (h t) -> p h t", t=2)
                        act(bn[:, 32:36], mv[:, :, 1], AF.Sqrt, [bnb, cst_b], [bnb], scale=1.0, bias=eps_col)
                        S.add("dve", "c", [lambda e, bn=bn: e.reciprocal(out=bn[:, 36:40], in_=bn[:, 32:36])], [bnb], [bnb])
                        for h in range(4):
                            ts("dve", o1[:, h * 512:(h + 1) * 512], o1[:, h * 512:(h + 1) * 512], bn[:, 24 + h * 2:24 + h * 2 + 1],
                               bn[:, 36 + h:37 + h], ALU.subtract, ALU.mult, [o1b, bnb], [o1b])
                        if n == NCC and l == 0:
                            dump("on", o1[:], o1b, 2048); dump("SG", SG[:], SGb, 2048); dump("bn", bn[:], bnb, 48)
                        Z, Zb = R["Z"].next()
                        tt("pool", Z[:], o1[:], SG[:], ALU.mult, [o1b, SGb], [Zb])
                        ZT, ZTb = R["ZT"].next()
                        for half in range(2):
                            tr_group([(ps_tr[:, k * 128:(k + 1) * 128], Z[:, (half * 8 + k) * 128:(half * 8 + k + 1) * 128]) for k in range(8)],
                                     [Zb], [ps_tr_b])
                            copy("act" if half == 0 else "dve", ZT[:, half * 1024:(half + 1) * 1024], ps_tr[:], [ps_tr_b], [ZTb])
                        for g in range(2):
                            mm_group([(ps_proj[g][:], ZT[:, k * 128:(k + 1) * 128], Wo[:, k, g * 512:(g + 1) * 512], k == 0, k == 15) for k in range(16)],
                                     [ZTb] + Wb[4:8], [ps_proj_b[g]])
                        if last and v == 1:
                            post_chunk(R, xt, xb, v, dbg_ctx, dst_buf, row0)
                        elif last:
                            post_chunk(R, xt, xb, v, out, dst_buf, row0 - CTX)
                        else:
                            post_chunk(R, xt, xb, v, dst_ap, dst_buf, row0)
                    S.barrier()

        def lru_layer(l, src_ap, src_buf, dst_ap, dst_buf, last):
            j = l // 2
            tiles = [("c", 0, CTX)] + [("l", i * 512, 512) for i in range(NL // 512)]
            XR = {"c": XRc, "l": XRl}
            XC = {"c": XCc, "l": XCl}
            H1 = {"c": H1c, "l": H1l}
            SGD = {"c": SGc, "l": SGl}
            les = contextlib.ExitStack()
            with les:
                lsb = lambda n, s, d=F32: les.enter_context(nc.sbuf_tensor(f"{n}_L{l}", s, d))
                prep_layer(l, les)
                cl = lsb("cl", [128, 20])
                sp_t = lsb("sp_t", [128, 120])
                cl_b = Buf("cl")
                hprev = lsb("hprev", [128, 2 * NCH])
                hprev_b = Buf("hprev")
                lam_j = lam_t[:, j * 20:(j + 1) * 20]
                A0 = lambda i: sp_t[:, i * 20:(i + 1) * 20]
                ts("dve", A0(1), lam_j, -1.0, None, ALU.mult, None, [small_b], [cl_b])
                tt("dve", A0(0), lam_j, A0(1), ALU.min, [small_b, cl_b], [cl_b])
                act(A0(1), A0(0), AF.Exp, [cl_b], [cl_b])
                ts("dve", A0(2), A0(1), 2.0, None, ALU.add, None, [cl_b], [cl_b])
                S.add("dve", "c", [lambda e: e.reciprocal(out=A0(3), in_=A0(2))], [cl_b], [cl_b])
                tt("dve", A0(2), A0(1), A0(3), ALU.mult, [cl_b], [cl_b])
                tt("dve", A0(3), A0(2), A0(2), ALU.mult, [cl_b], [cl_b])
                ts("dve", A0(4), A0(3), 1.0 / 17.0, 1.0 / 15.0, ALU.mult, ALU.add, [cl_b], [cl_b])
                for cden in (13.0, 11.0, 9.0, 7.0, 5.0, 3.0, 1.0):
                    tt("dve", A0(4), A0(4), A0(3), ALU.mult, [cl_b], [cl_b])
                    ts("dve", A0(4), A0(4), 1.0 / cden, None, ALU.add, None, [cl_b], [cl_b])
                tt("dve", A0(4), A0(4), A0(2), ALU.mult, [cl_b], [cl_b])
                ts("dve", A0(5), lam_j, -1.0, 0.0, ALU.mult, ALU.max, [small_b, cl_b], [cl_b])
                stt(A0(5), A0(4), 2.0, A0(5), ALU.mult, ALU.add, [cl_b], [cl_b])
                ts("dve", cl[:], A0(5), -8.0, None, ALU.mult, None, [cl_b], [cl_b])

                XRb = {("c", 0): Buf("xrc")}
                XCb, H1b, SGb_d = {}, {}, {}
                for (sq, t0, nt) in tiles:
                    XRb[(sq, t0)] = Buf(f"xr{sq}{t0}")
                    XCb[(sq, t0)] = Buf(f"xc{sq}{t0}")
                    H1b[(sq, t0)] = Buf(f"h1{sq}{t0}")
                    SGb_d[(sq, t0)] = Buf(f"sg{sq}{t0}")
                pad_b = {("c", "L"): Buf("padcL"), ("c", "R"): Buf("padcR"), ("l", "L"): Buf("padlL"), ("l", "R"): Buf("padlR")}
                for s in range(5):
                    dma_pool(wslot(s).rearrange("p (k c) -> p k c", k=8),
                             lru_w_in[j][:, s * 512:(s + 1) * 512].rearrange("(k p) c -> p k c", p=128), [], [Wb[s]])
                GW = W[:, 5 * 4096:5 * 4096 + 40 * 128].rearrange("p (d g c j) -> p d g c j", d=2, g=2, c=NCH)
                for d in range(2):
                    dma_pool(GW[:, d, 0], lru_wa[j, d].rearrange("c i j -> i c j"), [], [Wb[5], Wb[6]])
                    dma_pool(GW[:, d, 1], lru_wx[j, d].rearrange("c i j -> i c j"), [], [Wb[5], Wb[6]])
                z3 = zeros_f[:, 0:20].rearrange("p (c t) -> p c t", t=2)
                for sq, ln in (("c", CTX), ("l", NL)):
                    dma_pool(XR[sq][:, :, 0:2].rearrange("c p t -> p c t"), z3, [cst_b], [pad_b[(sq, "L")]])
                    if sq == "c" or not PAIR:
                        dma_pool(XR[sq][:, :, ln + 2:ln + 4].rearrange("c p t -> p c t"), z3, [cst_b], [pad_b[(sq, "R")]])

                pes = contextlib.ExitStack()
                with pes:
                    R = {
                        "x": Ring(nc, pes, "l0x", [128, D], F32, 2), "junk": Ring(nc, pes, "l0j", [128, D], F32, 1),
                        "stat": Ring(nc, pes, "l0s", [128, 4], F32, 2), "xhat": Ring(nc, pes, "l0xh", [128, D], BF16, 2),
                        "hT": Ring(nc, pes, "l0hT", [128, 8 * 512], BF16, 2), "xr": Ring(nc, pes, "l0xr", [128, 512], F32, 3),
                        "sg": Ring(nc, pes, "l0sg", [128, 512], BF16, 3), "xg": Ring(nc, pes, "l0xg", [128, 1024], F32, 2),
                        "halo": Ring(nc, pes, "l0halo", [128, 64], F32, 2),
                    }
                    for (sq, t0, nt) in tiles:
                        base = 0 if sq == "c" else CTX
                        v = 1 if sq == "c" else 0
                        hT, hTb = R["hT"].next()
                        hv = hT[:].rearrange("p (k t) -> p k t", k=8)
                        for c in range(nt // 128):
                            norm_chunk(R, src_ap, src_buf, base + t0 + c * 128, v, lambda k, hv=hv, c=c: hv[:, k, c * 128:(c + 1) * 128], hTb)
                        for cc in range(2 * NCH):
                            pp, ppb = ps_proj[cc % 2], ps_proj_b[cc % 2]
                            wv = wslot(cc // 4).rearrange("p (k c) -> p k c", k=8)
                            mm_group([(pp[:, 0:nt], wv[:, k, (cc % 4) * 128:(cc % 4 + 1) * 128], hv[:, k, 0:nt], k == 0, k == 7) for k in range(8)],
                                     [hTb, Wb[cc // 4]], [ppb])
                            if cc < NCH:
                                xr, xrb = R["xr"].next()
                                copy("act", xr[:, 0:nt], pp[:, 0:nt], [ppb], [xrb])
                                dma_pool(XR[sq][cc, :, 2 + t0:2 + t0 + nt], xr[:, 0:nt], [xrb], [XRb[(sq, t0)]])
                            else:
                                sg, sgb = R["sg"].next()
                                act(sg[:, 0:nt], pp[:, 0:nt], AF.Silu, [ppb], [sgb])
                                dma_pool(SGD[sq][cc - NCH, :, t0:t0 + nt], sg[:, 0:nt], [sgb], [SGb_d[(sq, t0)]])
                    if PAIR:
                        lastb = XRb[("l", NL - 512)]
                        hl, hlb = R["halo"].next()
                        dma_sp(hl[:, 0:20].rearrange("p (c t) -> p c t", t=2), XRl[:, :, NL:NL + 2].rearrange("c p t -> p c t"), [lastb], [hlb])
                        hr, hrb = R["halo"].next()
                        hc_, hcb_ = exchange_start(l, hl[:, 0:32], hlb, 32)
                        exchange_finish(R, hc_, hcb_, hr, hrb, 32)
                        h3 = hr[:, 0:20].rearrange("p (c t) -> p c t", t=2)
                        dma_pool(XRl[:, :, NL + 2:NL + 3].rearrange("c p t -> p c t"), h3[:, :, 1:2], [hrb], [pad_b[("l", "R")]], slow=True)
                        dma_pool(XRl[:, :, NL + 3:NL + 4].rearrange("c p t -> p c t"), h3[:, :, 0:1], [hrb], [pad_b[("l", "R")]], slow=True)
                    S.barrier()

                def gates(R, d, cc, nt, xc, xcb):
                    xb16, xb16b = R["xcb"].next()
                    copy("act", xb16[:, 0:nt], xc[:, 0:nt], [xcb], [xb16b])
                    pr, prb = ps_o[cc % 2], ps_o_b[cc % 2]
                    pg, pgb = ps_su[cc % 2], ps_su_b[cc % 2]
                    mm_group([(pr[:, 0:nt], GW[:, d, 0, cc, :], xb16[:, 0:nt], True, True)], [xb16b, Wb[5], Wb[6]], [prb])
                    mm_group([(pg[:, 0:nt], GW[:, d, 1, cc, :], xb16[:, 0:nt], True, True)], [xb16b, Wb[5], Wb[6]], [pgb])
                    r_, rb = R["r"].next()
                    gi, gib = R["gi"].next()
                    a_, ab = R["a"].next()
                    q_, qb = R["q"].next()
                    u_, ub = R["u"].next()
                    bcol = (j * 2 + d) * NCH + cc
                    act(r_[:, 0:nt], pr[:, 0:nt], AF.Sigmoid, [prb, small_b], [rb], bias=ba_t[:, bcol:bcol + 1])
                    act(gi[:, 0:nt], pg[:, 0:nt], AF.Sigmoid, [pgb, small_b], [gib], bias=bx_t[:, bcol:bcol + 1])
                    act(a_[:, 0:nt], r_[:, 0:nt], AF.Exp, [rb, cl_b], [ab], scale=cl[:, d * NCH + cc:d * NCH + cc + 1])
                    tt("pool", q_[:, 0:nt], a_[:, 0:nt], a_[:, 0:nt], ALU.mult, [ab], [qb])
                    act(q_[:, 0:nt], q_[:, 0:nt], AF.Sqrt, [qb, cst_b], [qb], scale=-1.0, bias=one_col)
                    tt("pool", u_[:, 0:nt], q_[:, 0:nt], gi[:, 0:nt], ALU.mult, [qb, gib], [ub])
                    tt("dve", u_[:, 0:nt], u_[:, 0:nt], xc[:, 0:nt], ALU.mult, [ub, xcb], [ub])
                    return a_, ab, u_, ub

                pes = contextlib.ExitStack()
                with pes:
                    R = {
                        "win": Ring(nc, pes, "l1w", [128, 516], F32, 2), "xc": Ring(nc, pes, "l1xc", [128, 512], F32, 2),
                        "xcb": Ring(nc, pes, "l1xcb", [128, 512], BF16, 2), "r": Ring(nc, pes, "l1r", [128, 512], F32, 2),
                        "gi": Ring(nc, pes, "l1gi", [128, 512], F32, 2), "a": Ring(nc, pes, "l1a", [128, 512], F32, 2),
                        "q": Ring(nc, pes, "l1q", [128, 512], F32, 2), "u": Ring(nc, pes, "l1u", [128, 512], F32, 2),
                        "h": Ring(nc, pes, "l1h", [128, 512], F32, 2),
                    }
                    S.add("dve", "c", [lambda e: e.memset(hprev[:], 0.0)], [], [hprev_b])
                    for ti, (sq, t0, nt) in enumerate(tiles):
                        ln = CTX if sq == "c" else NL
                        rd = [XRb[(sq, t0)]]
                        if t0 >= 512:
                            rd.append(XRb[(sq, t0 - 512)])
                        else:
                            rd.append(pad_b[(sq, "L")])
                        if t0 + nt < ln:
                            rd.append(XRb[(sq, t0 + nt)])
                        else:
                            rd.append(pad_b[(sq, "R")])
                        for cc in range(NCH):
                            win, winb = R["win"].next()
                            dma_sp(win[:, 0:nt + 4], XR[sq][cc, :, t0:t0 + nt + 4], rd, [winb])
                            xc, xcb = R["xc"].next()
                            wc = lambda tap: cw_t[:, (j * NCH + cc) * 5 + tap:(j * NCH + cc) * 5 + tap + 1]
                            ts("dve", xc[:, 0:nt], win[:, 0:nt], wc(0), cb_t[:, j * NCH + cc:j * NCH + cc + 1], ALU.mult, ALU.add,
                               [winb, small_b], [xcb])
                            for tap in range(1, 5):
                                stt(xc[:, 0:nt], win[:, tap:tap + nt], wc(tap), xc[:, 0:nt], ALU.mult, ALU.add, [winb, small_b, xcb], [xcb])
                            dma_pool(XC[sq][cc, :, t0:t0 + nt], xc[:, 0:nt], [xcb], [XCb[(sq, t0)]])
                            a_, ab, u_, ub = gates(R, 0, cc, nt, xc, xcb)
                            h_, hb = R["h"].next()
                            S.add("dve", "c", [lambda e, h_=h_, a_=a_, u_=u_, cc=cc, nt=nt: e.tensor_tensor_scan(
                                out=h_[:, 0:nt], data0=a_[:, 0:nt], data1=u_[:, 0:nt], initial=hprev[:, cc:cc + 1], op0=ALU.mult, op1=ALU.add)],
                                [ab, ub, hprev_b], [hb])
                            copy("pool", hprev[:, cc:cc + 1], h_[:, nt - 1:nt], [hb], [hprev_b])
                            dma_pool(H1[sq][cc, :, t0:t0 + nt], h_[:, 0:nt], [hb], [H1b[(sq, t0)]])
                    S.barrier()

                pes = contextlib.ExitStack()
                with pes:
                    R = {
                        "xc": Ring(nc, pes, "l2xc", [128, 512], F32, 2),
                        "xcb": Ring(nc, pes, "l2xcb", [128, 512], BF16, 2), "r": Ring(nc, pes, "l2r", [128, 512], F32, 2),
                        "gi": Ring(nc, pes, "l2gi", [128, 512], F32, 2), "a": Ring(nc, pes, "l2a", [128, 512], F32, 2),
                        "q": Ring(nc, pes, "l2q", [128, 512], F32, 2), "u": Ring(nc, pes, "l2u", [128, 512], F32, 2),
                        "h": Ring(nc, pes, "l2h", [128, 512], F32, 2), "h1": Ring(nc, pes, "l2h1", [128, 512], F32, 2),
                        "sg": Ring(nc, pes, "l2sg", [128, 512], BF16, 2), "Z": Ring(nc, pes, "l2Z", [128, NCH * 512], BF16, 2),
                        "x": Ring(nc, pes, "l2x", [128, D], F32, 2), "junk": Ring(nc, pes, "l2j", [128, D], F32, 1),
                        "stat": Ring(nc, pes, "l2s", [128, 4], F32, 2), "tmp": Ring(nc, pes, "l2tmp", [128, D], F32, 1),
                        "xn": Ring(nc, pes, "l2xn", [128, D], F32, 2), "xg": Ring(nc, pes, "l2xg", [128, 1024], F32, 3),
                    }
                    Wo = W[:, 0:NCH * 1024].rearrange("p (k c) -> p k c", k=NCH)
                    dma_pool(Wo, lru_w_out[j].rearrange("(k p) c -> p k c", p=128), [], [Wb[0], Wb[1], Wb[2]])
                    hp2 = hprev[:, NCH:2 * NCH]
                    if PAIR:
                        pst_ = pes.enter_context(nc.sbuf_tensor(f"lpstate{l}", [128, 32], F32))
                        pst_b = Buf("lpstate")
                        hx, hxb = R["xg"].next()
                        copy("dve", hx[:, 0:NCH], hprev[:, 0:NCH], [hprev_b], [hxb])
                        sc_, scb_ = exchange_start(l - 1, hx[:, 0:32], hxb, 32)
                        exchange_finish(R, sc_, scb_, pst_, pst_b, 32)
                    order = [tiles[0]] + tiles[:0:-1]
                    for (sq, t0, nt) in order:
                        base = 0 if sq == "c" else CTX
                        v = 1 if sq == "c" else 0
                        if PAIR and sq == "l" and t0 == NL - 512:
                            copy("dve", hp2, pst_[:, 0:NCH], [pst_b, hprev_b], [hprev_b])
                        skip_out = last and sq == "c" and not dbg
                        Z, Zb = R["Z"].next()
                        zv = Z[:].rearrange("p (c t) -> p c t", c=NCH)
                        for cc in range(NCH):
                            xc, xcb = R["xc"].next()
                            dma_sp(xc[:, 0:nt], XC[sq][cc, :, t0:t0 + nt], [XCb[(sq, t0)]], [xcb])
                            a_, ab, u_, ub = gates(R, 1, cc, nt, xc, xcb)
                            h_, hb = R["h"].next()
                            S.add("dve", "c", [lambda e, h_=h_, a_=a_, u_=u_, cc=cc, nt=nt: e.tensor_tensor_scan(
                                out=h_[:, 0:nt][:, ::-1], data0=a_[:, 0:nt][:, ::-1], data1=u_[:, 0:nt][:, ::-1],
                                initial=hp2[:, cc:cc + 1], op0=ALU.mult, op1=ALU.add)], [ab, ub, hprev_b], [hb])
                            copy("pool", hp2[:, cc:cc + 1], h_[:, 0:1], [hb], [hprev_b])
                            if skip_out:
                                continue
                            h1, h1b = R["h1"].next()
                            dma_sp(h1[:, 0:nt], H1[sq][cc, :, t0:t0 + nt], [H1b[(sq, t0)]], [h1b])
                            sg, sgb = R["sg"].next()
                            dma_sp(sg[:, 0:nt], SGD[sq][cc, :, t0:t0 + nt], [SGb_d[(sq, t0)]], [sgb])
                            tt("pool", h1[:, 0:nt], h1[:, 0:nt], h_[:, 0:nt], ALU.add, [h1b, hb], [h1b])
                            tt("dve", zv[:, cc, 0:nt], h1[:, 0:nt], sg[:, 0:nt], ALU.mult, [h1b, sgb], [Zb])
                        if skip_out:
                            continue
                        for c in range(nt // 128):
                            row0 = base + t0 + c * 128
                            xt, xb = R["x"].next()
                            dma_sp(xt[:], src_ap[row0:row0 + 128, :], [src_buf], [xb])
                            for g in range(2):
                                mm_group([(ps_proj[g][:], zv[:, cc, c * 128:(c + 1) * 128], Wo[:, cc, g * 512:(g + 1) * 512], cc == 0, cc == NCH - 1)
                                          for cc in range(NCH)], [Zb, Wb[0], Wb[1], Wb[2]], [ps_proj_b[g]])
                            if last and v == 1:
                                post_chunk(R, xt, xb, v, dbg_ctx, dst_buf, row0)
                            elif last:
                                post_chunk(R, xt, xb, v, out, dst_buf, row0 - CTX)
                            else:
                                post_chunk(R, xt, xb, v, dst_ap, dst_buf, row0)
                    S.barrier()

        xin_b = Buf("xin")
        Xb = [Buf("Xs0"), Buf("Xs1")]
        out_b = Buf("out")
        src_ap, src_buf = xin, xin_b
        for l in range(depth):
            last = l == depth - 1
            dst_ap, dst_buf = (Xs[l % 2], Xb[l % 2])
            if last:
                dst_buf = out_b
            if l % 2 == 0:
                retention_layer(l, src_ap, src_buf, dst_ap, dst_buf, last)
            else:
                lru_layer(l, src_ap, src_buf, dst_ap, dst_buf, last)
            src_ap, src_buf = dst_ap, dst_buf
            if not last:
                S.rotate()
        S.final_wait("sp")

        block = es.enter_context(nc.Block())

        @block.tensor
        def _(e):
            S.emit("pe", e)

        @block.scalar
        def _(e):
            S.emit("act", e)

        @block.vector
        def _(e):
            S.emit("dve", e)

        @block.gpsimd
        def _(e):
            S.emit("pool", e)

        @block.sync
        def _(e):
            S.emit("sp", e)
    return nc


def _col(vec, nk):
    return np.ascontiguousarray(np.asarray(vec, np.float32).reshape(nk, 128).T)


def _rope_tables():
    n_rows = SEQ // GRID_W
    row = np.repeat(np.arange(n_rows, dtype=np.float32), GRID_W)
    col = np.tile(np.arange(GRID_W, dtype=np.float32), n_rows)
    n_freq = 64
    inv = (np.float32(10000.0) ** (-np.arange(n_freq, dtype=np.float32) / np.float32(n_freq))).astype(np.float32)
    ang = np.concatenate([row[:, None] * inv, col[:, None] * inv], axis=-1).astype(np.float32)
    return np.cos(ang).astype(np.float32), np.sin(ang).astype(np.float32)


def _core_inputs(core, inp, shared):
    if PAIR:
        b, half = core // 2, core % 2
    else:
        b, half = core, 0
    flip = half == 1
    dirs = (1, 0) if flip else (0, 1)
    x = inp["x"][b]
    ctx = inp["ctx"][b]
    cos, sin = shared["rope"]
    if PAIR:
        xs = x[half * NL:(half + 1) * NL]
        cs, sn = cos[half * NL:(half + 1) * NL], sin[half * NL:(half + 1) * NL]
    else:
        xs, cs, sn = x, cos, sin
    if flip:
        xs, cs, sn, ctx = xs[::-1], cs[::-1], sn[::-1], ctx[::-1]
    m = {}
    m["xin"] = np.ascontiguousarray(np.concatenate([ctx, xs], 0), dtype=np.float32)
    m["ropec"] = np.ascontiguousarray(np.concatenate([np.ones((CTX, 128), np.float32), cs], 0))
    m["ropes"] = np.ascontiguousarray(np.concatenate([np.zeros((CTX, 128), np.float32), sn], 0))
    cc = np.stack([_col(inp["c"][b], 8), _col(inp["c_ctx"], 8)], -1).reshape(128, 16)
    m["c_col"] = np.ascontiguousarray(cc)
    lg = np.asarray(inp["ret_log_decay"], np.float32)[:, list(dirs), :]
    m["ret_lg"] = np.ascontiguousarray(np.broadcast_to(lg.reshape(1, 16), (128, 16)))
    sel_d = list(dirs)
    m["lru_wa"] = np.ascontiguousarray(np.asarray(inp["lru_w_a"], np.float32)[:, sel_d])
    m["lru_wx"] = np.ascontiguousarray(np.asarray(inp["lru_w_x"], np.float32)[:, sel_d])
    def dcol(a):
        a = np.asarray(a, np.float32)[:, sel_d]
        return np.ascontiguousarray(np.concatenate([_col(a[j, d], NCH) for j in range(2) for d in range(2)], 1))
    m["lru_ba"] = dcol(inp["lru_b_a"])
    m["lru_bx"] = dcol(inp["lru_b_x"])
    m["lru_lam"] = dcol(inp["lru_lambda"])
    cw = np.asarray(inp["lru_conv_w"], np.float32)
    z = np.zeros_like(cw[:, :1])
    cw5 = np.concatenate([z, cw[:, ::-1]], 1) if flip else np.concatenate([cw, z], 1)
    cwc = np.stack([np.stack([_col(cw5[j, t], NCH) for t in range(5)], -1) for j in range(2)], 1)
    m["lru_cw"] = np.ascontiguousarray(cwc.reshape(128, 100))
    cst = shared["consts"].copy()
    if PAIR:
        cst[:, 649] = 1.0 if half == 1 else 0.0
        cst[:, 650] = 1.0 if half == 0 else 0.0
    m["consts"] = cst
    for k in ("mod_w", "mod_b_col", "npre_col", "npost_col", "ret_w_in", "ret_gn_col", "ret_w_out", "lru_w_in", "lru_cb", "lru_w_out"):
        m[k] = shared[k]
    return m


def _shared_inputs(inp):
    sh = {}
    sh["rope"] = _rope_tables()
    sh["mod_w"] = np.ascontiguousarray(inp["mod_w"], dtype=np.float32)
    sh["mod_b_col"] = np.ascontiguousarray(np.concatenate([_col(inp["mod_b"][l], 24) for l in range(DEPTH)], 1))
    sh["npre_col"] = np.ascontiguousarray(np.concatenate([_col(inp["norm_pre"][l], 8) for l in range(DEPTH)], 1))
    sh["npost_col"] = np.ascontiguousarray(np.concatenate([_col(inp["norm_post"][l], 8) for l in range(DEPTH)], 1))
    perm = np.arange(6144)
    for blk in range(2):
        for h in range(RET_H):
            base = blk * 1024 + h * 256
            perm[base:base + 256] = np.concatenate([base + np.arange(0, 256, 2), base + np.arange(1, 256, 2)])
    sh["ret_w_in"] = np.ascontiguousarray(np.asarray(inp["ret_w_in"], np.float32)[:, :, perm])
    sh["ret_gn_col"] = np.ascontiguousarray(np.concatenate([_col(inp["ret_gn"][j], 16) for j in range(2)], 1))
    sh["ret_w_out"] = np.ascontiguousarray(inp["ret_w_out"], dtype=np.float32)
    sh["lru_w_in"] = np.ascontiguousarray(inp["lru_w_in"], dtype=np.float32)
    sh["lru_cb"] = np.ascontiguousarray(np.concatenate([_col(inp["lru_conv_b"][j], NCH) for j in range(2)], 1))
    sh["lru_w_out"] = np.ascontiguousarray(inp["lru_w_out"], dtype=np.float32)
    cst = np.zeros((128, 128 * 5 + 16), np.float32)
    p = np.arange(128, dtype=np.float32)
    cst[:, 0:128] = np.eye(128, dtype=np.float32)
    cst[:, 128:256] = 1.0
    cst[:, 256:384] = (p[:, None] <= p[None, :])
    cst[:, 384:512] = (p[:, None] >= p[None, :])
    coefs = np.stack([127 - p, p + 1, -(p + 1), p, 128 - p, p - 128, np.full(128, 128.0, np.float32)], 1)
    cst[:, 640:647] = coefs
    cst[:, 647] = EPS
    cst[:, 648] = 1.0
    sh["consts"] = cst
    return sh


_NC_CACHE = {}


def kernel(**inputs):
    inp = {k: np.asarray(v) for k, v in inputs.items()}
    if "nc" not in _NC_CACHE:
        _NC_CACHE["nc"] = build_program()
    nc = _NC_CACHE["nc"]
    shared = _shared_inputs(inp)
    in_maps = [_core_inputs(c, inp, shared) for c in range(NCORES)]
    res = run_bass_kernel_spmd(nc, in_maps, core_ids=list(range(NCORES)))
    outp = np.empty((BATCH, SEQ, D), np.float32)
    for c in range(NCORES):
        o = np.asarray(res.results[c]["out"], np.float32)
        if PAIR:
            b, half = c // 2, c % 2
            outp[b, half * NL:(half + 1) * NL] = o[::-1] if half == 1 else o
        else:
            outp[c] = o
    return outp
```

```python
import contextlib
import numpy as np
import concourse.bass as bass
import concourse.mybir as mybir
from concourse.bass_utils import run_bass_kernel_spmd

F32 = mybir.dt.float32
BF16 = mybir.dt.bfloat16
AF = mybir.ActivationFunctionType
ALU = mybir.AluOpType
AX = mybir.AxisListType

D = 1024
DEPTH = 4
SEQ = 8192
BATCH = 4
CTX = 256
GRID_W = 64
EPS = 1e-6
RET_H = 4
LRU_W = 1280
NCH = 10
PAIR = True
NCORES = 8 if PAIR else 4
NL = SEQ // 2 if PAIR else SEQ
NTOK = CTX + NL
NCHUNK = NTOK // 128
NCC = CTX // 128


class Sem:
    def __init__(self, handle, stream, cls, step):
        self.h = handle
        self.stream = stream
        self.cls = cls
        self.step = step
        self.count = 0


class Buf:
    __slots__ = ("name", "w", "r")

    def __init__(self, name=""):
        self.name = name
        self.w = None
        self.r = {}


class Stream:
    def __init__(self, name):
        self.name = name
        self.items = []
        self.waited = {}


class Sched:
    NDMA = 14

    def __init__(self, nc, es):
        self.nc = nc
        self.es = es
        self.streams = {n: Stream(n) for n in ("pe", "act", "dve", "pool", "sp")}
        self.sems = {}
        self.all_sems = []
        self.nsem = 0
        self.dma_sems = {}
        self.dma_rr = {}
        for st in ("pool", "sp"):
            self.dma_sems[st] = [self._new_sem(st, "d", 16) for _ in range(self.NDMA)]
            self.dma_rr[st] = 0
        self.rotate()

    def _new_sem(self, st, cls, step):
        h = self.es.enter_context(self.nc.semaphore(f"s{self.nsem}_{st}_{cls}"))
        self.nsem += 1
        sm = Sem(h, st, cls, step)
        self.all_sems.append(sm)
        return sm

    def rotate(self):
        for st in ("pe", "act", "dve", "pool"):
            self.sems[(st, "c")] = self._new_sem(st, "c", 1)

    def add(self, stream, cls, fns, reads=(), writes=()):
        st = self.streams[stream]
        need = {}
        if cls == "d":
            i = self.dma_rr[stream]
            self.dma_rr[stream] = (i + 1) % self.NDMA
            sm = self.dma_sems[stream][i]
            if sm.count > 0 and st.waited.get(sm, 0) < sm.count:
                need[sm] = sm.count
        elif cls == "cc":
            sm = self._new_sem(stream, "cc", 1)
        else:
            sm = self.sems[(stream, cls)]
        raw = set()
        oth = set()
        for b in reads:
            if b.w is not None:
                raw.add(b.w)
        for b in writes:
            if b.w is not None:
                oth.add(b.w)
            for k, v in b.r.items():
                oth.add((k, v))
        for (dsm, v) in raw | oth:
            if dsm.stream == stream:
                if stream == "pe":
                    continue
                if dsm.cls == "c" and cls == "c" and (dsm, v) not in raw:
                    continue
            if st.waited.get(dsm, 0) >= v:
                continue
            if need.get(dsm, 0) < v:
                need[dsm] = v
        for k, v in need.items():
            st.waited[k] = v
        sm.count += sm.step
        tok = (sm, sm.count)
        st.items.append((list(need.items()), fns, sm))
        for b in reads:
            if b.r.get(sm, 0) < sm.count:
                b.r[sm] = sm.count
        for b in writes:
            b.w = tok
            b.r = {}
        return tok

    def barrier(self):
        for st in self.streams.values():
            need = []
            for sm in self.all_sems:
                if sm.count > 0 and st.waited.get(sm, 0) < sm.count:
                    if sm.stream == st.name and st.name == "pe":
                        continue
                    need.append((sm, sm.count))
                    st.waited[sm] = sm.count
            if need:
                st.items.append((need, [], None))

    def final_wait(self, stream="sp"):
        st = self.streams[stream]
        need = [(sm, sm.count) for sm in self.all_sems if sm.count > 0 and st.waited.get(sm, 0) < sm.count]
        st.items.append((need, [], None))

    def emit(self, stream, eng):
        for waits, fns, sm in self.streams[stream].items:
            for (wsm, v) in waits:
                eng.wait_ge(wsm.h, v)
            ins = None
            for f in fns:
                ins = f(eng)
            if ins is not None and sm is not None:
                ins.then_inc(sm.h, sm.step)


class Ring:
    _uid = [0]

    def __init__(self, nc, es, name, shape, dtype, n):
        Ring._uid[0] += 1
        name = f"{name}u{Ring._uid[0]}_"
        self.t = [es.enter_context(nc.sbuf_tensor(f"{name}{i}", shape, dtype)) for i in range(n)]
        self.b = [Buf(f"{name}{i}") for i in range(n)]
        self.i = 0

    def next(self):
        i = self.i
        self.i = (i + 1) % len(self.t)
        return self.t[i], self.b[i]


def build_program(depth=DEPTH, ncores=NCORES, dbg=False):
    nc = bass.Bass("TRN2", target_bir_lowering=False)
    dt_in = lambda n, s: nc.dram_tensor(n, s, F32, kind="ExternalInput").ap()
    xin = dt_in("xin", [NTOK, D])
    ropec = dt_in("ropec", [NTOK, 128])
    ropes = dt_in("ropes", [NTOK, 128])
    c_col = dt_in("c_col", [128, 16])
    mod_w = dt_in("mod_w", [DEPTH, D, 3 * D])
    mod_b_col = dt_in("mod_b_col", [128, DEPTH * 24])
    npre_col = dt_in("npre_col", [128, DEPTH * 8])
    npost_col = dt_in("npost_col", [128, DEPTH * 8])
    ret_w_in = dt_in("ret_w_in", [2, D, 6144])
    ret_lg = dt_in("ret_lg", [128, 16])
    ret_gn_col = dt_in("ret_gn_col", [128, 32])
    ret_w_out = dt_in("ret_w_out", [2, 2048, D])
    lru_w_in = dt_in("lru_w_in", [2, D, 2 * LRU_W])
    lru_cw = dt_in("lru_cw", [128, 2 * NCH * 5])
    lru_cb = dt_in("lru_cb", [128, 2 * NCH])
    lru_wa = dt_in("lru_wa", [2, 2, NCH, 128, 128])
    lru_wx = dt_in("lru_wx", [2, 2, NCH, 128, 128])
    lru_ba = dt_in("lru_ba", [128, 2 * 2 * NCH])
    lru_bx = dt_in("lru_bx", [128, 2 * 2 * NCH])
    lru_lam = dt_in("lru_lam", [128, 2 * 2 * NCH])
    lru_w_out = dt_in("lru_w_out", [2, LRU_W, D])
    consts = dt_in("consts", [128, 128 * 5 + 16])
    out = nc.dram_tensor("out", [NL, D], F32, kind="ExternalOutput").ap()
    dbg_ctx = nc.dram_tensor("dbg_ctx", [CTX, D], F32, kind="ExternalOutput").ap() if dbg else None
    rgroups = [[2 * i, 2 * i + 1] for i in range(ncores // 2)]
    dumps = {}
    DUMP_B = Buf("dump")

    dram = lambda n, s, d=F32: nc.dram_tensor(n, s, d)
    Xs = [dram("Xs0", [NTOK, D]), dram("Xs1", [NTOK, D])]
    QTd = dram("QTd", [NCHUNK, 128, 1024], BF16)
    KTd = dram("KTd", [NCHUNK, 128, 1024], BF16)
    Krd = dram("Krd", [NCHUNK, 128, 1024], BF16)
    Vd = dram("Vd", [NCHUNK, 128, 2048], BF16)
    O1d = dram("O1d", [NCHUNK, 128, 2048])
    XRc = dram("XRc", [NCH, 128, CTX + 4])
    XRl = dram("XRl", [NCH, 128, NL + 4])
    XCc = dram("XCc", [NCH, 128, CTX])
    XCl = dram("XCl", [NCH, 128, NL])
    H1c = dram("H1c", [NCH, 128, CTX])
    H1l = dram("H1l", [NCH, 128, NL])
    SGc = dram("SGc", [NCH, 128, CTX], BF16)
    SGl = dram("SGl", [NCH, 128, NL], BF16)
    cc_state_in = [dram(f"ccsi{j}", [128, 4096]) for j in range(2)]
    cc_state_out = [dram(f"ccso{j}", [256, 4096]) for j in range(2)]
    cc_small_in = [dram(f"ccmi{j}", [128, 32]) for j in range(4)]
    cc_small_out = [dram(f"ccmo{j}", [256, 32]) for j in range(4)]

    es = contextlib.ExitStack()
    with es:
        S = Sched(nc, es)
        sb = lambda n, s, d=F32: es.enter_context(nc.sbuf_tensor(n, s, d))
        W = sb("W", [128, 8 * 4096], BF16)
        Wb = [Buf(f"W{i}") for i in range(8)]
        wslot = lambda s: W[:, s * 4096:(s + 1) * 4096]
        cst = sb("cst", [128, 128 * 5 + 16])
        cst_b = Buf("cst")
        ident_f = cst[:, 0:128]
        ones_f = cst[:, 128:256]
        tri1 = cst[:, 256:384]
        tri2 = cst[:, 384:512]
        zeros_f = cst[:, 512:640]
        coef = cst[:, 640:647]
        eps_col = cst[:, 647:648]
        one_col = cst[:, 648:649]
        sel0 = cst[:, 649:650]
        sel1 = cst[:, 650:651]
        ident_b = sb("ident_b", [128, 128], BF16)
        ident_bb = Buf("ident_b")
        ccol = sb("ccol", [128, 16])
        act_bf = sb("act_bf", [128, 16], BF16)
        act_b = Buf("act")
        small = sb("small", [128, DEPTH * 24 + DEPTH * 16 + 16 + 32 + 100 + 120])
        small_b = Buf("small")
        o = 0
        modb_t = small[:, o:o + DEPTH * 24]; o += DEPTH * 24
        npre_t = small[:, o:o + DEPTH * 8]; o += DEPTH * 8
        npost_t = small[:, o:o + DEPTH * 8]; o += DEPTH * 8
        lg_t = small[:, o:o + 16]; o += 16
        gn_t = small[:, o:o + 32]; o += 32
        cw_t = small[:, o:o + 100]; o += 100
        cb_t = small[:, o:o + 20]; o += 20
        ba_t = small[:, o:o + 40]; o += 40
        bx_t = small[:, o:o + 40]; o += 40
        lam_t = sb("lam_t", [128, 40])
        modT = sb("modT", [128, 48])
        modT_b = Buf("modT")
        A_col = sb("A_col", [128, 16])
        G2col = sb("G2col", [128, 16])
        col_b = Buf("cols")
        G2bc = [sb("G2bc0", [128, D]), sb("G2bc1", [128, D])]
        G2bc_b = Buf("G2bc")
        ps_proj = [es.enter_context(nc.psum_tensor(f"ps_proj{i}", [128, 512], F32)) for i in range(2)]
        ps_proj_b = [Buf("pp0"), Buf("pp1")]
        ps_tr = es.enter_context(nc.psum_tensor("ps_tr", [128, 1024], BF16))
        ps_tr_b = Buf("ptr")
        ps_st = es.enter_context(nc.psum_tensor("ps_st", [128, 512], F32))
        ps_st_b = Buf("pst")
        ps_o = [es.enter_context(nc.psum_tensor(f"ps_o{i}", [128, 512], F32)) for i in range(2)]
        ps_o_b = [Buf("po0"), Buf("po1")]
        ps_su = [es.enter_context(nc.psum_tensor(f"ps_su{i}", [128, 512], F32)) for i in range(2)]
        ps_su_b = [Buf("psu0"), Buf("psu1")]

        def dma_sp(out_ap, in_ap, reads, writes):
            S.add("sp", "d", [lambda e: e.dma_start(out=out_ap, in_=in_ap)], reads, writes)

        def dma_pool(out_ap, in_ap, reads, writes, slow=False):
            if slow:
                S.add("pool", "d", [lambda e: e.dma_start(out=out_ap, in_=in_ap, allow_slow_non_contiguous=True)], reads, writes)
            else:
                S.add("pool", "d", [lambda e: e.dma_start(out=out_ap, in_=in_ap)], reads, writes)

        def act(out_ap, in_ap, func, reads, writes, scale=None, bias=None):
            kw = {}
            if scale is not None:
                kw["scale"] = scale
            if bias is not None:
                kw["bias"] = bias
            S.add("act", "c", [lambda e: e.activation(out=out_ap, in_=in_ap, func=func, **kw)], reads, writes)

        def ts(stream, out_ap, in_ap, s1, s2, op0, op1, reads, writes):
            if op1 is None:
                S.add(stream, "c", [lambda e: e.tensor_scalar(out=out_ap, in0=in_ap, scalar1=s1, scalar2=None, op0=op0)], reads, writes)
            else:
                S.add(stream, "c", [lambda e: e.tensor_scalar(out=out_ap, in0=in_ap, scalar1=s1, scalar2=s2, op0=op0, op1=op1)], reads, writes)

        def tt(stream, out_ap, a, b, op, reads, writes):
            S.add(stream, "c", [lambda e: e.tensor_tensor(out=out_ap, in0=a, in1=b, op=op)], reads, writes)

        def stt(out_ap, in0, scalar, in1, op0, op1, reads, writes):
            S.add("dve", "c", [lambda e: e.scalar_tensor_tensor(out=out_ap, in0=in0, scalar=scalar, in1=in1, op0=op0, op1=op1)], reads, writes)

        def copy(stream, out_ap, in_ap, reads, writes):
            if stream == "act":
                S.add("act", "c", [lambda e: e.copy(out=out_ap, in_=in_ap)], reads, writes)
            else:
                S.add(stream, "c", [lambda e: e.tensor_copy(out=out_ap, in_=in_ap)], reads, writes)

        def mm_group(specs, reads, writes):
            fns = []
            for (o_, l_, r_, st_, sp_) in specs:
                fns.append(lambda e, o_=o_, l_=l_, r_=r_, st_=st_, sp_=sp_: e.matmul(o_, lhsT=l_, rhs=r_, start=st_, stop=sp_))
            S.add("pe", "c", fns, reads, writes)

        def tr_group(specs, reads, writes):
            fns = []
            for (o_, i_) in specs:
                fns.append(lambda e, o_=o_, i_=i_: e.transpose(o_, i_, ident_b[:]))
            S.add("pe", "c", fns, list(reads) + [ident_bb], writes)

        def dump(name, ap, buf, F):
            if not dbg:
                return
            t = nc.dram_tensor("dmp_" + name, [128, F], F32, kind="ExternalOutput").ap()
            dumps[name] = t
            dma_pool(t[:, :], ap, [buf], [DUMP_B])

        dma_sp(cst[:], consts[:, :], [], [cst_b])
        dma_pool(ident_b[:], consts[:, 0:128], [], [ident_bb])
        dma_sp(ccol[:], c_col[:, :], [], [act_b])
        dma_sp(modb_t, mod_b_col[:, :], [], [small_b])
        dma_sp(npre_t, npre_col[:, :], [], [small_b])
        dma_sp(npost_t, npost_col[:, :], [], [small_b])
        dma_sp(lg_t, ret_lg[:, :], [], [small_b])
        dma_sp(gn_t, ret_gn_col[:, :], [], [small_b])
        dma_sp(cw_t, lru_cw[:, :], [], [small_b])
        dma_sp(cb_t, lru_cb[:, :], [], [small_b])
        dma_sp(ba_t, lru_ba[:, :], [], [small_b])
        dma_sp(bx_t, lru_bx[:, :], [], [small_b])
        dma_sp(lam_t[:], lru_lam[:, :], [], [small_b])
        act(act_bf[:], ccol[:], AF.Silu, [act_b], [act_b])

        def prep_layer(l, les):
            lsb = lambda n, s, d=F32: les.enter_context(nc.sbuf_tensor(n, s, d))
            dg = Ring(nc, les, f"dg{l}_", [128, 128], F32, 2)
            actv = act_bf[:].rearrange("p (k v) -> p k v", v=2)
            for cg in range(6):
                s = cg % 2
                wv = wslot(s).rearrange("p (k c) -> p k c", k=8)
                dma_pool(wv, mod_w[l][:, cg * 512:(cg + 1) * 512].rearrange("(k p) c -> p k c", p=128), [], [Wb[s]])
                specs = []
                for cc in range(4):
                    ck = cg * 4 + cc
                    for k in range(8):
                        specs.append((ps_st[:, ck * 2:ck * 2 + 2], wv[:, k, cc * 128:(cc + 1) * 128], actv[:, k, :], k == 0, k == 7))
                mm_group(specs, [Wb[s], act_b], [ps_st_b])
            tt("dve", modT[:].rearrange("p (c v) -> p c v", v=2), ps_st[:, 0:48].rearrange("p (c v) -> p c v", v=2),
               modb_t[:, l * 24:(l + 1) * 24].unsqueeze(2).to_broadcast([128, 24, 2]), ALU.add, [ps_st_b, small_b], [modT_b])
            npre_bc = npre_t[:, l * 8:(l + 1) * 8].unsqueeze(2).to_broadcast([128, 8, 2])
            npost_bc = npost_t[:, l * 8:(l + 1) * 8].unsqueeze(2).to_broadcast([128, 8, 2])
            v3 = lambda t: t.rearrange("p (c v) -> p c v", v=2)
            stt(v3(A_col[:]), v3(modT[:, 16:32]), 1.0, npre_bc, ALU.add, ALU.mult, [modT_b, small_b], [col_b])
            tt("dve", v3(G2col[:]), v3(modT[:, 32:48]), npost_bc, ALU.mult, [modT_b, small_b, col_b], [col_b])
            for v in range(2):
                for half in range(2):
                    specs = []
                    dbs = []
                    for kk in range(4):
                        k = half * 4 + kk
                        dt_, db_ = dg.next()
                        ts("dve", dt_[:], ident_f, G2col[:, k * 2 + v:k * 2 + v + 1], None, ALU.mult, None, [cst_b, col_b], [db_])
                        mm_group([(ps_o[half][:, kk * 128:(kk + 1) * 128], ones_f, dt_[:], True, True)], [db_, cst_b], [ps_o_b[half]])
                    copy("act", G2bc[v][:, half * 512:(half + 1) * 512], ps_o[half][:], [ps_o_b[half]], [G2bc_b])

        def norm_chunk(R, src_ap, src_buf, row0, v, hT_ap_fn, hT_buf, keep_x=False):
            xt, xb = R["x"].next()
            dma_sp(xt[:], src_ap[row0:row0 + 128, :], [src_buf], [xb])
            jt, jb = R["junk"].next()
            st_t, st_b = R["stat"].next()
            act(jt[:], xt[:], AF.Square, [xb], [jb])
            S.add("dve", "c", [lambda e: e.tensor_reduce(out=st_t[:, 0:1], in_=jt[:], axis=AX.X, op=ALU.add)], [jb], [st_b])
            act(st_t[:, 1:2], st_t[:, 0:1], AF.Sqrt, [st_b, cst_b], [st_b], scale=1.0 / D, bias=eps_col)
            S.add("dve", "c", [lambda e: e.reciprocal(out=st_t[:, 2:3], in_=st_t[:, 1:2])], [st_b], [st_b])
            xh, xhb = R["xhat"].next()
            ts("dve", xh[:], xt[:], st_t[:, 2:3], None, ALU.mult, None, [xb, st_b], [xhb])
            tr_group([(ps_tr[:, k * 128:(k + 1) * 128], xh[:, k * 128:(k + 1) * 128]) for k in range(8)], [xhb], [ps_tr_b])
            for k in range(8):
                act(hT_ap_fn(k), ps_tr[:, k * 128:(k + 1) * 128], AF.Identity, [ps_tr_b, col_b, modT_b], [hT_buf],
                    scale=A_col[:, k * 2 + v:k * 2 + v + 1], bias=modT[:, k * 2 + v:k * 2 + v + 1])
            return xt, xb

        def post_chunk(R, xt, xb, v, dst_ap, dst_buf, dst_row0):
            jt, jb = R["junk"].next()
            st_t, st_b = R["stat"].next()
            for g in range(2):
                act(jt[:, g * 512:(g + 1) * 512], ps_proj[g][:], AF.Square, [ps_proj_b[g]], [jb])
            S.add("dve", "c", [lambda e: e.tensor_reduce(out=st_t[:, 0:1], in_=jt[:], axis=AX.X, op=ALU.add)], [jb], [st_b])
            act(st_t[:, 1:2], st_t[:, 0:1], AF.Sqrt, [st_b, cst_b], [st_b], scale=1.0 / D, bias=eps_col)
            S.add("dve", "c", [lambda e: e.reciprocal(out=st_t[:, 2:3], in_=st_t[:, 1:2])], [st_b], [st_b])
            tm, tmb = R["tmp"].next()
            xn, xnb = R["xn"].next()
            for g in range(2):
                stt(tm[:, g * 512:(g + 1) * 512], ps_proj[g][:], st_t[:, 2:3], G2bc[v][:, g * 512:(g + 1) * 512],
                    ALU.mult, ALU.mult, [ps_proj_b[g], st_b, G2bc_b], [tmb])
            tt("pool", xn[:], tm[:], xt[:], ALU.add, [tmb, xb], [xnb])
            if dst_row0 == CTX and v == 0 and "tm" not in dumps:
                dump("tm", tm[:], tmb, 1024); dump("ysq", jt[:], jb, 1024); dump("stat", st_t[:], st_b, 4)
            dma_pool(dst_ap[dst_row0:dst_row0 + 128, :], xn[:], [xnb], [dst_buf])

        def exchange_start(idx_big, src_ap, src_buf, F):
            if F > 32:
                cin, cout = cc_state_in[idx_big], cc_state_out[idx_big]
            else:
                cin, cout = cc_small_in[idx_big], cc_small_out[idx_big]
            cb = Buf("ccin")
            cob = Buf("ccout")
            dma_pool(cin[:, 0:F], src_ap, (src_buf if isinstance(src_buf, list) else [src_buf]), [cb])
            S.add("pool", "cc", [lambda e: e.collective_compute(
                "AllGather", ALU.bypass, replica_groups=rgroups,
                ins=[cin[:, :]], outs=[cout[:, :]])], [cb], [cob])
            return cout, cob

        def exchange_finish(R, cout, cob, dst_ap, dst_buf, F):
            step = min(F, 1024)
            for p0 in range(0, F, step):
                g0, g0b = R["xg"].next()
                g1, g1b = R["xg"].next()
                dma_sp(g0[:, 0:step], cout[0:128, p0:p0 + step], [cob], [g0b])
                dma_sp(g1[:, 0:step], cout[128:256, p0:p0 + step], [cob], [g1b])
                ts("dve", g0[:, 0:step], g0[:, 0:step], sel0, None, ALU.mult, None, [g0b, cst_b], [g0b])
                stt(dst_ap[:, p0:p0 + step], g1[:, 0:step], sel1, g0[:, 0:step], ALU.mult, ALU.add, [g0b, g1b, cst_b],
                    (dst_buf if isinstance(dst_buf, list) else [dst_buf]))

        def retention_layer(l, src_ap, src_buf, dst_ap, dst_buf, last):
            j = l // 2
            les = contextlib.ExitStack()
            with les:
                lsb = lambda n, s, d=F32: les.enter_context(nc.sbuf_tensor(f"{n}_L{l}", s, d))
                prep_layer(l, les)
                state = lsb("state", [128, 4096])
                state_bf = lsb("state_bf", [128, 4096], BF16)
                state_hb = [Buf(f"state{h}") for h in range(4)]
                statebf_hb = [Buf(f"state_bf{h}") for h in range(4)]
                state_b = state_hb
                statebf_b = statebf_hb
                mask = [lsb("mask1", [128, 512]), lsb("mask2", [128, 512])]
                tab = lsb("tab", [128, 2 * 28])
                lgn = lsb("lgn", [128, 8])
                tab_b = Buf("tab")
                mask_b = Buf("mask")
                ts("dve", tab[:, 0:8], lg_t[:, j * 8:(j + 1) * 8], -1.0, None, ALU.mult, None, [small_b], [tab_b])
                tt("dve", lgn[:], lg_t[:, j * 8:(j + 1) * 8], tab[:, 0:8], ALU.min, [small_b, tab_b], [tab_b])
                for d in range(2):
                    for i in range(7):
                        ts("dve", tab[:, d * 28 + i * 4:d * 28 + i * 4 + 4], lgn[:, d * 4:(d + 1) * 4], coef[:, i:i + 1], None,
                           ALU.mult, None, [tab_b, cst_b], [tab_b])
                act(tab[:], tab[:], AF.Exp, [tab_b], [tab_b])
                for d in range(2):
                    for i in ((0, 2) if d == 0 else (3, 5)):
                        ts("dve", tab[:, d * 28 + i * 4:d * 28 + i * 4 + 4], tab[:, d * 28 + i * 4:d * 28 + i * 4 + 4], 0.0625, None,
                           ALU.mult, None, [tab_b], [tab_b])
                T = lambda d, i, h: tab[:, d * 28 + i * 4 + h:d * 28 + i * 4 + h + 1]
                for d in range(2):
                    for h in range(4):
                        ts("dve", mask[d][:, h * 128:(h + 1) * 128], tri1 if d == 0 else tri2, T(d, 2 if d == 0 else 5, h), None,
                           ALU.mult, None, [tab_b, cst_b], [mask_b])
                KD = (0, 3)
                QD = (1, 4)

                def chunk_rows(n):
                    return n * 128, (1 if n < NCC else 0)

                def scan_part(R, d, n, QT, QTb, KT, KTb, Kr, Krb, V, Vb, o_t, o_b, o1_t, o1_b, need_o):
                    if need_o:
                        specs = []
                        for h in range(4):
                            for dc in range(2):
                                specs.append((ps_st[:, h * 128:(h + 1) * 128], KT[:, (2 * h + dc) * 128:(2 * h + dc + 1) * 128],
                                              QT[:, (2 * h + dc) * 128:(2 * h + dc + 1) * 128], dc == 0, dc == 1))
                        mm_group(specs, [KTb, QTb], [ps_st_b])
                        P, Pb = R["P"].next()
                        tt("dve", P[:], ps_st[:], mask[d][:], ALU.mult, [ps_st_b, mask_b], [Pb])
                    Kd, Kdb = R["Kdec"].next()
                    ts_bc = tab[:, d * 28 + KD[d] * 4:d * 28 + KD[d] * 4 + 4].unsqueeze(2).to_broadcast([128, 4, 256])
                    tt("pool", Kd[:].rearrange("p (h c) -> p h c", h=4), Kr[:].rearrange("p (h c) -> p h c", h=4), ts_bc, ALU.mult, [Krb, tab_b], [Kdb])
                    for h in range(4):
                        if need_o:
                            po, pob = ps_o[h % 2], ps_o_b[h % 2]
                            specs = [(po[:], P[:, h * 128:(h + 1) * 128], V[:, h * 512:(h + 1) * 512], True, False)]
                            for dc in range(2):
                                specs.append((po[:], QT[:, (2 * h + dc) * 128:(2 * h + dc + 1) * 128],
                                              state_bf[:, (h * 2 + dc) * 512:(h * 2 + dc + 1) * 512], False, dc == 1))
                            mm_group(specs, [Pb, Vb, QTb, statebf_hb[h]], [pob])
                            if d == 0:
                                act(o_t[:, h * 512:(h + 1) * 512], po[:], AF.Identity, [pob, tab_b], [o_b], scale=T(d, QD[d], h))
                            else:
                                stt(o_t[:, h * 512:(h + 1) * 512], po[:], T(d, QD[d], h), o1_t[:, h * 512:(h + 1) * 512],
                                    ALU.mult, ALU.add, [pob, tab_b, o1_b], [o_b])
                        for dc in range(2):
                            mm_group([(ps_su[dc][:], Kd[:, h * 256 + dc * 128:h * 256 + (dc + 1) * 128], V[:, h * 512:(h + 1) * 512], True, True)],
                                     [Kdb, Vb], [ps_su_b[dc]])
                            sl = slice((h * 2 + dc) * 512, (h * 2 + dc + 1) * 512)
                            stt(state[:, sl], state[:, sl], T(d, 6, h), ps_su[dc][:], ALU.mult, ALU.add,
                                [ps_su_b[dc], tab_b, state_hb[h]], [state_hb[h]])
                        for dc in range(2):
                            sl = slice((h * 2 + dc) * 512, (h * 2 + dc + 1) * 512)
                            copy("act", state_bf[:, sl], state[:, sl], [state_hb[h]], [statebf_hb[h]])

                def zero_state():
                    S.add("dve", "c", [lambda e: e.memset(state[:], 0.0)], [], state_hb)
                    S.add("pool", "c", [lambda e: e.memset(state_bf[:], 0.0)], [], statebf_hb)

                Qb_d = [Buf(f"QTd{n}") for n in range(NCHUNK)]
                Kb_d = [Buf(f"KTd{n}") for n in range(NCHUNK)]
                Krb_d = [Buf(f"Krd{n}") for n in range(NCHUNK)]
                Vb_d = [Buf(f"Vd{n}") for n in range(NCHUNK)]
                O1b_d = [Buf(f"O1d{n}") for n in range(NCHUNK)]

                pes = contextlib.ExitStack()
                with pes:
                    R = {
                        "x": Ring(nc, pes, "r1x", [128, D], F32, 2), "junk": Ring(nc, pes, "r1j", [128, D], F32, 1),
                        "stat": Ring(nc, pes, "r1s", [128, 4], F32, 2), "xhat": Ring(nc, pes, "r1xh", [128, D], BF16, 2),
                        "hT": Ring(nc, pes, "r1hT", [128, D], BF16, 3), "cs": Ring(nc, pes, "r1cs", [128, 256], F32, 2),
                        "qkf": Ring(nc, pes, "r1qkf", [128, 1024], F32, 2), "rt": Ring(nc, pes, "r1rt", [128, 2048], F32, 2),
                        "Qr": Ring(nc, pes, "r1Qr", [128, D], BF16, 2), "Kr": Ring(nc, pes, "r1Kr", [128, D], BF16, 2),
                        "QT": Ring(nc, pes, "r1QT", [128, D], BF16, 2), "KT": Ring(nc, pes, "r1KT", [128, D], BF16, 2),
                        "V": Ring(nc, pes, "r1V", [128, 2048], BF16, 2), "P": Ring(nc, pes, "r1P", [128, 512], BF16, 2),
                        "Kdec": Ring(nc, pes, "r1Kd", [128, D], BF16, 2), "o1": Ring(nc, pes, "r1o1", [128, 2048], F32, 2),
                    }
                    for cg in range(8):
                        dma_pool(wslot(cg).rearrange("p (k c) -> p k c", k=8),
                                 ret_w_in[j][:, cg * 512:(cg + 1) * 512].rearrange("(k p) c -> p k c", p=128), [], [Wb[cg]])
                    zero_state()
                    def stageA0(n):
                        c = {"n": n}
                        row0, v = chunk_rows(n)
                        hT, hTb = R["hT"].next()
                        norm_chunk(R, src_ap, src_buf, row0, v, lambda k, hT=hT: hT[:, k * 128:(k + 1) * 128], hTb)
                        c.update(hT=hT, hTb=hTb)
                        return c

                    def stageA1(c):
                        n = c["n"]
                        row0, v = chunk_rows(n)
                        hT, hTb = c["hT"], c["hTb"]
                        cs, csb = R["cs"].next()
                        dma_sp(cs[:, 0:128], ropec[row0:row0 + 128, :], [], [csb])
                        dma_sp(cs[:, 128:256], ropes[row0:row0 + 128, :], [], [csb])
                        cosb = cs[:, 0:128].unsqueeze(1).to_broadcast([128, 4, 128])
                        sinb = cs[:, 128:256].unsqueeze(1).to_broadcast([128, 4, 128])
                        Qr, Qrb = R["Qr"].next()
                        Kr, Krb = R["Kr"].next()
                        V, Vb = R["V"].next()
                        qf = None
                        for ci, cg in enumerate((2, 3, 0, 1, 4, 5, 6, 7)):
                            pp, ppb = ps_proj[ci % 2], ps_proj_b[ci % 2]
                            wv = wslot(cg).rearrange("p (k c) -> p k c", k=8)
                            mm_group([(pp[:], hT[:, k * 128:(k + 1) * 128], wv[:, k, :], k == 0, k == 7) for k in range(8)],
                                     [hTb, Wb[cg]], [ppb])
                            if cg < 4:
                                if cg % 2 == 0:
                                    qf, qfb = R["qkf"].next()
                                copy("act", qf[:, (cg % 2) * 512:(cg % 2 + 1) * 512], pp[:], [ppb], [qfb])
                                if cg % 2 == 1:
                                    dst, dstb = (Qr, Qrb) if cg < 2 else (Kr, Krb)
                                    eng = "dve" if cg < 2 else "pool"
                                    rt, rtb = R["rt"].next()
                                    q4 = qf[:].rearrange("p (h e j) -> p h e j", h=4, e=2)
                                    te, to = q4[:, :, 0, :], q4[:, :, 1, :]
                                    r4 = rt[:].rearrange("p (a h j) -> p a h j", a=4, h=4)
                                    d4 = dst[:].rearrange("p (h e j) -> p h e j", h=4, e=2)
                                    tt(eng, r4[:, 0], te, cosb, ALU.mult, [qfb, csb], [rtb])
                                    tt(eng, r4[:, 1], to, sinb, ALU.mult, [qfb, csb], [rtb])
                                    tt(eng, r4[:, 2], te, sinb, ALU.mult, [qfb, csb], [rtb])
                                    tt(eng, r4[:, 3], to, cosb, ALU.mult, [qfb, csb], [rtb])
                                    tt(eng, d4[:, :, 0, :], r4[:, 0], r4[:, 1], ALU.subtract, [rtb], [dstb])
                                    tt(eng, d4[:, :, 1, :], r4[:, 2], r4[:, 3], ALU.add, [rtb], [dstb])
                            else:
                                copy("act", V[:, (cg - 4) * 512:(cg - 3) * 512], pp[:], [ppb], [Vb])
                        QT, QTb = R["QT"].next()
                        KT, KTb = R["KT"].next()
                        tr_group([(ps_tr[:, k * 128:(k + 1) * 128], Qr[:, k * 128:(k + 1) * 128]) for k in range(8)], [Qrb], [ps_tr_b])
                        copy("dve", QT[:], ps_tr[:], [ps_tr_b], [QTb])
                        tr_group([(ps_tr[:, k * 128:(k + 1) * 128], Kr[:, k * 128:(k + 1) * 128]) for k in range(8)], [Krb], [ps_tr_b])
                        copy("act", KT[:], ps_tr[:], [ps_tr_b], [KTb])
                        c.update(QT=QT, QTb=QTb, KT=KT, KTb=KTb, Kr=Kr, Krb=Krb, V=V, Vb=Vb)
                        return c

                    def stageB1(c):
                        n = c["n"]
                        o1, o1b = R["o1"].next()
                        scan_part(R, 0, n, c["QT"], c["QTb"], c["KT"], c["KTb"], c["Kr"], c["Krb"], c["V"], c["Vb"], o1, o1b, None, None, True)
                        dma_pool(QTd[n], c["QT"][:], [c["QTb"]], [Qb_d[n]])
                        dma_pool(KTd[n], c["KT"][:], [c["KTb"]], [Kb_d[n]])
                        dma_pool(Krd[n], c["Kr"][:], [c["Krb"]], [Krb_d[n]])
                        dma_pool(Vd[n], c["V"][:], [c["Vb"]], [Vb_d[n]])
                        dma_pool(O1d[n], o1[:], [o1b], [O1b_d[n]])

                    c0s = {0: stageA0(0)}
                    if NCHUNK > 1:
                        c0s[1] = stageA0(1)
                    prev = stageA1(c0s.pop(0))
                    for n in range(NCHUNK):
                        if n + 2 < NCHUNK:
                            c0s[n + 2] = stageA0(n + 2)
                        nxt = stageA1(c0s.pop(n + 1)) if n + 1 < NCHUNK else None
                        stageB1(prev)
                        prev = nxt
                    S.barrier()
                pes = contextlib.ExitStack()
                with pes:
                    R = {
                        "x": Ring(nc, pes, "r2x", [128, D], F32, 2), "junk": Ring(nc, pes, "r2j", [128, D], F32, 1),
                        "stat": Ring(nc, pes, "r2s", [128, 4], F32, 2), "xhat": Ring(nc, pes, "r2xh", [128, D], BF16, 2),
                        "hT": Ring(nc, pes, "r2hT", [128, D], BF16, 2),
                        "QT": Ring(nc, pes, "r2QT", [128, D], BF16, 2), "KT": Ring(nc, pes, "r2KT", [128, D], BF16, 2),
                        "Kr": Ring(nc, pes, "r2Kr", [128, D], BF16, 2),
                        "V": Ring(nc, pes, "r2V", [128, 2048], BF16, 2), "P": Ring(nc, pes, "r2P", [128, 512], BF16, 2),
                        "Kdec": Ring(nc, pes, "r2Kd", [128, D], BF16, 2), "o1": Ring(nc, pes, "r2o1", [128, 2048], F32, 2),
                        "SG": Ring(nc, pes, "r2SG", [128, 2048], BF16, 1), "Z": Ring(nc, pes, "r2Z", [128, 2048], BF16, 1),
                        "ZT": Ring(nc, pes, "r2ZT", [128, 2048], BF16, 1), "tmp": Ring(nc, pes, "r2tmp", [128, D], F32, 1),
                        "xn": Ring(nc, pes, "r2xn", [128, D], F32, 2),
                        "bn": Ring(nc, pes, "r2bn", [128, 48], F32, 2),
                    }
                    R["xg"] = R["xn"]
                    R["wst"] = R["xn"]
                    if PAIR:
                        ex_cout, ex_cob = exchange_start(j, state[:], state_hb, 4096)
                    for cg in range(4):
                        dma_pool(wslot(cg).rearrange("p (k c) -> p k c", k=8),
                                 ret_w_in[j][:, (12 - 4 + cg) * 512:(12 - 3 + cg) * 512].rearrange("(k p) c -> p k c", p=128), [], [Wb[cg]])
                    Wo = W[:, 4 * 4096:8 * 4096].rearrange("p (k c) -> p k c", k=16)
                    for k in range(16):
                        wt_, wtb = R["wst"].next()
                        dma_sp(wt_[:], ret_w_out[j][k * 128:(k + 1) * 128, :], [], [wtb])
                        ts("dve", Wo[:, k, :], wt_[:], gn_t[:, j * 16 + k:j * 16 + k + 1], None, ALU.mult, None, [wtb, small_b], [Wb[4 + k // 4]])
                    zero_state()
                    order = list(range(NCC - 1, -1, -1)) + list(range(NCHUNK - 1, NCC - 1, -1))

                    def loads2(n):
                        c = {"n": n}
                        c["row0"], c["v"] = chunk_rows(n)
                        c["skip"] = last and c["v"] == 1 and not dbg
                        c["Kr"], c["Krb"] = R["Kr"].next()
                        c["V"], c["Vb"] = R["V"].next()
                        dma_sp(c["Kr"][:], Krd[n], [Krb_d[n]], [c["Krb"]])
                        dma_sp(c["V"][:], Vd[n], [Vb_d[n]], [c["Vb"]])
                        if not c["skip"]:
                            c["QT"], c["QTb"] = R["QT"].next()
                            c["KT"], c["KTb"] = R["KT"].next()
                            dma_sp(c["QT"][:], QTd[n], [Qb_d[n]], [c["QTb"]])
                            dma_sp(c["KT"][:], KTd[n], [Kb_d[n]], [c["KTb"]])
                        return c

                    def stageB2(c):
                        n = c["n"]
                        if n == NCHUNK - 1 and PAIR:
                            exchange_finish(R, ex_cout, ex_cob, state, state_hb, 4096)
                            copy("pool", state_bf[:], state[:], state_hb, statebf_hb)
                        if c["skip"]:
                            scan_part(R, 1, n, None, None, None, None, c["Kr"], c["Krb"], c["V"], c["Vb"], None, None, None, None, False)
                            return
                        o1, o1b = c["o1"], c["o1b"]
                        scan_part(R, 1, n, c["QT"], c["QTb"], c["KT"], c["KTb"], c["Kr"], c["Krb"], c["V"], c["Vb"], o1, o1b, o1, o1b, True)

                    def stageC2(c):
                        if c["skip"]:
                            return
                        n, row0, v = c["n"], c["row0"], c["v"]
                        o1, o1b = c["o1"], c["o1b"]
                        hT, hTb = R["hT"].next()
                        xt, xb = norm_chunk(R, src_ap, src_buf, row0, v, lambda k, hT=hT: hT[:, k * 128:(k + 1) * 128], hTb)
                        SG, SGb = R["SG"].next()
                        for cg in range(4):
                            pp, ppb = ps_proj[cg % 2], ps_proj_b[cg % 2]
                            wv = wslot(cg).rearrange("p (k c) -> p k c", k=8)
                            mm_group([(pp[:], hT[:, k * 128:(k + 1) * 128], wv[:, k, :], k == 0, k == 7) for k in range(8)],
                                     [hTb, Wb[cg]], [ppb])
                            act(SG[:, cg * 512:(cg + 1) * 512], pp[:], AF.Silu, [ppb], [SGb])
                        bn, bnb = R["bn"].next()
                        for h in range(4):
                            S.add("dve", "c", [lambda e, h=h, bn=bn, o1=o1: e.bn_stats(out=bn[:, h * 6:(h + 1) * 6], in_=o1[:, h * 512:(h + 1) * 512])], [o1b], [bnb])
                        for h in range(4):
                            S.add("dve", "c", [lambda e, h=h, bn=bn: e.bn_aggr(out=bn[:, 24 + h * 2:24 + h * 2 + 2], in_=bn[:, h * 6:(h + 1) * 6])], [bnb], [bnb])
                        mv = bn[:, 24:32].rearrange("p (h t) -> p h t", t=2)
                        act(bn[:, 32:36], mv[:, :, 1], AF.Sqrt, [bnb, cst_b], [bnb], scale=1.0, bias=eps_col)
                        S.add("dve", "c", [lambda e, bn=bn: e.reciprocal(out=bn[:, 36:40], in_=bn[:, 32:36])], [bnb], [bnb])
                        for h in range(4):
                            ts("dve", o1[:, h * 512:(h + 1) * 512], o1[:, h * 512:(h + 1) * 512], bn[:, 24 + h * 2:24 + h * 2 + 1],
                               bn[:, 36 + h:37 + h], ALU.subtract, ALU.mult, [o1b, bnb], [o1b])
                        Z, Zb = R["Z"].next()
                        tt("pool", Z[:], o1[:], SG[:], ALU.mult, [o1b, SGb], [Zb])
                        ZT, ZTb = R["ZT"].next()
                        for half in range(2):
                            tr_group([(ps_tr[:, k * 128:(k + 1) * 128], Z[:, (half * 8 + k) * 128:(half * 8 + k + 1) * 128]) for k in range(8)],
                                     [Zb], [ps_tr_b])
                            copy("act" if half == 0 else "dve", ZT[:, half * 1024:(half + 1) * 1024], ps_tr[:], [ps_tr_b], [ZTb])
                        for g in range(2):
                            mm_group([(ps_proj[g][:], ZT[:, k * 128:(k + 1) * 128], Wo[:, k, g * 512:(g + 1) * 512], k == 0, k == 15) for k in range(16)],
                                     [ZTb] + Wb[4:8], [ps_proj_b[g]])
                        if last and v == 1:
                            post_chunk(R, xt, xb, v, dbg_ctx, dst_buf, row0)
                        elif last:
                            post_chunk(R, xt, xb, v, out, dst_buf, row0 - CTX)
                        else:
                            post_chunk(R, xt, xb, v, dst_ap, dst_buf, row0)

                    cs2 = {0: loads2(order[0])}
                    for i, n in enumerate(order):
                        c = cs2[i]
                        if not c["skip"]:
                            c["o1"], c["o1b"] = R["o1"].next()
                            dma_sp(c["o1"][:], O1d[n], [O1b_d[n]], [c["o1b"]])
                        if i + 1 < len(order):
                            cs2[i + 1] = loads2(order[i + 1])
                        stageB2(c)
                        if i >= 1:
                            stageC2(cs2[i - 1])
                            del cs2[i - 1]
                    stageC2(cs2[len(order) - 1])
                    S.barrier()

        def lru_layer(l, src_ap, src_buf, dst_ap, dst_buf, last):
            j = l // 2
            tiles = [("c", 0, CTX)] + [("l", i * 512, 512) for i in range(NL // 512)]
            XR = {"c": XRc, "l": XRl}
            XC = {"c": XCc, "l": XCl}
            H1 = {"c": H1c, "l": H1l}
            SGD = {"c": SGc, "l": SGl}
            les = contextlib.ExitStack()
            with les:
                lsb = lambda n, s, d=F32: les.enter_context(nc.sbuf_tensor(f"{n}_L{l}", s, d))
                prep_layer(l, les)
                cl = lsb("cl", [128, 20])
                sp_t = lsb("sp_t", [128, 120])
                cl_b = Buf("cl")
                hprev = lsb("hprev", [128, 2 * NCH])
                hprev_b = Buf("hprev")
                lam_j = lam_t[:, j * 20:(j + 1) * 20]
                A0 = lambda i: sp_t[:, i * 20:(i + 1) * 20]
                ts("dve", A0(1), lam_j, -1.0, None, ALU.mult, None, [small_b], [cl_b])
                tt("dve", A0(0), lam_j, A0(1), ALU.min, [small_b, cl_b], [cl_b])
                act(A0(1), A0(0), AF.Exp, [cl_b], [cl_b])
                ts("dve", A0(2), A0(1), 2.0, None, ALU.add, None, [cl_b], [cl_b])
                S.add("dve", "c", [lambda e: e.reciprocal(out=A0(3), in_=A0(2))], [cl_b], [cl_b])
                tt("dve", A0(2), A0(1), A0(3), ALU.mult, [cl_b], [cl_b])
                tt("dve", A0(3), A0(2), A0(2), ALU.mult, [cl_b], [cl_b])
                ts("dve", A0(4), A0(3), 1.0 / 17.0, 1.0 / 15.0, ALU.mult, ALU.add, [cl_b], [cl_b])
                for cden in (13.0, 11.0, 9.0, 7.0, 5.0, 3.0, 1.0):
                    tt("dve", A0(4), A0(4), A0(3), ALU.mult, [cl_b], [cl_b])
                    ts("dve", A0(4), A0(4), 1.0 / cden, None, ALU.add, None, [cl_b], [cl_b])
                tt("dve", A0(4), A0(4), A0(2), ALU.mult, [cl_b], [cl_b])
                ts("dve", A0(5), lam_j, -1.0, 0.0, ALU.mult, ALU.max, [small_b, cl_b], [cl_b])
                stt(A0(5), A0(4), 2.0, A0(5), ALU.mult, ALU.add, [cl_b], [cl_b])
                ts("dve", cl[:], A0(5), -8.0, None, ALU.mult, None, [cl_b], [cl_b])

                XRb = {("c", 0): Buf("xrc")}
                XCb, H1b, SGb_d = {}, {}, {}
                for (sq, t0, nt) in tiles:
                    XRb[(sq, t0)] = Buf(f"xr{sq}{t0}")
                    XCb[(sq, t0)] = Buf(f"xc{sq}{t0}")
                    H1b[(sq, t0)] = Buf(f"h1{sq}{t0}")
                    SGb_d[(sq, t0)] = Buf(f"sg{sq}{t0}")
                pad_b = {("c", "L"): Buf("padcL"), ("c", "R"): Buf("padcR"), ("l", "L"): Buf("padlL"), ("l", "R"): Buf("padlR")}
                for s in range(5):
                    dma_pool(wslot(s).rearrange("p (k c) -> p k c", k=8),
                             lru_w_in[j][:, s * 512:(s + 1) * 512].rearrange("(k p) c -> p k c", p=128), [], [Wb[s]])
                GW = W[:, 5 * 4096:5 * 4096 + 40 * 128].rearrange("p (d g c j) -> p d g c j", d=2, g=2, c=NCH)
                for d in range(2):
                    dma_pool(GW[:, d, 0], lru_wa[j, d].rearrange("c i j -> i c j"), [], [Wb[5], Wb[6]])
                    dma_pool(GW[:, d, 1], lru_wx[j, d].rearrange("c i j -> i c j"), [], [Wb[5], Wb[6]])
                z3 = zeros_f[:, 0:20].rearrange("p (c t) -> p c t", t=2)
                for sq, ln in (("c", CTX), ("l", NL)):
                    dma_pool(XR[sq][:, :, 0:2].rearrange("c p t -> p c t"), z3, [cst_b], [pad_b[(sq, "L")]])
                    if sq == "c" or not PAIR:
                        dma_pool(XR[sq][:, :, ln + 2:ln + 4].rearrange("c p t -> p c t"), z3, [cst_b], [pad_b[(sq, "R")]])

                pes = contextlib.ExitStack()
                with pes:
                    R = {
                        "x": Ring(nc, pes, "l0x", [128, D], F32, 2), "junk": Ring(nc, pes, "l0j", [128, D], F32, 1),
                        "stat": Ring(nc, pes, "l0s", [128, 4], F32, 2), "xhat": Ring(nc, pes, "l0xh", [128, D], BF16, 2),
                        "hT": Ring(nc, pes, "l0hT", [128, 8 * 512], BF16, 2), "xr": Ring(nc, pes, "l0xr", [128, 512], F32, 3),
                        "sg": Ring(nc, pes, "l0sg", [128, 512], BF16, 3), "xg": Ring(nc, pes, "l0xg", [128, 1024], F32, 2),
                        "halo": Ring(nc, pes, "l0halo", [128, 64], F32, 2),
                    }
                    for (sq, t0, nt) in tiles:
                        base = 0 if sq == "c" else CTX
                        v = 1 if sq == "c" else 0
                        hT, hTb = R["hT"].next()
                        hv = hT[:].rearrange("p (k t) -> p k t", k=8)
                        for c in range(nt // 128):
                            norm_chunk(R, src_ap, src_buf, base + t0 + c * 128, v, lambda k, hv=hv, c=c: hv[:, k, c * 128:(c + 1) * 128], hTb)
                        for cc in range(2 * NCH):
                            pp, ppb = ps_proj[cc % 2], ps_proj_b[cc % 2]
                            wv = wslot(cc // 4).rearrange("p (k c) -> p k c", k=8)
                            mm_group([(pp[:, 0:nt], wv[:, k, (cc % 4) * 128:(cc % 4 + 1) * 128], hv[:, k, 0:nt], k == 0, k == 7) for k in range(8)],
                                     [hTb, Wb[cc // 4]], [ppb])
                            if cc < NCH:
                                xr, xrb = R["xr"].next()
                                copy("act", xr[:, 0:nt], pp[:, 0:nt], [ppb], [xrb])
                                dma_pool(XR[sq][cc, :, 2 + t0:2 + t0 + nt], xr[:, 0:nt], [xrb], [XRb[(sq, t0)]])
                            else:
                                sg, sgb = R["sg"].next()
                                act(sg[:, 0:nt], pp[:, 0:nt], AF.Silu, [ppb], [sgb])
                                dma_pool(SGD[sq][cc - NCH, :, t0:t0 + nt], sg[:, 0:nt], [sgb], [SGb_d[(sq, t0)]])
                    if PAIR:
                        lastb = XRb[("l", NL - 512)]
                        hl, hlb = R["halo"].next()
                        dma_sp(hl[:, 0:20].rearrange("p (c t) -> p c t", t=2), XRl[:, :, NL:NL + 2].rearrange("c p t -> p c t"), [lastb], [hlb])
                        hr, hrb = R["halo"].next()
                        hc_, hcb_ = exchange_start(l, hl[:, 0:32], hlb, 32)
                        exchange_finish(R, hc_, hcb_, hr, hrb, 32)
                        h3 = hr[:, 0:20].rearrange("p (c t) -> p c t", t=2)
                        dma_pool(XRl[:, :, NL + 2:NL + 3].rearrange("c p t -> p c t"), h3[:, :, 1:2], [hrb], [pad_b[("l", "R")]], slow=True)
                        dma_pool(XRl[:, :, NL + 3:NL + 4].rearrange("c p t -> p c t"), h3[:, :, 0:1], [hrb], [pad_b[("l", "R")]], slow=True)
                    S.barrier()

                GS = 5
                hp1_cb = [Buf(f"hp1_{c}") for c in range(NCH)]
                hp2_cb = [Buf(f"hp2_{c}") for c in range(NCH)]

                def gates_p1(R, d, cc, nt, xc, xcb):
                    xb16, xb16b = R["xcb"].next()
                    copy("act", xb16[:, 0:nt], xc[:, 0:nt], [xcb], [xb16b])
                    pr, prb = ps_o[cc % 2], ps_o_b[cc % 2]
                    pg, pgb = ps_su[cc % 2], ps_su_b[cc % 2]
                    mm_group([(pr[:, 0:nt], GW[:, d, 0, cc, :], xb16[:, 0:nt], True, True)], [xb16b, Wb[5], Wb[6]], [prb])
                    mm_group([(pg[:, 0:nt], GW[:, d, 1, cc, :], xb16[:, 0:nt], True, True)], [xb16b, Wb[5], Wb[6]], [pgb])
                    r_, rb = R["r"].next()
                    gi, gib = R["gi"].next()
                    bcol = (j * 2 + d) * NCH + cc
                    act(r_[:, 0:nt], pr[:, 0:nt], AF.Sigmoid, [prb, small_b], [rb], bias=ba_t[:, bcol:bcol + 1])
                    act(gi[:, 0:nt], pg[:, 0:nt], AF.Sigmoid, [pgb, small_b], [gib], bias=bx_t[:, bcol:bcol + 1])
                    return dict(cc=cc, xc=xc, xcb=xcb, r=r_, rb=rb, gi=gi, gib=gib)

                def gates_rest(R, d, nt, cx):
                    for c_ in cx:
                        c_["a"], c_["ab"] = R["a"].next()
                        cc = c_["cc"]
                        act(c_["a"][:, 0:nt], c_["r"][:, 0:nt], AF.Exp, [c_["rb"], cl_b], [c_["ab"]], scale=cl[:, d * NCH + cc:d * NCH + cc + 1])
                    for c_ in cx:
                        c_["q"], c_["qb"] = R["q"].next()
                        tt("pool", c_["q"][:, 0:nt], c_["a"][:, 0:nt], c_["a"][:, 0:nt], ALU.mult, [c_["ab"]], [c_["qb"]])
                    for c_ in cx:
                        act(c_["q"][:, 0:nt], c_["q"][:, 0:nt], AF.Sqrt, [c_["qb"], cst_b], [c_["qb"]], scale=-1.0, bias=one_col)
                    for c_ in cx:
                        c_["u"], c_["ub"] = R["u"].next()
                        tt("pool", c_["u"][:, 0:nt], c_["q"][:, 0:nt], c_["gi"][:, 0:nt], ALU.mult, [c_["qb"], c_["gib"]], [c_["ub"]])
                        tt("dve", c_["u"][:, 0:nt], c_["u"][:, 0:nt], c_["xc"][:, 0:nt], ALU.mult, [c_["ub"], c_["xcb"]], [c_["ub"]])


                pes = contextlib.ExitStack()
                with pes:
                    R = {
                        "win": Ring(nc, pes, "l1w", [128, 516], F32, 2), "xc": Ring(nc, pes, "l1xc", [128, 512], F32, 6),
                        "xcb": Ring(nc, pes, "l1xcb", [128, 512], BF16, 2), "r": Ring(nc, pes, "l1r", [128, 512], F32, 6),
                        "gi": Ring(nc, pes, "l1gi", [128, 512], F32, 6), "a": Ring(nc, pes, "l1a", [128, 512], F32, 6),
                        "q": Ring(nc, pes, "l1q", [128, 512], F32, 6), "u": Ring(nc, pes, "l1u", [128, 512], F32, 6),
                        "h": Ring(nc, pes, "l1h", [128, 512], F32, 3),
                    }
                    S.add("dve", "c", [lambda e: e.memset(hprev[:], 0.0)], [], hp1_cb + hp2_cb)
                    for ti, (sq, t0, nt) in enumerate(tiles):
                        ln = CTX if sq == "c" else NL
                        rd = [XRb[(sq, t0)]]
                        if t0 >= 512:
                            rd.append(XRb[(sq, t0 - 512)])
                        else:
                            rd.append(pad_b[(sq, "L")])
                        if t0 + nt < ln:
                            rd.append(XRb[(sq, t0 + nt)])
                        else:
                            rd.append(pad_b[(sq, "R")])
                        for g0 in range(0, NCH, GS):
                            cx = []
                            for cc in range(g0, g0 + GS):
                                win, winb = R["win"].next()
                                dma_sp(win[:, 0:nt + 4], XR[sq][cc, :, t0:t0 + nt + 4], rd, [winb])
                                xc, xcb = R["xc"].next()
                                wc = lambda tap, cc=cc: cw_t[:, (j * NCH + cc) * 5 + tap:(j * NCH + cc) * 5 + tap + 1]
                                ts("dve", xc[:, 0:nt], win[:, 0:nt], wc(0), cb_t[:, j * NCH + cc:j * NCH + cc + 1], ALU.mult, ALU.add,
                                   [winb, small_b], [xcb])
                                for tap in range(1, 5):
                                    stt(xc[:, 0:nt], win[:, tap:tap + nt], wc(tap), xc[:, 0:nt], ALU.mult, ALU.add, [winb, small_b, xcb], [xcb])
                                dma_pool(XC[sq][cc, :, t0:t0 + nt], xc[:, 0:nt], [xcb], [XCb[(sq, t0)]])
                                cx.append(gates_p1(R, 0, cc, nt, xc, xcb))
                            gates_rest(R, 0, nt, cx)
                            for c_ in cx:
                                cc, a_, ab, u_, ub = c_["cc"], c_["a"], c_["ab"], c_["u"], c_["ub"]
                                h_, hb = R["h"].next()
                                S.add("dve", "c", [lambda e, h_=h_, a_=a_, u_=u_, cc=cc, nt=nt: e.tensor_tensor_scan(
                                    out=h_[:, 0:nt], data0=a_[:, 0:nt], data1=u_[:, 0:nt], initial=hprev[:, cc:cc + 1], op0=ALU.mult, op1=ALU.add)],
                                    [ab, ub, hp1_cb[cc]], [hb])
                                copy("act", hprev[:, cc:cc + 1], h_[:, nt - 1:nt], [hb], [hp1_cb[cc]])
                                dma_pool(H1[sq][cc, :, t0:t0 + nt], h_[:, 0:nt], [hb], [H1b[(sq, t0)]])
                    S.barrier()

                pes = contextlib.ExitStack()
                with pes:
                    R = {
                        "xc": Ring(nc, pes, "l2xc", [128, 512], F32, 5),
                        "xcb": Ring(nc, pes, "l2xcb", [128, 512], BF16, 2), "r": Ring(nc, pes, "l2r", [128, 512], F32, 5),
                        "gi": Ring(nc, pes, "l2gi", [128, 512], F32, 5), "a": Ring(nc, pes, "l2a", [128, 512], F32, 5),
                        "q": Ring(nc, pes, "l2q", [128, 512], F32, 5), "u": Ring(nc, pes, "l2u", [128, 512], F32, 5),
                        "h": Ring(nc, pes, "l2h", [128, 512], F32, 2), "h1": Ring(nc, pes, "l2h1", [128, 512], F32, 2),
                        "sg": Ring(nc, pes, "l2sg", [128, 512], BF16, 2), "Z": Ring(nc, pes, "l2Z", [128, NCH * 512], BF16, 1),
                        "x": Ring(nc, pes, "l2x", [128, D], F32, 2), "junk": Ring(nc, pes, "l2j", [128, D], F32, 1),
                        "stat": Ring(nc, pes, "l2s", [128, 4], F32, 2), "tmp": Ring(nc, pes, "l2tmp", [128, D], F32, 1),
                        "xn": Ring(nc, pes, "l2xn", [128, D], F32, 2),
                    }
                    R["xg"] = R["xn"]
                    Wo = W[:, 0:NCH * 1024].rearrange("p (k c) -> p k c", k=NCH)
                    dma_pool(Wo, lru_w_out[j].rearrange("(k p) c -> p k c", p=128), [], [Wb[0], Wb[1], Wb[2]])
                    hp2 = hprev[:, NCH:2 * NCH]
                    if PAIR:
                        pst_ = pes.enter_context(nc.sbuf_tensor(f"lpstate{l}", [128, 32], F32))
                        pst_b = Buf("lpstate")
                        hx, hxb = R["xg"].next()
                        copy("dve", hx[:, 0:NCH], hprev[:, 0:NCH], hp1_cb, [hxb])
                        sc_, scb_ = exchange_start(l - 1, hx[:, 0:32], hxb, 32)
                        exchange_finish(R, sc_, scb_, pst_, pst_b, 32)
                    order = [tiles[0]] + tiles[:0:-1]
                    for (sq, t0, nt) in order:
                        base = 0 if sq == "c" else CTX
                        v = 1 if sq == "c" else 0
                        if PAIR and sq == "l" and t0 == NL - 512:
                            copy("dve", hp2, pst_[:, 0:NCH], [pst_b] + hp2_cb, hp2_cb)
                        skip_out = last and sq == "c" and not dbg
                        Z, Zb = R["Z"].next()
                        zv = Z[:].rearrange("p (c t) -> p c t", c=NCH)
                        for g0 in range(0, NCH, GS):
                            cx = []
                            for cc in range(g0, g0 + GS):
                                xc, xcb = R["xc"].next()
                                dma_sp(xc[:, 0:nt], XC[sq][cc, :, t0:t0 + nt], [XCb[(sq, t0)]], [xcb])
                                cx.append(gates_p1(R, 1, cc, nt, xc, xcb))
                            gates_rest(R, 1, nt, cx)
                            for c_ in cx:
                                cc, a_, ab, u_, ub = c_["cc"], c_["a"], c_["ab"], c_["u"], c_["ub"]
                                h_, hb = R["h"].next()
                                S.add("dve", "c", [lambda e, h_=h_, a_=a_, u_=u_, cc=cc, nt=nt: e.tensor_tensor_scan(
                                    out=h_[:, 0:nt][:, ::-1], data0=a_[:, 0:nt][:, ::-1], data1=u_[:, 0:nt][:, ::-1],
                                    initial=hp2[:, cc:cc + 1], op0=ALU.mult, op1=ALU.add)], [ab, ub, hp2_cb[cc]], [hb])
                                copy("act", hp2[:, cc:cc + 1], h_[:, 0:1], [hb], [hp2_cb[cc]])
                                if skip_out:
                                    continue
                                h1, h1b = R["h1"].next()
                                dma_sp(h1[:, 0:nt], H1[sq][cc, :, t0:t0 + nt], [H1b[(sq, t0)]], [h1b])
                                sg, sgb = R["sg"].next()
                                dma_sp(sg[:, 0:nt], SGD[sq][cc, :, t0:t0 + nt], [SGb_d[(sq, t0)]], [sgb])
                                tt("pool", h1[:, 0:nt], h1[:, 0:nt], h_[:, 0:nt], ALU.add, [h1b, hb], [h1b])
                                tt("dve", zv[:, cc, 0:nt], h1[:, 0:nt], sg[:, 0:nt], ALU.mult, [h1b, sgb], [Zb])
                        if skip_out:
                            continue
                        for c in range(nt // 128):
                            row0 = base + t0 + c * 128
                            xt, xb = R["x"].next()
                            dma_sp(xt[:], src_ap[row0:row0 + 128, :], [src_buf], [xb])
                            for g in range(2):
                                mm_group([(ps_proj[g][:], zv[:, cc, c * 128:(c + 1) * 128], Wo[:, cc, g * 512:(g + 1) * 512], cc == 0, cc == NCH - 1)
                                          for cc in range(NCH)], [Zb, Wb[0], Wb[1], Wb[2]], [ps_proj_b[g]])
                            if last and v == 1:
                                post_chunk(R, xt, xb, v, dbg_ctx, dst_buf, row0)
                            elif last:
                                post_chunk(R, xt, xb, v, out, dst_buf, row0 - CTX)
                            else:
                                post_chunk(R, xt, xb, v, dst_ap, dst_buf, row0)
                    S.barrier()

        xin_b = Buf("xin")
        Xb = [Buf("Xs0"), Buf("Xs1")]
        out_b = Buf("out")
        src_ap, src_buf = xin, xin_b
        for l in range(depth):
            last = l == depth - 1
            dst_ap, dst_buf = (Xs[l % 2], Xb[l % 2])
            if last:
                dst_buf = out_b
            if l % 2 == 0:
                retention_layer(l, src_ap, src_buf, dst_ap, dst_buf, last)
            else:
                lru_layer(l, src_ap, src_buf, dst_ap, dst_buf, last)
            src_ap, src_buf = dst_ap, dst_buf
            if not last:
                S.rotate()
        S.final_wait("sp")

        block = es.enter_context(nc.Block())

        @block.tensor
        def _(e):
            S.emit("pe", e)

        @block.scalar
        def _(e):
            S.emit("act", e)

        @block.vector
        def _(e):
            S.emit("dve", e)

        @block.gpsimd
        def _(e):
            S.emit("pool", e)

        @block.sync
        def _(e):
            S.emit("sp", e)
    return nc


def _col(vec, nk):
    return np.ascontiguousarray(np.asarray(vec, np.float32).reshape(nk, 128).T)


def _rope_tables():
    n_rows = SEQ // GRID_W
    row = np.repeat(np.arange(n_rows, dtype=np.float32), GRID_W)
    col = np.tile(np.arange(GRID_W, dtype=np.float32), n_rows)
    n_freq = 64
    inv = (np.float32(10000.0) ** (-np.arange(n_freq, dtype=np.float32) / np.float32(n_freq))).astype(np.float32)
    ang = np.concatenate([row[:, None] * inv, col[:, None] * inv], axis=-1).astype(np.float32)
    return np.cos(ang).astype(np.float32), np.sin(ang).astype(np.float32)


def _core_inputs(core, inp, shared):
    if PAIR:
        b, half = core // 2, core % 2
    else:
        b, half = core, 0
    flip = half == 1
    dirs = (1, 0) if flip else (0, 1)
    x = inp["x"][b]
    ctx = inp["ctx"][b]
    cos, sin = shared["rope"]
    if PAIR:
        xs = x[half * NL:(half + 1) * NL]
        cs, sn = cos[half * NL:(half + 1) * NL], sin[half * NL:(half + 1) * NL]
    else:
        xs, cs, sn = x, cos, sin
    if flip:
        xs, cs, sn, ctx = xs[::-1], cs[::-1], sn[::-1], ctx[::-1]
    m = {}
    m["xin"] = np.ascontiguousarray(np.concatenate([ctx, xs], 0), dtype=np.float32)
    m["ropec"] = np.ascontiguousarray(np.concatenate([np.ones((CTX, 128), np.float32), cs], 0))
    m["ropes"] = np.ascontiguousarray(np.concatenate([np.zeros((CTX, 128), np.float32), sn], 0))
    cc = np.stack([_col(inp["c"][b], 8), _col(inp["c_ctx"], 8)], -1).reshape(128, 16)
    m["c_col"] = np.ascontiguousarray(cc)
    lg = np.asarray(inp["ret_log_decay"], np.float32)[:, list(dirs), :]
    m["ret_lg"] = np.ascontiguousarray(np.broadcast_to(lg.reshape(1, 16), (128, 16)))
    sel_d = list(dirs)
    m["lru_wa"] = np.ascontiguousarray(np.asarray(inp["lru_w_a"], np.float32)[:, sel_d])
    m["lru_wx"] = np.ascontiguousarray(np.asarray(inp["lru_w_x"], np.float32)[:, sel_d])
    def dcol(a):
        a = np.asarray(a, np.float32)[:, sel_d]
        return np.ascontiguousarray(np.concatenate([_col(a[j, d], NCH) for j in range(2) for d in range(2)], 1))
    m["lru_ba"] = dcol(inp["lru_b_a"])
    m["lru_bx"] = dcol(inp["lru_b_x"])
    m["lru_lam"] = dcol(inp["lru_lambda"])
    cw = np.asarray(inp["lru_conv_w"], np.float32)
    z = np.zeros_like(cw[:, :1])
    cw5 = np.concatenate([z, cw[:, ::-1]], 1) if flip else np.concatenate([cw, z], 1)
    cwc = np.stack([np.stack([_col(cw5[j, t], NCH) for t in range(5)], -1) for j in range(2)], 1)
    m["lru_cw"] = np.ascontiguousarray(cwc.reshape(128, 100))
    cst = shared["consts"].copy()
    if PAIR:
        cst[:, 649] = 1.0 if half == 1 else 0.0
        cst[:, 650] = 1.0 if half == 0 else 0.0
    m["consts"] = cst
    for k in ("mod_w", "mod_b_col", "npre_col", "npost_col", "ret_w_in", "ret_gn_col", "ret_w_out", "lru_w_in", "lru_cb", "lru_w_out"):
        m[k] = shared[k]
    return m


def _shared_inputs(inp):
    sh = {}
    sh["rope"] = _rope_tables()
    sh["mod_w"] = np.ascontiguousarray(inp["mod_w"], dtype=np.float32)
    sh["mod_b_col"] = np.ascontiguousarray(np.concatenate([_col(inp["mod_b"][l], 24) for l in range(DEPTH)], 1))
    sh["npre_col"] = np.ascontiguousarray(np.concatenate([_col(inp["norm_pre"][l], 8) for l in range(DEPTH)], 1))
    sh["npost_col"] = np.ascontiguousarray(np.concatenate([_col(inp["norm_post"][l], 8) for l in range(DEPTH)], 1))
    perm = np.arange(6144)
    for blk in range(2):
        for h in range(RET_H):
            base = blk * 1024 + h * 256
            perm[base:base + 256] = np.concatenate([base + np.arange(0, 256, 2), base + np.arange(1, 256, 2)])
    sh["ret_w_in"] = np.ascontiguousarray(np.asarray(inp["ret_w_in"], np.float32)[:, :, perm])
    sh["ret_gn_col"] = np.ascontiguousarray(np.concatenate([_col(inp["ret_gn"][j], 16) for j in range(2)], 1))
    sh["ret_w_out"] = np.ascontiguousarray(inp["ret_w_out"], dtype=np.float32)
    sh["lru_w_in"] = np.ascontiguousarray(inp["lru_w_in"], dtype=np.float32)
    sh["lru_cb"] = np.ascontiguousarray(np.concatenate([_col(inp["lru_conv_b"][j], NCH) for j in range(2)], 1))
    sh["lru_w_out"] = np.ascontiguousarray(inp["lru_w_out"], dtype=np.float32)
    cst = np.zeros((128, 128 * 5 + 16), np.float32)
    p = np.arange(128, dtype=np.float32)
    cst[:, 0:128] = np.eye(128, dtype=np.float32)
    cst[:, 128:256] = 1.0
    cst[:, 256:384] = (p[:, None] <= p[None, :])
    cst[:, 384:512] = (p[:, None] >= p[None, :])
    coefs = np.stack([127 - p, p + 1, -(p + 1), p, 128 - p, p - 128, np.full(128, 128.0, np.float32)], 1)
    cst[:, 640:647] = coefs
    cst[:, 647] = EPS
    cst[:, 648] = 1.0
    sh["consts"] = cst
    return sh


_NC_CACHE = {}


def kernel(**inputs):
    inp = {k: np.asarray(v) for k, v in inputs.items()}
    if "nc" not in _NC_CACHE:
        _NC_CACHE["nc"] = build_program()
    nc = _NC_CACHE["nc"]
    shared = _shared_inputs(inp)
    in_maps = [_core_inputs(c, inp, shared) for c in range(NCORES)]
    res = run_bass_kernel_spmd(nc, in_maps, core_ids=list(range(NCORES)))
    outp = np.empty((BATCH, SEQ, D), np.float32)
    for c in range(NCORES):
        o = np.asarray(res.results[c]["out"], np.float32)
        if PAIR:
            b, half = c // 2, c % 2
            outp[b, half * NL:(half + 1) * NL] = o[::-1] if half == 1 else o
        else:
            outp[c] = o
    return outp
```

```python
import contextlib
import numpy as np
import concourse.bass as bass
import concourse.mybir as mybir
from concourse.bass_utils import run_bass_kernel_spmd

F32 = mybir.dt.float32
BF16 = mybir.dt.bfloat16
AF = mybir.ActivationFunctionType
ALU = mybir.AluOpType
AX = mybir.AxisListType

D = 1024
DEPTH = 4
SEQ = 8192
BATCH = 4
CTX = 256
GRID_W = 64
EPS = 1e-6
RET_H = 4
LRU_W = 1280
NCH = 10
PAIR = True
NCORES = 8 if PAIR else 4
NL = SEQ // 2 if PAIR else SEQ
NTOK = CTX + NL
NCHUNK = NTOK // 128
NCC = CTX // 128


class Sem:
    def __init__(self, handle, stream, cls, step):
        self.h = handle
        self.stream = stream
        self.cls = cls
        self.step = step
        self.count = 0


class Buf:
    __slots__ = ("name", "w", "r")

    def __init__(self, name=""):
        self.name = name
        self.w = None
        self.r = {}


class Stream:
    def __init__(self, name):
        self.name = name
        self.items = []
        self.waited = {}


class Sched:
    NDMA = 14

    def __init__(self, nc, es):
        self.nc = nc
        self.es = es
        self.streams = {n: Stream(n) for n in ("pe", "act", "dve", "pool", "sp")}
        self.sems = {}
        self.all_sems = []
        self.nsem = 0
        self.dma_sems = {}
        self.dma_rr = {}
        for st in ("pool", "sp"):
            self.dma_sems[st] = [self._new_sem(st, "d", 16) for _ in range(self.NDMA)]
            self.dma_rr[st] = 0
        self.rotate()

    def _new_sem(self, st, cls, step):
        h = self.es.enter_context(self.nc.semaphore(f"s{self.nsem}_{st}_{cls}"))
        self.nsem += 1
        sm = Sem(h, st, cls, step)
        self.all_sems.append(sm)
        return sm

    def rotate(self):
        for st in ("pe", "act", "dve", "pool"):
            self.sems[(st, "c")] = self._new_sem(st, "c", 1)

    def add(self, stream, cls, fns, reads=(), writes=()):
        st = self.streams[stream]
        need = {}
        if cls == "d":
            i = self.dma_rr[stream]
            self.dma_rr[stream] = (i + 1) % self.NDMA
            sm = self.dma_sems[stream][i]
            if sm.count > 0 and st.waited.get(sm, 0) < sm.count:
                need[sm] = sm.count
        elif cls == "cc":
            sm = self._new_sem(stream, "cc", 1)
        else:
            sm = self.sems[(stream, cls)]
        raw = set()
        oth = set()
        for b in reads:
            if b.w is not None:
                raw.add(b.w)
        for b in writes:
            if b.w is not None:
                oth.add(b.w)
            for k, v in b.r.items():
                oth.add((k, v))
        for (dsm, v) in raw | oth:
            if dsm.stream == stream:
                if stream == "pe":
                    continue
                if dsm.cls == "c" and cls == "c" and (dsm, v) not in raw:
                    continue
            if st.waited.get(dsm, 0) >= v:
                continue
            if need.get(dsm, 0) < v:
                need[dsm] = v
        for k, v in need.items():
            st.waited[k] = v
        sm.count += sm.step
        tok = (sm, sm.count)
        st.items.append((list(need.items()), fns, sm))
        for b in reads:
            if b.r.get(sm, 0) < sm.count:
                b.r[sm] = sm.count
        for b in writes:
            b.w = tok
            b.r = {}
        return tok

    def barrier(self):
        for st in self.streams.values():
            need = []
            for sm in self.all_sems:
                if sm.count > 0 and st.waited.get(sm, 0) < sm.count:
                    if sm.stream == st.name and st.name == "pe":
                        continue
                    need.append((sm, sm.count))
                    st.waited[sm] = sm.count
            if need:
                st.items.append((need, [], None))

    def final_wait(self, stream="sp"):
        st = self.streams[stream]
        need = [(sm, sm.count) for sm in self.all_sems if sm.count > 0 and st.waited.get(sm, 0) < sm.count]
        st.items.append((need, [], None))

    def emit(self, stream, eng):
        for waits, fns, sm in self.streams[stream].items:
            for (wsm, v) in waits:
                eng.wait_ge(wsm.h, v)
            ins = None
            for f in fns:
                ins = f(eng)
            if ins is not None and sm is not None:
                ins.then_inc(sm.h, sm.step)


class Ring:
    _uid = [0]

    def __init__(self, nc, es, name, shape, dtype, n):
        Ring._uid[0] += 1
        name = f"{name}u{Ring._uid[0]}_"
        self.t = [es.enter_context(nc.sbuf_tensor(f"{name}{i}", shape, dtype)) for i in range(n)]
        self.b = [Buf(f"{name}{i}") for i in range(n)]
        self.i = 0

    def next(self):
        i = self.i
        self.i = (i + 1) % len(self.t)
        return self.t[i], self.b[i]


def build_program(depth=DEPTH, ncores=NCORES, dbg=False):
    nc = bass.Bass("TRN2", target_bir_lowering=False)
    dt_in = lambda n, s: nc.dram_tensor(n, s, F32, kind="ExternalInput").ap()
    xin = dt_in("xin", [NTOK, D])
    ropec = dt_in("ropec", [NTOK, 128])
    ropes = dt_in("ropes", [NTOK, 128])
    c_col = dt_in("c_col", [128, 16])
    mod_w = dt_in("mod_w", [DEPTH, D, 3 * D])
    mod_b_col = dt_in("mod_b_col", [128, DEPTH * 24])
    npre_col = dt_in("npre_col", [128, DEPTH * 8])
    npost_col = dt_in("npost_col", [128, DEPTH * 8])
    ret_w_in = dt_in("ret_w_in", [2, D, 6144])
    ret_lg = dt_in("ret_lg", [128, 16])
    ret_gn_col = dt_in("ret_gn_col", [128, 32])
    ret_w_out = dt_in("ret_w_out", [2, 2048, D])
    lru_w_in = dt_in("lru_w_in", [2, D, 2 * LRU_W])
    lru_cw = dt_in("lru_cw", [128, 2 * NCH * 5])
    lru_cb = dt_in("lru_cb", [128, 2 * NCH])
    lru_wa = dt_in("lru_wa", [2, 2, NCH, 128, 128])
    lru_wx = dt_in("lru_wx", [2, 2, NCH, 128, 128])
    lru_ba = dt_in("lru_ba", [128, 2 * 2 * NCH])
    lru_bx = dt_in("lru_bx", [128, 2 * 2 * NCH])
    lru_lam = dt_in("lru_lam", [128, 2 * 2 * NCH])
    lru_w_out = dt_in("lru_w_out", [2, LRU_W, D])
    consts = dt_in("consts", [128, 128 * 5 + 16])
    out = nc.dram_tensor("out", [NL, D], F32, kind="ExternalOutput").ap()
    dbg_ctx = nc.dram_tensor("dbg_ctx", [CTX, D], F32, kind="ExternalOutput").ap() if dbg else None
    rgroups = [[2 * i, 2 * i + 1] for i in range(ncores // 2)]
    dumps = {}
    DUMP_B = Buf("dump")

    dram = lambda n, s, d=F32: nc.dram_tensor(n, s, d)
    Xs = [dram("Xs0", [NTOK, D]), dram("Xs1", [NTOK, D])]
    QTd = dram("QTd", [NCHUNK, 128, 1024], BF16)
    KTd = dram("KTd", [NCHUNK, 128, 1024], BF16)
    Krd = dram("Krd", [NCHUNK, 128, 1024], BF16)
    Vd = dram("Vd", [NCHUNK, 128, 2048], BF16)
    O1d = dram("O1d", [NCHUNK, 128, 2048])
    XRc = dram("XRc", [NCH, 128, CTX + 4])
    XRl = dram("XRl", [NCH, 128, NL + 4])
    XCc = dram("XCc", [NCH, 128, CTX])
    XCl = dram("XCl", [NCH, 128, NL])
    H1c = dram("H1c", [NCH, 128, CTX])
    H1l = dram("H1l", [NCH, 128, NL])
    SGc = dram("SGc", [NCH, 128, CTX], BF16)
    SGl = dram("SGl", [NCH, 128, NL], BF16)
    cc_state_in = [dram(f"ccsi{j}", [128, 4096]) for j in range(2)]
    cc_state_out = [dram(f"ccso{j}", [256, 4096]) for j in range(2)]
    cc_small_in = [dram(f"ccmi{j}", [128, 32]) for j in range(4)]
    cc_small_out = [dram(f"ccmo{j}", [256, 32]) for j in range(4)]

    es = contextlib.ExitStack()
    with es:
        S = Sched(nc, es)
        sb = lambda n, s, d=F32: es.enter_context(nc.sbuf_tensor(n, s, d))
        W = sb("W", [128, 8 * 4096], BF16)
        Wb = [Buf(f"W{i}") for i in range(8)]
        wslot = lambda s: W[:, s * 4096:(s + 1) * 4096]
        cst = sb("cst", [128, 128 * 5 + 16])
        cst_b = Buf("cst")
        ident_f = cst[:, 0:128]
        ones_f = cst[:, 128:256]
        tri1 = cst[:, 256:384]
        tri2 = cst[:, 384:512]
        zeros_f = cst[:, 512:640]
        coef = cst[:, 640:647]
        eps_col = cst[:, 647:648]
        one_col = cst[:, 648:649]
        sel0 = cst[:, 649:650]
        sel1 = cst[:, 650:651]
        ident_b = sb("ident_b", [128, 128], BF16)
        ident_bb = Buf("ident_b")
        ccol = sb("ccol", [128, 16])
        act_bf = sb("act_bf", [128, 16], BF16)
        act_b = Buf("act")
        small = sb("small", [128, DEPTH * 24 + DEPTH * 16 + 16 + 32 + 100 + 120])
        small_b = Buf("small")
        o = 0
        modb_t = small[:, o:o + DEPTH * 24]; o += DEPTH * 24
        npre_t = small[:, o:o + DEPTH * 8]; o += DEPTH * 8
        npost_t = small[:, o:o + DEPTH * 8]; o += DEPTH * 8
        lg_t = small[:, o:o + 16]; o += 16
        gn_t = small[:, o:o + 32]; o += 32
        cw_t = small[:, o:o + 100]; o += 100
        cb_t = small[:, o:o + 20]; o += 20
        ba_t = small[:, o:o + 40]; o += 40
        bx_t = small[:, o:o + 40]; o += 40
        lam_t = sb("lam_t", [128, 40])
        modT = sb("modT", [128, 48])
        modT_b = Buf("modT")
        A_col = sb("A_col", [128, 16])
        G2col = sb("G2col", [128, 16])
        col_b = Buf("cols")
        G2bc = [sb("G2bc0", [128, D]), sb("G2bc1", [128, D])]
        G2bc_b = Buf("G2bc")
        ps_proj = [es.enter_context(nc.psum_tensor(f"ps_proj{i}", [128, 512], F32)) for i in range(2)]
        ps_proj_b = [Buf("pp0"), Buf("pp1")]
        ps_tr = es.enter_context(nc.psum_tensor("ps_tr", [128, 1024], BF16))
        ps_tr_b = Buf("ptr")
        ps_st = es.enter_context(nc.psum_tensor("ps_st", [128, 512], F32))
        ps_st_b = Buf("pst")
        ps_o = [es.enter_context(nc.psum_tensor(f"ps_o{i}", [128, 512], F32)) for i in range(2)]
        ps_o_b = [Buf("po0"), Buf("po1")]
        ps_su = [es.enter_context(nc.psum_tensor(f"ps_su{i}", [128, 512], F32)) for i in range(2)]
        ps_su_b = [Buf("psu0"), Buf("psu1")]

        def dma_sp(out_ap, in_ap, reads, writes):
            S.add("sp", "d", [lambda e: e.dma_start(out=out_ap, in_=in_ap)], reads, writes)

        def dma_pool(out_ap, in_ap, reads, writes, slow=False):
            if slow:
                S.add("pool", "d", [lambda e: e.dma_start(out=out_ap, in_=in_ap, allow_slow_non_contiguous=True)], reads, writes)
            else:
                S.add("pool", "d", [lambda e: e.dma_start(out=out_ap, in_=in_ap)], reads, writes)

        def act(out_ap, in_ap, func, reads, writes, scale=None, bias=None):
            kw = {}
            if scale is not None:
                kw["scale"] = scale
            if bias is not None:
                kw["bias"] = bias
            S.add("act", "c", [lambda e: e.activation(out=out_ap, in_=in_ap, func=func, **kw)], reads, writes)

        def ts(stream, out_ap, in_ap, s1, s2, op0, op1, reads, writes):
            if op1 is None:
                S.add(stream, "c", [lambda e: e.tensor_scalar(out=out_ap, in0=in_ap, scalar1=s1, scalar2=None, op0=op0)], reads, writes)
            else:
                S.add(stream, "c", [lambda e: e.tensor_scalar(out=out_ap, in0=in_ap, scalar1=s1, scalar2=s2, op0=op0, op1=op1)], reads, writes)

        def tt(stream, out_ap, a, b, op, reads, writes):
            S.add(stream, "c", [lambda e: e.tensor_tensor(out=out_ap, in0=a, in1=b, op=op)], reads, writes)

        def stt(out_ap, in0, scalar, in1, op0, op1, reads, writes):
            S.add("dve", "c", [lambda e: e.scalar_tensor_tensor(out=out_ap, in0=in0, scalar=scalar, in1=in1, op0=op0, op1=op1)], reads, writes)

        def copy(stream, out_ap, in_ap, reads, writes):
            if stream == "act":
                S.add("act", "c", [lambda e: e.copy(out=out_ap, in_=in_ap)], reads, writes)
            else:
                S.add(stream, "c", [lambda e: e.tensor_copy(out=out_ap, in_=in_ap)], reads, writes)

        def mm_group(specs, reads, writes):
            fns = []
            for (o_, l_, r_, st_, sp_) in specs:
                fns.append(lambda e, o_=o_, l_=l_, r_=r_, st_=st_, sp_=sp_: e.matmul(o_, lhsT=l_, rhs=r_, start=st_, stop=sp_))
            S.add("pe", "c", fns, reads, writes)

        def tr_group(specs, reads, writes):
            fns = []
            for (o_, i_) in specs:
                fns.append(lambda e, o_=o_, i_=i_: e.transpose(o_, i_, ident_b[:]))
            S.add("pe", "c", fns, list(reads) + [ident_bb], writes)

        def dump(name, ap, buf, F):
            if not dbg:
                return
            t = nc.dram_tensor("dmp_" + name, [128, F], F32, kind="ExternalOutput").ap()
            dumps[name] = t
            dma_pool(t[:, :], ap, [buf], [DUMP_B])

        dma_sp(cst[:], consts[:, :], [], [cst_b])
        dma_pool(ident_b[:], consts[:, 0:128], [], [ident_bb])
        dma_sp(ccol[:], c_col[:, :], [], [act_b])
        dma_sp(modb_t, mod_b_col[:, :], [], [small_b])
        dma_sp(npre_t, npre_col[:, :], [], [small_b])
        dma_sp(npost_t, npost_col[:, :], [], [small_b])
        dma_sp(lg_t, ret_lg[:, :], [], [small_b])
        dma_sp(gn_t, ret_gn_col[:, :], [], [small_b])
        dma_sp(cw_t, lru_cw[:, :], [], [small_b])
        dma_sp(cb_t, lru_cb[:, :], [], [small_b])
        dma_sp(ba_t, lru_ba[:, :], [], [small_b])
        dma_sp(bx_t, lru_bx[:, :], [], [small_b])
        dma_sp(lam_t[:], lru_lam[:, :], [], [small_b])
        act(act_bf[:], ccol[:], AF.Silu, [act_b], [act_b])

        def prep_layer(l, les):
            lsb = lambda n, s, d=F32: les.enter_context(nc.sbuf_tensor(n, s, d))
            dg = Ring(nc, les, f"dg{l}_", [128, 128], F32, 2)
            actv = act_bf[:].rearrange("p (k v) -> p k v", v=2)
            for cg in range(6):
                s = cg % 2
                wv = wslot(s).rearrange("p (k c) -> p k c", k=8)
                dma_pool(wv, mod_w[l][:, cg * 512:(cg + 1) * 512].rearrange("(k p) c -> p k c", p=128), [], [Wb[s]])
                specs = []
                for cc in range(4):
                    ck = cg * 4 + cc
                    for k in range(8):
                        specs.append((ps_st[:, ck * 2:ck * 2 + 2], wv[:, k, cc * 128:(cc + 1) * 128], actv[:, k, :], k == 0, k == 7))
                mm_group(specs, [Wb[s], act_b], [ps_st_b])
            tt("dve", modT[:].rearrange("p (c v) -> p c v", v=2), ps_st[:, 0:48].rearrange("p (c v) -> p c v", v=2),
               modb_t[:, l * 24:(l + 1) * 24].unsqueeze(2).to_broadcast([128, 24, 2]), ALU.add, [ps_st_b, small_b], [modT_b])
            npre_bc = npre_t[:, l * 8:(l + 1) * 8].unsqueeze(2).to_broadcast([128, 8, 2])
            npost_bc = npost_t[:, l * 8:(l + 1) * 8].unsqueeze(2).to_broadcast([128, 8, 2])
            v3 = lambda t: t.rearrange("p (c v) -> p c v", v=2)
            stt(v3(A_col[:]), v3(modT[:, 16:32]), 1.0, npre_bc, ALU.add, ALU.mult, [modT_b, small_b], [col_b])
            tt("dve", v3(G2col[:]), v3(modT[:, 32:48]), npost_bc, ALU.mult, [modT_b, small_b, col_b], [col_b])
            for v in range(2):
                for half in range(2):
                    specs = []
                    dbs = []
                    for kk in range(4):
                        k = half * 4 + kk
                        dt_, db_ = dg.next()
                        ts("dve", dt_[:], ident_f, G2col[:, k * 2 + v:k * 2 + v + 1], None, ALU.mult, None, [cst_b, col_b], [db_])
                        mm_group([(ps_o[half][:, kk * 128:(kk + 1) * 128], ones_f, dt_[:], True, True)], [db_, cst_b], [ps_o_b[half]])
                    copy("act", G2bc[v][:, half * 512:(half + 1) * 512], ps_o[half][:], [ps_o_b[half]], [G2bc_b])

        def norm_a(R, src_ap, src_buf, row0):
            xt, xb = R["x"].next()
            dma_sp(xt[:], src_ap[row0:row0 + 128, :], [src_buf], [xb])
            jt, jb = R["junk"].next()
            st_t, st_b = R["stat"].next()
            act(jt[:], xt[:], AF.Square, [xb], [jb])
            S.add("dve", "c", [lambda e: e.tensor_reduce(out=st_t[:, 0:1], in_=jt[:], axis=AX.X, op=ALU.add)], [jb], [st_b])
            act(st_t[:, 1:2], st_t[:, 0:1], AF.Sqrt, [st_b, cst_b], [st_b], scale=1.0 / D, bias=eps_col)
            S.add("dve", "c", [lambda e: e.reciprocal(out=st_t[:, 2:3], in_=st_t[:, 1:2])], [st_b], [st_b])
            xh, xhb = R["xhat"].next()
            ts("dve", xh[:], xt[:], st_t[:, 2:3], None, ALU.mult, None, [xb, st_b], [xhb])
            return xt, xb, xh, xhb

        def norm_b(xh, xhb, v, hT_ap_fn, hT_buf):
            tr_group([(ps_tr[:, k * 128:(k + 1) * 128], xh[:, k * 128:(k + 1) * 128]) for k in range(8)], [xhb], [ps_tr_b])
            for k in range(8):
                act(hT_ap_fn(k), ps_tr[:, k * 128:(k + 1) * 128], AF.Identity, [ps_tr_b, col_b, modT_b], [hT_buf],
                    scale=A_col[:, k * 2 + v:k * 2 + v + 1], bias=modT[:, k * 2 + v:k * 2 + v + 1])

        def norm_chunk(R, src_ap, src_buf, row0, v, hT_ap_fn, hT_buf, keep_x=False):
            xt, xb, xh, xhb = norm_a(R, src_ap, src_buf, row0)
            norm_b(xh, xhb, v, hT_ap_fn, hT_buf)
            return xt, xb

        def post_chunk(R, xt, xb, v, dst_ap, dst_buf, dst_row0):
            jt, jb = R["junk"].next()
            st_t, st_b = R["stat"].next()
            for g in range(2):
                act(jt[:, g * 512:(g + 1) * 512], ps_proj[g][:], AF.Square, [ps_proj_b[g]], [jb])
            S.add("dve", "c", [lambda e: e.tensor_reduce(out=st_t[:, 0:1], in_=jt[:], axis=AX.X, op=ALU.add)], [jb], [st_b])
            act(st_t[:, 1:2], st_t[:, 0:1], AF.Sqrt, [st_b, cst_b], [st_b], scale=1.0 / D, bias=eps_col)
            S.add("dve", "c", [lambda e: e.reciprocal(out=st_t[:, 2:3], in_=st_t[:, 1:2])], [st_b], [st_b])
            tm, tmb = R["tmp"].next()
            xn, xnb = R["xn"].next()
            for g in range(2):
                stt(tm[:, g * 512:(g + 1) * 512], ps_proj[g][:], st_t[:, 2:3], G2bc[v][:, g * 512:(g + 1) * 512],
                    ALU.mult, ALU.mult, [ps_proj_b[g], st_b, G2bc_b], [tmb])
            tt("pool", xn[:], tm[:], xt[:], ALU.add, [tmb, xb], [xnb])
            if dst_row0 == CTX and v == 0 and "tm" not in dumps:
                dump("tm", tm[:], tmb, 1024); dump("ysq", jt[:], jb, 1024); dump("stat", st_t[:], st_b, 4)
            dma_pool(dst_ap[dst_row0:dst_row0 + 128, :], xn[:], [xnb], [dst_buf])

        def exchange_start(idx_big, src_ap, src_buf, F):
            if F > 32:
                cin, cout = cc_state_in[idx_big], cc_state_out[idx_big]
            else:
                cin, cout = cc_small_in[idx_big], cc_small_out[idx_big]
            cb = Buf("ccin")
            cob = Buf("ccout")
            dma_pool(cin[:, 0:F], src_ap, (src_buf if isinstance(src_buf, list) else [src_buf]), [cb])
            S.add("pool", "cc", [lambda e: e.collective_compute(
                "AllGather", ALU.bypass, replica_groups=rgroups,
                ins=[cin[:, :]], outs=[cout[:, :]])], [cb], [cob])
            return cout, cob

        def exchange_finish(R, cout, cob, dst_ap, dst_buf, F):
            step = min(F, 1024)
            for p0 in range(0, F, step):
                g0, g0b = R["xg"].next()
                g1, g1b = R["xg"].next()
                dma_sp(g0[:, 0:step], cout[0:128, p0:p0 + step], [cob], [g0b])
                dma_sp(g1[:, 0:step], cout[128:256, p0:p0 + step], [cob], [g1b])
                ts("dve", g0[:, 0:step], g0[:, 0:step], sel0, None, ALU.mult, None, [g0b, cst_b], [g0b])
                stt(dst_ap[:, p0:p0 + step], g1[:, 0:step], sel1, g0[:, 0:step], ALU.mult, ALU.add, [g0b, g1b, cst_b],
                    (dst_buf if isinstance(dst_buf, list) else [dst_buf]))

        def retention_layer(l, src_ap, src_buf, dst_ap, dst_buf, last):
            j = l // 2
            les = contextlib.ExitStack()
            with les:
                lsb = lambda n, s, d=F32: les.enter_context(nc.sbuf_tensor(f"{n}_L{l}", s, d))
                prep_layer(l, les)
                state = lsb("state", [128, 4096])
                state_bf = lsb("state_bf", [128, 4096], BF16)
                state_hb = [Buf(f"state{h}") for h in range(4)]
                statebf_hb = [Buf(f"state_bf{h}") for h in range(4)]
                state_b = state_hb
                statebf_b = statebf_hb
                mask = [lsb("mask1", [128, 512]), lsb("mask2", [128, 512])]
                tab = lsb("tab", [128, 2 * 28])
                lgn = lsb("lgn", [128, 8])
                tab_b = Buf("tab")
                mask_b = Buf("mask")
                ts("dve", tab[:, 0:8], lg_t[:, j * 8:(j + 1) * 8], -1.0, None, ALU.mult, None, [small_b], [tab_b])
                tt("dve", lgn[:], lg_t[:, j * 8:(j + 1) * 8], tab[:, 0:8], ALU.min, [small_b, tab_b], [tab_b])
                for d in range(2):
                    for i in range(7):
                        ts("dve", tab[:, d * 28 + i * 4:d * 28 + i * 4 + 4], lgn[:, d * 4:(d + 1) * 4], coef[:, i:i + 1], None,
                           ALU.mult, None, [tab_b, cst_b], [tab_b])
                act(tab[:], tab[:], AF.Exp, [tab_b], [tab_b])
                for d in range(2):
                    for i in ((0, 2) if d == 0 else (3, 5)):
                        ts("dve", tab[:, d * 28 + i * 4:d * 28 + i * 4 + 4], tab[:, d * 28 + i * 4:d * 28 + i * 4 + 4], 0.0625, None,
                           ALU.mult, None, [tab_b], [tab_b])
                T = lambda d, i, h: tab[:, d * 28 + i * 4 + h:d * 28 + i * 4 + h + 1]
                for d in range(2):
                    for h in range(4):
                        ts("dve", mask[d][:, h * 128:(h + 1) * 128], tri1 if d == 0 else tri2, T(d, 2 if d == 0 else 5, h), None,
                           ALU.mult, None, [tab_b, cst_b], [mask_b])
                KD = (0, 3)
                QD = (1, 4)

                def chunk_rows(n):
                    return n * 128, (1 if n < NCC else 0)

                def scan_part(R, d, n, QT, QTb, KT, KTb, Kr, Krb, V, Vb, o_t, o_b, o1_t, o1_b, need_o):
                    if need_o:
                        specs = []
                        for h in range(4):
                            for dc in range(2):
                                specs.append((ps_st[:, h * 128:(h + 1) * 128], KT[:, (2 * h + dc) * 128:(2 * h + dc + 1) * 128],
                                              QT[:, (2 * h + dc) * 128:(2 * h + dc + 1) * 128], dc == 0, dc == 1))
                        mm_group(specs, [KTb, QTb], [ps_st_b])
                        P, Pb = R["P"].next()
                        tt("dve", P[:], ps_st[:], mask[d][:], ALU.mult, [ps_st_b, mask_b], [Pb])
                    Kd, Kdb = R["Kdec"].next()
                    ts_bc = tab[:, d * 28 + KD[d] * 4:d * 28 + KD[d] * 4 + 4].unsqueeze(2).to_broadcast([128, 4, 256])
                    tt("pool", Kd[:].rearrange("p (h c) -> p h c", h=4), Kr[:].rearrange("p (h c) -> p h c", h=4), ts_bc, ALU.mult, [Krb, tab_b], [Kdb])
                    for h in range(4):
                        if need_o:
                            po, pob = ps_o[h % 2], ps_o_b[h % 2]
                            specs = [(po[:], P[:, h * 128:(h + 1) * 128], V[:, h * 512:(h + 1) * 512], True, False)]
                            for dc in range(2):
                                specs.append((po[:], QT[:, (2 * h + dc) * 128:(2 * h + dc + 1) * 128],
                                              state_bf[:, (h * 2 + dc) * 512:(h * 2 + dc + 1) * 512], False, dc == 1))
                            mm_group(specs, [Pb, Vb, QTb, statebf_hb[h]], [pob])
                            if d == 0:
                                act(o_t[:, h * 512:(h + 1) * 512], po[:], AF.Identity, [pob, tab_b], [o_b], scale=T(d, QD[d], h))
                            else:
                                stt(o_t[:, h * 512:(h + 1) * 512], po[:], T(d, QD[d], h), o1_t[:, h * 512:(h + 1) * 512],
                                    ALU.mult, ALU.add, [pob, tab_b, o1_b], [o_b])
                        for dc in range(2):
                            mm_group([(ps_su[dc][:], Kd[:, h * 256 + dc * 128:h * 256 + (dc + 1) * 128], V[:, h * 512:(h + 1) * 512], True, True)],
                                     [Kdb, Vb], [ps_su_b[dc]])
                            sl = slice((h * 2 + dc) * 512, (h * 2 + dc + 1) * 512)
                            stt(state[:, sl], state[:, sl], T(d, 6, h), ps_su[dc][:], ALU.mult, ALU.add,
                                [ps_su_b[dc], tab_b, state_hb[h]], [state_hb[h]])
                        for dc in range(2):
                            sl = slice((h * 2 + dc) * 512, (h * 2 + dc + 1) * 512)
                            copy("act", state_bf[:, sl], state[:, sl], [state_hb[h]], [statebf_hb[h]])

                def zero_state():
                    S.add("dve", "c", [lambda e: e.memset(state[:], 0.0)], [], state_hb)
                    S.add("pool", "c", [lambda e: e.memset(state_bf[:], 0.0)], [], statebf_hb)

                Qb_d = [Buf(f"QTd{n}") for n in range(NCHUNK)]
                Kb_d = [Buf(f"KTd{n}") for n in range(NCHUNK)]
                Krb_d = [Buf(f"Krd{n}") for n in range(NCHUNK)]
                Vb_d = [Buf(f"Vd{n}") for n in range(NCHUNK)]
                O1b_d = [Buf(f"O1d{n}") for n in range(NCHUNK)]

                pes = contextlib.ExitStack()
                with pes:
                    R = {
                        "x": Ring(nc, pes, "r1x", [128, D], F32, 2), "junk": Ring(nc, pes, "r1j", [128, D], F32, 1),
                        "stat": Ring(nc, pes, "r1s", [128, 4], F32, 3), "xhat": Ring(nc, pes, "r1xh", [128, D], BF16, 3),
                        "hT": Ring(nc, pes, "r1hT", [128, D], BF16, 2), "cs": Ring(nc, pes, "r1cs", [128, 256], F32, 2),
                        "qkf": Ring(nc, pes, "r1qkf", [128, 1024], F32, 2), "rt": Ring(nc, pes, "r1rt", [128, 2048], F32, 2),
                        "Qr": Ring(nc, pes, "r1Qr", [128, D], BF16, 2), "Kr": Ring(nc, pes, "r1Kr", [128, D], BF16, 2),
                        "QT": Ring(nc, pes, "r1QT", [128, D], BF16, 2), "KT": Ring(nc, pes, "r1KT", [128, D], BF16, 2),
                        "V": Ring(nc, pes, "r1V", [128, 2048], BF16, 2), "P": Ring(nc, pes, "r1P", [128, 512], BF16, 2),
                        "Kdec": Ring(nc, pes, "r1Kd", [128, D], BF16, 2), "o1": Ring(nc, pes, "r1o1", [128, 2048], F32, 2),
                    }
                    for cg in range(8):
                        dma_pool(wslot(cg).rearrange("p (k c) -> p k c", k=8),
                                 ret_w_in[j][:, cg * 512:(cg + 1) * 512].rearrange("(k p) c -> p k c", p=128), [], [Wb[cg]])
                    zero_state()
                    def stageA0a(n):
                        c = {"n": n}
                        row0, v = chunk_rows(n)
                        _, _, c["xh"], c["xhb"] = norm_a(R, src_ap, src_buf, row0)
                        return c

                    def stageA0b(c):
                        row0, v = chunk_rows(c["n"])
                        hT, hTb = R["hT"].next()
                        norm_b(c["xh"], c["xhb"], v, lambda k, hT=hT: hT[:, k * 128:(k + 1) * 128], hTb)
                        c.update(hT=hT, hTb=hTb)
                        return c

                    def stageA1(c, inject=None):
                        n = c["n"]
                        row0, v = chunk_rows(n)
                        hT, hTb = c["hT"], c["hTb"]
                        cs, csb = R["cs"].next()
                        dma_sp(cs[:, 0:128], ropec[row0:row0 + 128, :], [], [csb])
                        dma_sp(cs[:, 128:256], ropes[row0:row0 + 128, :], [], [csb])
                        cosb = cs[:, 0:128].unsqueeze(1).to_broadcast([128, 4, 128])
                        sinb = cs[:, 128:256].unsqueeze(1).to_broadcast([128, 4, 128])
                        Qr, Qrb = R["Qr"].next()
                        Kr, Krb = R["Kr"].next()
                        V, Vb = R["V"].next()
                        qf = None
                        inj = None
                        for ci, cg in enumerate((2, 3, 0, 1, 4, 5, 6, 7)):
                            pp, ppb = ps_proj[ci % 2], ps_proj_b[ci % 2]
                            wv = wslot(cg).rearrange("p (k c) -> p k c", k=8)
                            mm_group([(pp[:], hT[:, k * 128:(k + 1) * 128], wv[:, k, :], k == 0, k == 7) for k in range(8)],
                                     [hTb, Wb[cg]], [ppb])
                            if cg < 4:
                                if cg % 2 == 0:
                                    qf, qfb = R["qkf"].next()
                                copy("act", qf[:, (cg % 2) * 512:(cg % 2 + 1) * 512], pp[:], [ppb], [qfb])
                                if cg % 2 == 1:
                                    dst, dstb = (Qr, Qrb) if cg < 2 else (Kr, Krb)
                                    eng = "dve" if cg < 2 else "pool"
                                    rt, rtb = R["rt"].next()
                                    q4 = qf[:].rearrange("p (h e j) -> p h e j", h=4, e=2)
                                    te, to = q4[:, :, 0, :], q4[:, :, 1, :]
                                    r4 = rt[:].rearrange("p (a h j) -> p a h j", a=4, h=4)
                                    d4 = dst[:].rearrange("p (h e j) -> p h e j", h=4, e=2)
                                    tt(eng, r4[:, 0], te, cosb, ALU.mult, [qfb, csb], [rtb])
                                    tt(eng, r4[:, 1], to, sinb, ALU.mult, [qfb, csb], [rtb])
                                    tt(eng, r4[:, 2], te, sinb, ALU.mult, [qfb, csb], [rtb])
                                    tt(eng, r4[:, 3], to, cosb, ALU.mult, [qfb, csb], [rtb])
                                    tt(eng, d4[:, :, 0, :], r4[:, 0], r4[:, 1], ALU.subtract, [rtb], [dstb])
                                    tt(eng, d4[:, :, 1, :], r4[:, 2], r4[:, 3], ALU.add, [rtb], [dstb])
                            else:
                                copy("act", V[:, (cg - 4) * 512:(cg - 3) * 512], pp[:], [ppb], [Vb])
                            if ci == 3 and inject is not None:
                                inj = stageA0a(inject)
                        QT, QTb = R["QT"].next()
                        KT, KTb = R["KT"].next()
                        tr_group([(ps_tr[:, k * 128:(k + 1) * 128], Qr[:, k * 128:(k + 1) * 128]) for k in range(8)], [Qrb], [ps_tr_b])
                        copy("dve", QT[:], ps_tr[:], [ps_tr_b], [QTb])
                        tr_group([(ps_tr[:, k * 128:(k + 1) * 128], Kr[:, k * 128:(k + 1) * 128]) for k in range(8)], [Krb], [ps_tr_b])
                        copy("act", KT[:], ps_tr[:], [ps_tr_b], [KTb])
                        c.update(QT=QT, QTb=QTb, KT=KT, KTb=KTb, Kr=Kr, Krb=Krb, V=V, Vb=Vb)
                        if inj is not None:
                            inj = stageA0b(inj)
                        return c, inj

                    def stageB1(c):
                        n = c["n"]
                        o1, o1b = R["o1"].next()
                        scan_part(R, 0, n, c["QT"], c["QTb"], c["KT"], c["KTb"], c["Kr"], c["Krb"], c["V"], c["Vb"], o1, o1b, None, None, True)
                        dma_pool(QTd[n], c["QT"][:], [c["QTb"]], [Qb_d[n]])
                        dma_pool(KTd[n], c["KT"][:], [c["KTb"]], [Kb_d[n]])
                        dma_pool(Krd[n], c["Kr"][:], [c["Krb"]], [Krb_d[n]])
                        dma_pool(Vd[n], c["V"][:], [c["Vb"]], [Vb_d[n]])
                        dma_pool(O1d[n], o1[:], [o1b], [O1b_d[n]])

                    c0 = stageA0b(stageA0a(0))
                    prev, nxt0 = stageA1(c0, inject=1 if NCHUNK > 1 else None)
                    for n in range(NCHUNK):
                        if n + 1 < NCHUNK:
                            nxt, nxt0 = stageA1(nxt0, inject=(n + 2) if n + 2 < NCHUNK else None)
                        else:
                            nxt = None
                        stageB1(prev)
                        prev = nxt
                    S.barrier()
                pes = contextlib.ExitStack()
                with pes:
                    R = {
                        "x": Ring(nc, pes, "r2x", [128, D], F32, 4), "junk": Ring(nc, pes, "r2j", [128, D], F32, 1),
                        "stat": Ring(nc, pes, "r2s", [128, 4], F32, 2), "xhat": Ring(nc, pes, "r2xh", [128, D], BF16, 2),
                        "hT": Ring(nc, pes, "r2hT", [128, D], BF16, 1),
                        "QT": Ring(nc, pes, "r2QT", [128, D], BF16, 2), "KT": Ring(nc, pes, "r2KT", [128, D], BF16, 2),
                        "Kr": Ring(nc, pes, "r2Kr", [128, D], BF16, 2),
                        "V": Ring(nc, pes, "r2V", [128, 2048], BF16, 2), "P": Ring(nc, pes, "r2P", [128, 512], BF16, 2),
                        "Kdec": Ring(nc, pes, "r2Kd", [128, D], BF16, 2), "o1": Ring(nc, pes, "r2o1", [128, 2048], F32, 2),
                        "SG": Ring(nc, pes, "r2SG", [128, 2048], BF16, 2), "Z": Ring(nc, pes, "r2Z", [128, 2048], BF16, 2),
                        "ZT": Ring(nc, pes, "r2ZT", [128, 2048], BF16, 1),
                        "xn": Ring(nc, pes, "r2xn", [128, D], F32, 1),
                        "bn": Ring(nc, pes, "r2bn", [128, 48], F32, 2),
                    }
                    R["xg"] = Ring(nc, pes, "r2xg", [128, 1024], F32, 2)
                    R["wst"] = R["x"]
                    R["tmp"] = R["junk"]
                    R["stat"] = Ring(nc, pes, "r2s2", [128, 4], F32, 6)
                    if PAIR:
                        ex_cout, ex_cob = exchange_start(j, state[:], state_hb, 4096)
                    for cg in range(4):
                        dma_pool(wslot(cg).rearrange("p (k c) -> p k c", k=8),
                                 ret_w_in[j][:, (12 - 4 + cg) * 512:(12 - 3 + cg) * 512].rearrange("(k p) c -> p k c", p=128), [], [Wb[cg]])
                    Wo = W[:, 4 * 4096:8 * 4096].rearrange("p (k c) -> p k c", k=16)
                    for k in range(16):
                        wt_, wtb = R["wst"].next()
                        dma_sp(wt_[:], ret_w_out[j][k * 128:(k + 1) * 128, :], [], [wtb])
                        ts("dve", Wo[:, k, :], wt_[:], gn_t[:, j * 16 + k:j * 16 + k + 1], None, ALU.mult, None, [wtb, small_b], [Wb[4 + k // 4]])
                    zero_state()
                    order = list(range(NCC - 1, -1, -1)) + list(range(NCHUNK - 1, NCC - 1, -1))

                    def mk2(n):
                        c = {"n": n}
                        c["row0"], c["v"] = chunk_rows(n)
                        c["skip"] = last and c["v"] == 1 and not dbg
                        return c

                    def loads2(c):
                        n = c["n"]
                        c["Kr"], c["Krb"] = R["Kr"].next()
                        c["V"], c["Vb"] = R["V"].next()
                        dma_sp(c["Kr"][:], Krd[n], [Krb_d[n]], [c["Krb"]])
                        dma_sp(c["V"][:], Vd[n], [Vb_d[n]], [c["Vb"]])
                        if not c["skip"]:
                            c["QT"], c["QTb"] = R["QT"].next()
                            c["KT"], c["KTb"] = R["KT"].next()
                            dma_sp(c["QT"][:], QTd[n], [Qb_d[n]], [c["QTb"]])
                            dma_sp(c["KT"][:], KTd[n], [Kb_d[n]], [c["KTb"]])
                        return c

                    def stageB2(c):
                        n = c["n"]
                        if n == NCHUNK - 1 and PAIR:
                            exchange_finish(R, ex_cout, ex_cob, state, state_hb, 4096)
                            copy("pool", state_bf[:], state[:], state_hb, statebf_hb)
                        if c["skip"]:
                            scan_part(R, 1, n, None, None, None, None, c["Kr"], c["Krb"], c["V"], c["Vb"], None, None, None, None, False)
                            return
                        o1, o1b = c["o1"], c["o1b"]
                        scan_part(R, 1, n, c["QT"], c["QTb"], c["KT"], c["KTb"], c["Kr"], c["Krb"], c["V"], c["Vb"], o1, o1b, o1, o1b, True)

                    def c2a_norm(c):
                        if c["skip"]:
                            return
                        c["xt"], c["xb"], c["xh"], c["xhb"] = norm_a(R, src_ap, src_buf, c["row0"])

                    def c2a_pe(c):
                        if c["skip"]:
                            return
                        hT, hTb = R["hT"].next()
                        norm_b(c["xh"], c["xhb"], c["v"], lambda k, hT=hT: hT[:, k * 128:(k + 1) * 128], hTb)
                        SG, SGb = R["SG"].next()
                        for cg in range(4):
                            pp, ppb = ps_proj[cg % 2], ps_proj_b[cg % 2]
                            wv = wslot(cg).rearrange("p (k c) -> p k c", k=8)
                            mm_group([(pp[:], hT[:, k * 128:(k + 1) * 128], wv[:, k, :], k == 0, k == 7) for k in range(8)],
                                     [hTb, Wb[cg]], [ppb])
                            act(SG[:, cg * 512:(cg + 1) * 512], pp[:], AF.Silu, [ppb], [SGb])
                        c["SG"], c["SGb"] = SG, SGb

                    def c2b(c):
                        if c["skip"]:
                            return
                        o1, o1b = c["o1"], c["o1b"]
                        bn, bnb = R["bn"].next()
                        for h in range(4):
                            S.add("dve", "c", [lambda e, h=h, bn=bn, o1=o1: e.bn_stats(out=bn[:, h * 6:(h + 1) * 6], in_=o1[:, h * 512:(h + 1) * 512])], [o1b], [bnb])
                        for h in range(4):
                            S.add("dve", "c", [lambda e, h=h, bn=bn: e.bn_aggr(out=bn[:, 24 + h * 2:24 + h * 2 + 2], in_=bn[:, h * 6:(h + 1) * 6])], [bnb], [bnb])
                        mv = bn[:, 24:32].rearrange("p (h t) -> p h t", t=2)
                        act(bn[:, 32:36], mv[:, :, 1], AF.Sqrt, [bnb, cst_b], [bnb], scale=1.0, bias=eps_col)
                        S.add("dve", "c", [lambda e, bn=bn: e.reciprocal(out=bn[:, 36:40], in_=bn[:, 32:36])], [bnb], [bnb])
                        for h in range(4):
                            ts("dve", o1[:, h * 512:(h + 1) * 512], o1[:, h * 512:(h + 1) * 512], bn[:, 24 + h * 2:24 + h * 2 + 1],
                               bn[:, 36 + h:37 + h], ALU.subtract, ALU.mult, [o1b, bnb], [o1b])
                        Z, Zb = R["Z"].next()
                        tt("pool", Z[:], o1[:], c["SG"][:], ALU.mult, [o1b, c["SGb"]], [Zb])
                        c["Z"], c["Zb"] = Z, Zb

                    def c2c(c):
                        if c["skip"]:
                            return
                        n, row0, v = c["n"], c["row0"], c["v"]
                        Z, Zb = c["Z"], c["Zb"]
                        ZT, ZTb = R["ZT"].next()
                        for half in range(2):
                            tr_group([(ps_tr[:, k * 128:(k + 1) * 128], Z[:, (half * 8 + k) * 128:(half * 8 + k + 1) * 128]) for k in range(8)],
                                     [Zb], [ps_tr_b])
                            copy("act" if half == 0 else "dve", ZT[:, half * 1024:(half + 1) * 1024], ps_tr[:], [ps_tr_b], [ZTb])
                        for g in range(2):
                            mm_group([(ps_proj[g][:], ZT[:, k * 128:(k + 1) * 128], Wo[:, k, g * 512:(g + 1) * 512], k == 0, k == 15) for k in range(16)],
                                     [ZTb] + Wb[4:8], [ps_proj_b[g]])
                        xt, xb = c["xt"], c["xb"]
                        if last and v == 1:
                            post_chunk(R, xt, xb, v, dbg_ctx, dst_buf, row0)
                        elif last:
                            post_chunk(R, xt, xb, v, out, dst_buf, row0 - CTX)
                        else:
                            post_chunk(R, xt, xb, v, dst_ap, dst_buf, row0)

                    NO = len(order)
                    cs2 = {0: loads2(mk2(order[0]))}
                    c2a_norm(cs2[0])
                    c2a_pe(cs2[0])
                    if NO > 1:
                        cs2[1] = mk2(order[1])
                        c2a_norm(cs2[1])
                    for i, n in enumerate(order):
                        c = cs2[i]
                        if not c["skip"]:
                            c["o1"], c["o1b"] = R["o1"].next()
                            dma_sp(c["o1"][:], O1d[n], [O1b_d[n]], [c["o1b"]])
                        if i + 2 < NO:
                            cs2[i + 2] = mk2(order[i + 2])
                        if i + 1 < NO:
                            loads2(cs2[i + 1])
                            c2a_pe(cs2[i + 1])
                        stageB2(c)
                        if i >= 1:
                            c2c(cs2[i - 1])
                            del cs2[i - 1]
                        c2b(c)
                        if i + 2 < NO:
                            c2a_norm(cs2[i + 2])
                    c2c(cs2[NO - 1])
                    S.barrier()

        def lru_layer(l, src_ap, src_buf, dst_ap, dst_buf, last):
            j = l // 2
            tiles = [("c", 0, CTX)] + [("l", i * 512, 512) for i in range(NL // 512)]
            XR = {"c": XRc, "l": XRl}
            XC = {"c": XCc, "l": XCl}
            H1 = {"c": H1c, "l": H1l}
            SGD = {"c": SGc, "l": SGl}
            les = contextlib.ExitStack()
            with les:
                lsb = lambda n, s, d=F32: les.enter_context(nc.sbuf_tensor(f"{n}_L{l}", s, d))
                prep_layer(l, les)
                cl = lsb("cl", [128, 20])
                sp_t = lsb("sp_t", [128, 120])
                cl_b = Buf("cl")
                hprev = lsb("hprev", [128, 2 * NCH])
                hprev_b = Buf("hprev")
                lam_j = lam_t[:, j * 20:(j + 1) * 20]
                A0 = lambda i: sp_t[:, i * 20:(i + 1) * 20]
                ts("dve", A0(1), lam_j, -1.0, None, ALU.mult, None, [small_b], [cl_b])
                tt("dve", A0(0), lam_j, A0(1), ALU.min, [small_b, cl_b], [cl_b])
                act(A0(1), A0(0), AF.Exp, [cl_b], [cl_b])
                ts("dve", A0(2), A0(1), 2.0, None, ALU.add, None, [cl_b], [cl_b])
                S.add("dve", "c", [lambda e: e.reciprocal(out=A0(3), in_=A0(2))], [cl_b], [cl_b])
                tt("dve", A0(2), A0(1), A0(3), ALU.mult, [cl_b], [cl_b])
                tt("dve", A0(3), A0(2), A0(2), ALU.mult, [cl_b], [cl_b])
                ts("dve", A0(4), A0(3), 1.0 / 17.0, 1.0 / 15.0, ALU.mult, ALU.add, [cl_b], [cl_b])
                for cden in (13.0, 11.0, 9.0, 7.0, 5.0, 3.0, 1.0):
                    tt("dve", A0(4), A0(4), A0(3), ALU.mult, [cl_b], [cl_b])
                    ts("dve", A0(4), A0(4), 1.0 / cden, None, ALU.add, None, [cl_b], [cl_b])
                tt("dve", A0(4), A0(4), A0(2), ALU.mult, [cl_b], [cl_b])
                ts("dve", A0(5), lam_j, -1.0, 0.0, ALU.mult, ALU.max, [small_b, cl_b], [cl_b])
                stt(A0(5), A0(4), 2.0, A0(5), ALU.mult, ALU.add, [cl_b], [cl_b])
                ts("dve", cl[:], A0(5), -8.0, None, ALU.mult, None, [cl_b], [cl_b])

                XRb = {("c", 0): Buf("xrc")}
                XCb, H1b, SGb_d = {}, {}, {}
                for (sq, t0, nt) in tiles:
                    XRb[(sq, t0)] = Buf(f"xr{sq}{t0}")
                    XCb[(sq, t0)] = Buf(f"xc{sq}{t0}")
                    H1b[(sq, t0)] = Buf(f"h1{sq}{t0}")
                    SGb_d[(sq, t0)] = Buf(f"sg{sq}{t0}")
                pad_b = {("c", "L"): Buf("padcL"), ("c", "R"): Buf("padcR"), ("l", "L"): Buf("padlL"), ("l", "R"): Buf("padlR")}
                for s in range(5):
                    dma_pool(wslot(s).rearrange("p (k c) -> p k c", k=8),
                             lru_w_in[j][:, s * 512:(s + 1) * 512].rearrange("(k p) c -> p k c", p=128), [], [Wb[s]])
                GW = W[:, 5 * 4096:5 * 4096 + 40 * 128].rearrange("p (d g c j) -> p d g c j", d=2, g=2, c=NCH)
                for d in range(2):
                    dma_pool(GW[:, d, 0], lru_wa[j, d].rearrange("c i j -> i c j"), [], [Wb[5], Wb[6]])
                    dma_pool(GW[:, d, 1], lru_wx[j, d].rearrange("c i j -> i c j"), [], [Wb[5], Wb[6]])
                z3 = zeros_f[:, 0:20].rearrange("p (c t) -> p c t", t=2)
                for sq, ln in (("c", CTX), ("l", NL)):
                    dma_pool(XR[sq][:, :, 0:2].rearrange("c p t -> p c t"), z3, [cst_b], [pad_b[(sq, "L")]])
                    if sq == "c" or not PAIR:
                        dma_pool(XR[sq][:, :, ln + 2:ln + 4].rearrange("c p t -> p c t"), z3, [cst_b], [pad_b[(sq, "R")]])

                GS = 5
                hp1_cb = [Buf(f"hp1_{c}") for c in range(NCH)]
                hp2_cb = [Buf(f"hp2_{c}") for c in range(NCH)]

                def gates_p1(R, d, cc, nt, xc, xcb):
                    xb16, xb16b = R["xcb"].next()
                    copy("act", xb16[:, 0:nt], xc[:, 0:nt], [xcb], [xb16b])
                    pr, prb = ps_o[cc % 2], ps_o_b[cc % 2]
                    pg, pgb = ps_su[cc % 2], ps_su_b[cc % 2]
                    mm_group([(pr[:, 0:nt], GW[:, d, 0, cc, :], xb16[:, 0:nt], True, True)], [xb16b, Wb[5], Wb[6]], [prb])
                    mm_group([(pg[:, 0:nt], GW[:, d, 1, cc, :], xb16[:, 0:nt], True, True)], [xb16b, Wb[5], Wb[6]], [pgb])
                    r_, rb = R["r"].next()
                    gi, gib = R["gi"].next()
                    bcol = (j * 2 + d) * NCH + cc
                    act(r_[:, 0:nt], pr[:, 0:nt], AF.Sigmoid, [prb, small_b], [rb], bias=ba_t[:, bcol:bcol + 1])
                    act(gi[:, 0:nt], pg[:, 0:nt], AF.Sigmoid, [pgb, small_b], [gib], bias=bx_t[:, bcol:bcol + 1])
                    return dict(cc=cc, xc=xc, xcb=xcb, r=r_, rb=rb, gi=gi, gib=gib)

                def gates_rest(R, d, nt, cx):
                    for c_ in cx:
                        c_["a"], c_["ab"] = R["a"].next()
                        cc = c_["cc"]
                        act(c_["a"][:, 0:nt], c_["r"][:, 0:nt], AF.Exp, [c_["rb"], cl_b], [c_["ab"]], scale=cl[:, d * NCH + cc:d * NCH + cc + 1])
                    for c_ in cx:
                        c_["q"], c_["qb"] = R["q"].next()
                        tt("pool", c_["q"][:, 0:nt], c_["a"][:, 0:nt], c_["a"][:, 0:nt], ALU.mult, [c_["ab"]], [c_["qb"]])
                    for c_ in cx:
                        act(c_["q"][:, 0:nt], c_["q"][:, 0:nt], AF.Sqrt, [c_["qb"], cst_b], [c_["qb"]], scale=-1.0, bias=one_col)
                    for c_ in cx:
                        c_["u"], c_["ub"] = R["u"].next()
                        tt("pool", c_["u"][:, 0:nt], c_["q"][:, 0:nt], c_["gi"][:, 0:nt], ALU.mult, [c_["qb"], c_["gib"]], [c_["ub"]])
                        tt("dve", c_["u"][:, 0:nt], c_["u"][:, 0:nt], c_["xc"][:, 0:nt], ALU.mult, [c_["ub"], c_["xcb"]], [c_["ub"]])


                pes = contextlib.ExitStack()
                with pes:
                    R0 = {
                        "x": Ring(nc, pes, "l0x", [128, D], F32, 2), "junk": Ring(nc, pes, "l0j", [128, D], F32, 1),
                        "stat": Ring(nc, pes, "l0s", [128, 4], F32, 2), "xhat": Ring(nc, pes, "l0xh", [128, D], BF16, 2),
                        "hT": Ring(nc, pes, "l0hT", [128, 8 * 512], BF16, 2), "xr": Ring(nc, pes, "l0xr", [128, 512], F32, 2),
                        "sg": Ring(nc, pes, "l0sg", [128, 512], BF16, 2),
                        "halo": Ring(nc, pes, "l0halo", [128, 64], F32, 2),
                    }
                    R0["xg"] = R0["x"]

                    def L0_tile(sq, t0, nt):
                        R = R0
                        base = 0 if sq == "c" else CTX
                        v = 1 if sq == "c" else 0
                        hT, hTb = R["hT"].next()
                        hv = hT[:].rearrange("p (k t) -> p k t", k=8)
                        for c in range(nt // 128):
                            norm_chunk(R, src_ap, src_buf, base + t0 + c * 128, v, lambda k, hv=hv, c=c: hv[:, k, c * 128:(c + 1) * 128], hTb)
                        for cc in range(2 * NCH):
                            pp, ppb = ps_proj[cc % 2], ps_proj_b[cc % 2]
                            wv = wslot(cc // 4).rearrange("p (k c) -> p k c", k=8)
                            mm_group([(pp[:, 0:nt], wv[:, k, (cc % 4) * 128:(cc % 4 + 1) * 128], hv[:, k, 0:nt], k == 0, k == 7) for k in range(8)],
                                     [hTb, Wb[cc // 4]], [ppb])
                            if cc < NCH:
                                xr, xrb = R["xr"].next()
                                copy("act", xr[:, 0:nt], pp[:, 0:nt], [ppb], [xrb])
                                dma_pool(XR[sq][cc, :, 2 + t0:2 + t0 + nt], xr[:, 0:nt], [xrb], [XRb[(sq, t0)]])
                            else:
                                sg, sgb = R["sg"].next()
                                act(sg[:, 0:nt], pp[:, 0:nt], AF.Silu, [ppb], [sgb])
                                dma_pool(SGD[sq][cc - NCH, :, t0:t0 + nt], sg[:, 0:nt], [sgb], [SGb_d[(sq, t0)]])
                    def L0_halo():
                        R = R0
                        lastb = XRb[("l", NL - 512)]
                        hl, hlb = R["halo"].next()
                        dma_sp(hl[:, 0:20].rearrange("p (c t) -> p c t", t=2), XRl[:, :, NL:NL + 2].rearrange("c p t -> p c t"), [lastb], [hlb])
                        hr, hrb = R["halo"].next()
                        hc_, hcb_ = exchange_start(l, hl[:, 0:32], hlb, 32)
                        exchange_finish(R, hc_, hcb_, hr, hrb, 32)
                        h3 = hr[:, 0:20].rearrange("p (c t) -> p c t", t=2)
                        dma_pool(XRl[:, :, NL + 2:NL + 3].rearrange("c p t -> p c t"), h3[:, :, 1:2], [hrb], [pad_b[("l", "R")]], slow=True)
                        dma_pool(XRl[:, :, NL + 3:NL + 4].rearrange("c p t -> p c t"), h3[:, :, 0:1], [hrb], [pad_b[("l", "R")]], slow=True)

                    R = {
                        "win": Ring(nc, pes, "l1w", [128, 516], F32, 2), "xc": Ring(nc, pes, "l1xc", [128, 512], F32, 5),
                        "xcb": Ring(nc, pes, "l1xcb", [128, 512], BF16, 2), "r": Ring(nc, pes, "l1r", [128, 512], F32, 5),
                        "gi": Ring(nc, pes, "l1gi", [128, 512], F32, 5), "a": Ring(nc, pes, "l1a", [128, 512], F32, 5),
                        "q": Ring(nc, pes, "l1q", [128, 512], F32, 5), "u": Ring(nc, pes, "l1u", [128, 512], F32, 5),
                        "h": Ring(nc, pes, "l1h", [128, 512], F32, 3),
                    }
                    S.add("dve", "c", [lambda e: e.memset(hprev[:], 0.0)], [], hp1_cb + hp2_cb)
                    def L1_tile(sq, t0, nt):
                        ln = CTX if sq == "c" else NL
                        rd = [XRb[(sq, t0)]]
                        if t0 >= 512:
                            rd.append(XRb[(sq, t0 - 512)])
                        else:
                            rd.append(pad_b[(sq, "L")])
                        if t0 + nt < ln:
                            rd.append(XRb[(sq, t0 + nt)])
                        else:
                            rd.append(pad_b[(sq, "R")])
                        for g0 in range(0, NCH, GS):
                            cx = []
                            for cc in range(g0, g0 + GS):
                                win, winb = R["win"].next()
                                dma_sp(win[:, 0:nt + 4], XR[sq][cc, :, t0:t0 + nt + 4], rd, [winb])
                                xc, xcb = R["xc"].next()
                                wc = lambda tap, cc=cc: cw_t[:, (j * NCH + cc) * 5 + tap:(j * NCH + cc) * 5 + tap + 1]
                                ts("dve", xc[:, 0:nt], win[:, 0:nt], wc(0), cb_t[:, j * NCH + cc:j * NCH + cc + 1], ALU.mult, ALU.add,
                                   [winb, small_b], [xcb])
                                for tap in range(1, 5):
                                    stt(xc[:, 0:nt], win[:, tap:tap + nt], wc(tap), xc[:, 0:nt], ALU.mult, ALU.add, [winb, small_b, xcb], [xcb])
                                dma_pool(XC[sq][cc, :, t0:t0 + nt], xc[:, 0:nt], [xcb], [XCb[(sq, t0)]])
                                cx.append(gates_p1(R, 0, cc, nt, xc, xcb))
                            gates_rest(R, 0, nt, cx)
                            for c_ in cx:
                                cc, a_, ab, u_, ub = c_["cc"], c_["a"], c_["ab"], c_["u"], c_["ub"]
                                h_, hb = R["h"].next()
                                S.add("dve", "c", [lambda e, h_=h_, a_=a_, u_=u_, cc=cc, nt=nt: e.tensor_tensor_scan(
                                    out=h_[:, 0:nt], data0=a_[:, 0:nt], data1=u_[:, 0:nt], initial=hprev[:, cc:cc + 1], op0=ALU.mult, op1=ALU.add)],
                                    [ab, ub, hp1_cb[cc]], [hb])
                                copy("act", hprev[:, cc:cc + 1], h_[:, nt - 1:nt], [hb], [hp1_cb[cc]])
                                dma_pool(H1[sq][cc, :, t0:t0 + nt], h_[:, 0:nt], [hb], [H1b[(sq, t0)]])

                    NT = len(tiles)
                    for ti in range(NT + 2):
                        if ti < NT:
                            L0_tile(*tiles[ti])
                            if ti == NT - 1 and PAIR:
                                L0_halo()
                        if 2 <= ti:
                            L1_tile(*tiles[ti - 2])
                    S.barrier()

                pes = contextlib.ExitStack()
                with pes:
                    R = {
                        "xc": Ring(nc, pes, "l2xc", [128, 512], F32, 5),
                        "xcb": Ring(nc, pes, "l2xcb", [128, 512], BF16, 2), "r": Ring(nc, pes, "l2r", [128, 512], F32, 5),
                        "gi": Ring(nc, pes, "l2gi", [128, 512], F32, 5), "a": Ring(nc, pes, "l2a", [128, 512], F32, 5),
                        "q": Ring(nc, pes, "l2q", [128, 512], F32, 5), "u": Ring(nc, pes, "l2u", [128, 512], F32, 5),
                        "h": Ring(nc, pes, "l2h", [128, 512], F32, 2), "h1": Ring(nc, pes, "l2h1", [128, 512], F32, 2),
                        "sg": Ring(nc, pes, "l2sg", [128, 512], BF16, 2), "Z": Ring(nc, pes, "l2Z", [128, NCH * 512], BF16, 1),
                        "x": Ring(nc, pes, "l2x", [128, D], F32, 2), "junk": Ring(nc, pes, "l2j", [128, D], F32, 1),
                        "stat": Ring(nc, pes, "l2s", [128, 4], F32, 2), "tmp": Ring(nc, pes, "l2tmp", [128, D], F32, 1),
                        "xn": Ring(nc, pes, "l2xn", [128, D], F32, 2),
                    }
                    R["xg"] = R["xn"]
                    Wo = W[:, 0:NCH * 1024].rearrange("p (k c) -> p k c", k=NCH)
                    dma_pool(Wo, lru_w_out[j].rearrange("(k p) c -> p k c", p=128), [], [Wb[0], Wb[1], Wb[2]])
                    hp2 = hprev[:, NCH:2 * NCH]
                    if PAIR:
                        pst_ = pes.enter_context(nc.sbuf_tensor(f"lpstate{l}", [128, 32], F32))
                        pst_b = Buf("lpstate")
                        hx, hxb = R["xg"].next()
                        copy("dve", hx[:, 0:NCH], hprev[:, 0:NCH], hp1_cb, [hxb])
                        sc_, scb_ = exchange_start(l - 1, hx[:, 0:32], hxb, 32)
                        exchange_finish(R, sc_, scb_, pst_, pst_b, 32)
                    order = [tiles[0]] + tiles[:0:-1]
                    for (sq, t0, nt) in order:
                        base = 0 if sq == "c" else CTX
                        v = 1 if sq == "c" else 0
                        if PAIR and sq == "l" and t0 == NL - 512:
                            copy("dve", hp2, pst_[:, 0:NCH], [pst_b] + hp2_cb, hp2_cb)
                        skip_out = last and sq == "c" and not dbg
                        Z, Zb = R["Z"].next()
                        zv = Z[:].rearrange("p (c t) -> p c t", c=NCH)
                        for g0 in range(0, NCH, GS):
                            cx = []
                            for cc in range(g0, g0 + GS):
                                xc, xcb = R["xc"].next()
                                dma_sp(xc[:, 0:nt], XC[sq][cc, :, t0:t0 + nt], [XCb[(sq, t0)]], [xcb])
                                cx.append(gates_p1(R, 1, cc, nt, xc, xcb))
                            gates_rest(R, 1, nt, cx)
                            for c_ in cx:
                                cc, a_, ab, u_, ub = c_["cc"], c_["a"], c_["ab"], c_["u"], c_["ub"]
                                h_, hb = R["h"].next()
                                S.add("dve", "c", [lambda e, h_=h_, a_=a_, u_=u_, cc=cc, nt=nt: e.tensor_tensor_scan(
                                    out=h_[:, 0:nt][:, ::-1], data0=a_[:, 0:nt][:, ::-1], data1=u_[:, 0:nt][:, ::-1],
                                    initial=hp2[:, cc:cc + 1], op0=ALU.mult, op1=ALU.add)], [ab, ub, hp2_cb[cc]], [hb])
                                copy("act", hp2[:, cc:cc + 1], h_[:, 0:1], [hb], [hp2_cb[cc]])
                                if skip_out:
                                    continue
                                h1, h1b = R["h1"].next()
                                dma_sp(h1[:, 0:nt], H1[sq][cc, :, t0:t0 + nt], [H1b[(sq, t0)]], [h1b])
                                sg, sgb = R["sg"].next()
                                dma_sp(sg[:, 0:nt], SGD[sq][cc, :, t0:t0 + nt], [SGb_d[(sq, t0)]], [sgb])
                                tt("pool", h1[:, 0:nt], h1[:, 0:nt], h_[:, 0:nt], ALU.add, [h1b, hb], [h1b])
                                tt("dve", zv[:, cc, 0:nt], h1[:, 0:nt], sg[:, 0:nt], ALU.mult, [h1b, sgb], [Zb])
                        if skip_out:
                            continue
                        for c in range(nt // 128):
                            row0 = base + t0 + c * 128
                            xt, xb = R["x"].next()
                            dma_sp(xt[:], src_ap[row0:row0 + 128, :], [src_buf], [xb])
                            for g in range(2):
                                mm_group([(ps_proj[g][:], zv[:, cc, c * 128:(c + 1) * 128], Wo[:, cc, g * 512:(g + 1) * 512], cc == 0, cc == NCH - 1)
                                          for cc in range(NCH)], [Zb, Wb[0], Wb[1], Wb[2]], [ps_proj_b[g]])
                            if last and v == 1:
                                post_chunk(R, xt, xb, v, dbg_ctx, dst_buf, row0)
                            elif last:
                                post_chunk(R, xt, xb, v, out, dst_buf, row0 - CTX)
                            else:
                                post_chunk(R, xt, xb, v, dst_ap, dst_buf, row0)
                    S.barrier()

        xin_b = Buf("xin")
        Xb = [Buf("Xs0"), Buf("Xs1")]
        out_b = Buf("out")
        src_ap, src_buf = xin, xin_b
        for l in range(depth):
            last = l == depth - 1
            dst_ap, dst_buf = (Xs[l % 2], Xb[l % 2])
            if last:
                dst_buf = out_b
            if l % 2 == 0:
                retention_layer(l, src_ap, src_buf, dst_ap, dst_buf, last)
            else:
                lru_layer(l, src_ap, src_buf, dst_ap, dst_buf, last)
            src_ap, src_buf = dst_ap, dst_buf
            if not last:
                S.rotate()
        S.final_wait("sp")

        block = es.enter_context(nc.Block())

        @block.tensor
        def _(e):
            S.emit("pe", e)

        @block.scalar
        def _(e):
            S.emit("act", e)

        @block.vector
        def _(e):
            S.emit("dve", e)

        @block.gpsimd
        def _(e):
            S.emit("pool", e)

        @block.sync
        def _(e):
            S.emit("sp", e)
    return nc


def _col(vec, nk):
    return np.ascontiguousarray(np.asarray(vec, np.float32).reshape(nk, 128).T)


def _rope_tables():
    n_rows = SEQ // GRID_W
    row = np.repeat(np.arange(n_rows, dtype=np.float32), GRID_W)
    col = np.tile(np.arange(GRID_W, dtype=np.float32), n_rows)
    n_freq = 64
    inv = (np.float32(10000.0) ** (-np.arange(n_freq, dtype=np.float32) / np.float32(n_freq))).astype(np.float32)
    ang = np.concatenate([row[:, None] * inv, col[:, None] * inv], axis=-1).astype(np.float32)
    return np.cos(ang).astype(np.float32), np.sin(ang).astype(np.float32)


def _core_inputs(core, inp, shared):
    if PAIR:
        b, half = core // 2, core % 2
    else:
        b, half = core, 0
    flip = half == 1
    dirs = (1, 0) if flip else (0, 1)
    x = inp["x"][b]
    ctx = inp["ctx"][b]
    cos, sin = shared["rope"]
    if PAIR:
        xs = x[half * NL:(half + 1) * NL]
        cs, sn = cos[half * NL:(half + 1) * NL], sin[half * NL:(half + 1) * NL]
    else:
        xs, cs, sn = x, cos, sin
    if flip:
        xs, cs, sn, ctx = xs[::-1], cs[::-1], sn[::-1], ctx[::-1]
    m = {}
    m["xin"] = np.ascontiguousarray(np.concatenate([ctx, xs], 0), dtype=np.float32)
    m["ropec"] = np.ascontiguousarray(np.concatenate([np.ones((CTX, 128), np.float32), cs], 0))
    m["ropes"] = np.ascontiguousarray(np.concatenate([np.zeros((CTX, 128), np.float32), sn], 0))
    cc = np.stack([_col(inp["c"][b], 8), _col(inp["c_ctx"], 8)], -1).reshape(128, 16)
    m["c_col"] = np.ascontiguousarray(cc)
    lg = np.asarray(inp["ret_log_decay"], np.float32)[:, list(dirs), :]
    m["ret_lg"] = np.ascontiguousarray(np.broadcast_to(lg.reshape(1, 16), (128, 16)))
    sel_d = list(dirs)
    m["lru_wa"] = np.ascontiguousarray(np.asarray(inp["lru_w_a"], np.float32)[:, sel_d])
    m["lru_wx"] = np.ascontiguousarray(np.asarray(inp["lru_w_x"], np.float32)[:, sel_d])
    def dcol(a):
        a = np.asarray(a, np.float32)[:, sel_d]
        return np.ascontiguousarray(np.concatenate([_col(a[j, d], NCH) for j in range(2) for d in range(2)], 1))
    m["lru_ba"] = dcol(inp["lru_b_a"])
    m["lru_bx"] = dcol(inp["lru_b_x"])
    m["lru_lam"] = dcol(inp["lru_lambda"])
    cw = np.asarray(inp["lru_conv_w"], np.float32)
    z = np.zeros_like(cw[:, :1])
    cw5 = np.concatenate([z, cw[:, ::-1]], 1) if flip else np.concatenate([cw, z], 1)
    cwc = np.stack([np.stack([_col(cw5[j, t], NCH) for t in range(5)], -1) for j in range(2)], 1)
    m["lru_cw"] = np.ascontiguousarray(cwc.reshape(128, 100))
    cst = shared["consts"].copy()
    if PAIR:
        cst[:, 649] = 1.0 if half == 1 else 0.0
        cst[:, 650] = 1.0 if half == 0 else 0.0
    m["consts"] = cst
    for k in ("mod_w", "mod_b_col", "npre_col", "npost_col", "ret_w_in", "ret_gn_col", "ret_w_out", "lru_w_in", "lru_cb", "lru_w_out"):
        m[k] = shared[k]
    return m


def _shared_inputs(inp):
    sh = {}
    sh["rope"] = _rope_tables()
    sh["mod_w"] = np.ascontiguousarray(inp["mod_w"], dtype=np.float32)
    sh["mod_b_col"] = np.ascontiguousarray(np.concatenate([_col(inp["mod_b"][l], 24) for l in range(DEPTH)], 1))
    sh["npre_col"] = np.ascontiguousarray(np.concatenate([_col(inp["norm_pre"][l], 8) for l in range(DEPTH)], 1))
    sh["npost_col"] = np.ascontiguousarray(np.concatenate([_col(inp["norm_post"][l], 8) for l in range(DEPTH)], 1))
    perm = np.arange(6144)
    for blk in range(2):
        for h in range(RET_H):
            base = blk * 1024 + h * 256
            perm[base:base + 256] = np.concatenate([base + np.arange(0, 256, 2), base + np.arange(1, 256, 2)])
    sh["ret_w_in"] = np.ascontiguousarray(np.asarray(inp["ret_w_in"], np.float32)[:, :, perm])
    sh["ret_gn_col"] = np.ascontiguousarray(np.concatenate([_col(inp["ret_gn"][j], 16) for j in range(2)], 1))
    sh["ret_w_out"] = np.ascontiguousarray(inp["ret_w_out"], dtype=np.float32)
    sh["lru_w_in"] = np.ascontiguousarray(inp["lru_w_in"], dtype=np.float32)
    sh["lru_cb"] = np.ascontiguousarray(np.concatenate([_col(inp["lru_conv_b"][j], NCH) for j in range(2)], 1))
    sh["lru_w_out"] = np.ascontiguousarray(inp["lru_w_out"], dtype=np.float32)
    cst = np.zeros((128, 128 * 5 + 16), np.float32)
    p = np.arange(128, dtype=np.float32)
    cst[:, 0:128] = np.eye(128, dtype=np.float32)
    cst[:, 128:256] = 1.0
    cst[:, 256:384] = (p[:, None] <= p[None, :])
    cst[:, 384:512] = (p[:, None] >= p[None, :])
    coefs = np.stack([127 - p, p + 1, -(p + 1), p, 128 - p, p - 128, np.full(128, 128.0, np.float32)], 1)
    cst[:, 640:647] = coefs
    cst[:, 647] = EPS
    cst[:, 648] = 1.0
    sh["consts"] = cst
    return sh


_NC_CACHE = {}


def kernel(**inputs):
    inp = {k: np.asarray(v) for k, v in inputs.items()}
    if "nc" not in _NC_CACHE:
        _NC_CACHE["nc"] = build_program()
    nc = _NC_CACHE["nc"]
    shared = _shared_inputs(inp)
    in_maps = [_core_inputs(c, inp, shared) for c in range(NCORES)]
    res = run_bass_kernel_spmd(nc, in_maps, core_ids=list(range(NCORES)))
    outp = np.empty((BATCH, SEQ, D), np.float32)
    for c in range(NCORES):
        o = np.asarray(res.results[c]["out"], np.float32)
        if PAIR:
            b, half = c // 2, c % 2
            outp[b, half * NL:(half + 1) * NL] = o[::-1] if half == 1 else o
        else:
            outp[c] = o
    return outp
```

```python
import contextlib
import numpy as np
import concourse.bass as bass
import concourse.mybir as mybir
from concourse.bass_utils import run_bass_kernel_spmd

F32 = mybir.dt.float32
BF16 = mybir.dt.bfloat16
AF = mybir.ActivationFunctionType
ALU = mybir.AluOpType
AX = mybir.AxisListType

D = 1024
DEPTH = 4
SEQ = 8192
BATCH = 4
CTX = 256
GRID_W = 64
EPS = 1e-6
RET_H = 4
LRU_W = 1280
NCH = 10
PAIR = True
NCORES = 8 if PAIR else 4
NL = SEQ // 2 if PAIR else SEQ
NTOK = CTX + NL
NCHUNK = NTOK // 128
NCC = CTX // 128


class Sem:
    def __init__(self, handle, stream, cls, step):
        self.h = handle
        self.stream = stream
        self.cls = cls
        self.step = step
        self.count = 0


class Buf:
    __slots__ = ("name", "w", "r")

    def __init__(self, name=""):
        self.name = name
        self.w = None
        self.r = {}


class Stream:
    def __init__(self, name):
        self.name = name
        self.items = []
        self.waited = {}


class Sched:
    NDMA = 14

    def __init__(self, nc, es):
        self.nc = nc
        self.es = es
        self.streams = {n: Stream(n) for n in ("pe", "act", "dve", "pool", "sp")}
        self.sems = {}
        self.all_sems = []
        self.nsem = 0
        self.dma_sems = {}
        self.dma_rr = {}
        for st in ("pool", "sp", "act"):
            self.dma_sems[st] = [self._new_sem(st, "d", 16) for _ in range(self.NDMA)]
            self.dma_rr[st] = 0
        self.rotate()

    def _new_sem(self, st, cls, step):
        h = self.es.enter_context(self.nc.semaphore(f"s{self.nsem}_{st}_{cls}"))
        self.nsem += 1
        sm = Sem(h, st, cls, step)
        self.all_sems.append(sm)
        return sm

    def rotate(self):
        for st in ("pe", "act", "dve", "pool"):
            self.sems[(st, "c")] = self._new_sem(st, "c", 1)

    def add(self, stream, cls, fns, reads=(), writes=()):
        st = self.streams[stream]
        need = {}
        if cls == "d":
            i = self.dma_rr[stream]
            self.dma_rr[stream] = (i + 1) % self.NDMA
            sm = self.dma_sems[stream][i]
            if sm.count > 0 and st.waited.get(sm, 0) < sm.count:
                need[sm] = sm.count
        elif cls == "cc":
            sm = self._new_sem(stream, "cc", 1)
        else:
            sm = self.sems[(stream, cls)]
        raw = set()
        oth = set()
        for b in reads:
            if b.w is not None:
                raw.add(b.w)
        for b in writes:
            if b.w is not None:
                oth.add(b.w)
            for k, v in b.r.items():
                oth.add((k, v))
        for (dsm, v) in raw | oth:
            if dsm.stream == stream:
                if stream == "pe":
                    continue
                if dsm.cls == "c" and cls == "c" and (dsm, v) not in raw:
                    continue
            if st.waited.get(dsm, 0) >= v:
                continue
            if need.get(dsm, 0) < v:
                need[dsm] = v
        for k, v in need.items():
            st.waited[k] = v
        sm.count += sm.step
        tok = (sm, sm.count)
        st.items.append((list(need.items()), fns, sm))
        for b in reads:
            if b.r.get(sm, 0) < sm.count:
                b.r[sm] = sm.count
        for b in writes:
            b.w = tok
            b.r = {}
        return tok

    def barrier(self):
        for st in self.streams.values():
            need = []
            for sm in self.all_sems:
                if sm.count > 0 and st.waited.get(sm, 0) < sm.count:
                    if sm.stream == st.name and st.name == "pe":
                        continue
                    need.append((sm, sm.count))
                    st.waited[sm] = sm.count
            if need:
                st.items.append((need, [], None))

    def final_wait(self, stream="sp"):
        st = self.streams[stream]
        need = [(sm, sm.count) for sm in self.all_sems if sm.count > 0 and st.waited.get(sm, 0) < sm.count]
        st.items.append((need, [], None))

    def emit(self, stream, eng):
        for waits, fns, sm in self.streams[stream].items:
            for (wsm, v) in waits:
                eng.wait_ge(wsm.h, v)
            ins = None
            for f in fns:
                ins = f(eng)
            if ins is not None and sm is not None:
                ins.then_inc(sm.h, sm.step)


class Ring:
    _uid = [0]

    def __init__(self, nc, es, name, shape, dtype, n):
        Ring._uid[0] += 1
        name = f"{name}u{Ring._uid[0]}_"
        self.t = [es.enter_context(nc.sbuf_tensor(f"{name}{i}", shape, dtype)) for i in range(n)]
        self.b = [Buf(f"{name}{i}") for i in range(n)]
        self.i = 0

    def next(self):
        i = self.i
        self.i = (i + 1) % len(self.t)
        return self.t[i], self.b[i]


def build_program(depth=DEPTH, ncores=NCORES, dbg=False):
    nc = bass.Bass("TRN2", target_bir_lowering=False)
    dt_in = lambda n, s: nc.dram_tensor(n, s, F32, kind="ExternalInput").ap()
    xin = dt_in("xin", [NTOK, D])
    ropec = dt_in("ropec", [NTOK, 128])
    ropes = dt_in("ropes", [NTOK, 128])
    c_col = dt_in("c_col", [128, 16])
    mod_w = dt_in("mod_w", [DEPTH, D, 3 * D])
    mod_b_col = dt_in("mod_b_col", [128, DEPTH * 24])
    npre_col = dt_in("npre_col", [128, DEPTH * 8])
    npost_col = dt_in("npost_col", [128, DEPTH * 8])
    ret_w_in = dt_in("ret_w_in", [2, D, 6144])
    ret_lg = dt_in("ret_lg", [128, 16])
    ret_gn_col = dt_in("ret_gn_col", [128, 32])
    ret_w_out = dt_in("ret_w_out", [2, 2048, D])
    lru_w_in = dt_in("lru_w_in", [2, D, 2 * LRU_W])
    lru_cw = dt_in("lru_cw", [128, 2 * NCH * 5])
    lru_cb = dt_in("lru_cb", [128, 2 * NCH])
    lru_wa = dt_in("lru_wa", [2, 2, NCH, 128, 128])
    lru_wx = dt_in("lru_wx", [2, 2, NCH, 128, 128])
    lru_ba = dt_in("lru_ba", [128, 2 * 2 * NCH])
    lru_bx = dt_in("lru_bx", [128, 2 * 2 * NCH])
    lru_lam = dt_in("lru_lam", [128, 2 * 2 * NCH])
    lru_w_out = dt_in("lru_w_out", [2, LRU_W, D])
    consts = dt_in("consts", [128, 128 * 5 + 16])
    out = nc.dram_tensor("out", [NL, D], F32, kind="ExternalOutput").ap()
    dbg_ctx = nc.dram_tensor("dbg_ctx", [CTX, D], F32, kind="ExternalOutput").ap() if dbg else None
    rgroups = [[2 * i, 2 * i + 1] for i in range(ncores // 2)]
    dumps = {}
    DUMP_B = Buf("dump")

    dram = lambda n, s, d=F32: nc.dram_tensor(n, s, d)
    Xs = [dram("Xs0", [NTOK, D]), dram("Xs1", [NTOK, D])]
    QTd = dram("QTd", [NCHUNK, 128, 1024], BF16)
    KTd = dram("KTd", [NCHUNK, 128, 1024], BF16)
    Krd = dram("Krd", [NCHUNK, 128, 1024], BF16)
    Vd = dram("Vd", [NCHUNK, 128, 2048], BF16)
    O1d = dram("O1d", [NCHUNK, 128, 2048])
    XRc = dram("XRc", [NCH, 128, CTX + 4])
    XRl = dram("XRl", [NCH, 128, NL + 4])
    XCc = dram("XCc", [NCH, 128, CTX])
    XCl = dram("XCl", [NCH, 128, NL])
    H1c = dram("H1c", [NCH, 128, CTX])
    H1l = dram("H1l", [NCH, 128, NL])
    SGc = dram("SGc", [NCH, 128, CTX], BF16)
    SGl = dram("SGl", [NCH, 128, NL], BF16)
    cc_state_in = [dram(f"ccsi{j}", [128, 4096]) for j in range(2)]
    cc_state_out = [dram(f"ccso{j}", [256, 4096]) for j in range(2)]
    cc_small_in = [dram(f"ccmi{j}", [128, 32]) for j in range(4)]
    cc_small_out = [dram(f"ccmo{j}", [256, 32]) for j in range(4)]

    es = contextlib.ExitStack()
    with es:
        S = Sched(nc, es)
        sb = lambda n, s, d=F32: es.enter_context(nc.sbuf_tensor(n, s, d))
        W = sb("W", [128, 8 * 4096], BF16)
        Wb = [Buf(f"W{i}") for i in range(8)]
        wslot = lambda s: W[:, s * 4096:(s + 1) * 4096]
        cst = sb("cst", [128, 128 * 5 + 16])
        cst_b = Buf("cst")
        ident_f = cst[:, 0:128]
        ones_f = cst[:, 128:256]
        tri1 = cst[:, 256:384]
        tri2 = cst[:, 384:512]
        zeros_f = cst[:, 512:640]
        coef = cst[:, 640:647]
        eps_col = cst[:, 647:648]
        one_col = cst[:, 648:649]
        sel0 = cst[:, 649:650]
        sel1 = cst[:, 650:651]
        ident_b = sb("ident_b", [128, 128], BF16)
        ident_bb = Buf("ident_b")
        ccol = sb("ccol", [128, 16])
        act_bf = sb("act_bf", [128, 16], BF16)
        act_b = Buf("act")
        small = sb("small", [128, DEPTH * 24 + DEPTH * 16 + 16 + 32 + 100 + 120])
        small_b = Buf("small")
        o = 0
        modb_t = small[:, o:o + DEPTH * 24]; o += DEPTH * 24
        npre_t = small[:, o:o + DEPTH * 8]; o += DEPTH * 8
        npost_t = small[:, o:o + DEPTH * 8]; o += DEPTH * 8
        lg_t = small[:, o:o + 16]; o += 16
        gn_t = small[:, o:o + 32]; o += 32
        cw_t = small[:, o:o + 100]; o += 100
        cb_t = small[:, o:o + 20]; o += 20
        ba_t = small[:, o:o + 40]; o += 40
        bx_t = small[:, o:o + 40]; o += 40
        lam_t = sb("lam_t", [128, 40])
        modT = sb("modT", [128, 48])
        modT_b = Buf("modT")
        A_col = sb("A_col", [128, 16])
        G2col = sb("G2col", [128, 16])
        col_b = Buf("cols")
        G2bc = [sb("G2bc0", [128, D]), sb("G2bc1", [128, D])]
        G2bc_b = Buf("G2bc")
        ps_proj = [es.enter_context(nc.psum_tensor(f"ps_proj{i}", [128, 512], F32)) for i in range(2)]
        ps_proj_b = [Buf("pp0"), Buf("pp1")]
        ps_tr = es.enter_context(nc.psum_tensor("ps_tr", [128, 1024], BF16))
        ps_tr_b = Buf("ptr")
        ps_st = es.enter_context(nc.psum_tensor("ps_st", [128, 512], F32))
        ps_st_b = Buf("pst")
        ps_o = [es.enter_context(nc.psum_tensor(f"ps_o{i}", [128, 512], F32)) for i in range(2)]
        ps_o_b = [Buf("po0"), Buf("po1")]
        ps_su = [es.enter_context(nc.psum_tensor(f"ps_su{i}", [128, 512], F32)) for i in range(2)]
        ps_su_b = [Buf("psu0"), Buf("psu1")]

        def dma_sp(out_ap, in_ap, reads, writes):
            S.add("sp", "d", [lambda e: e.dma_start(out=out_ap, in_=in_ap)], reads, writes)

        def dma_act(out_ap, in_ap, reads, writes):
            S.add("act", "d", [lambda e: e.dma_start(out=out_ap, in_=in_ap)], reads, writes)

        def dma_pool(out_ap, in_ap, reads, writes, slow=False):
            if slow:
                S.add("pool", "d", [lambda e: e.dma_start(out=out_ap, in_=in_ap, allow_slow_non_contiguous=True)], reads, writes)
            else:
                S.add("pool", "d", [lambda e: e.dma_start(out=out_ap, in_=in_ap)], reads, writes)

        def act(out_ap, in_ap, func, reads, writes, scale=None, bias=None):
            kw = {}
            if scale is not None:
                kw["scale"] = scale
            if bias is not None:
                kw["bias"] = bias
            S.add("act", "c", [lambda e: e.activation(out=out_ap, in_=in_ap, func=func, **kw)], reads, writes)

        def ts(stream, out_ap, in_ap, s1, s2, op0, op1, reads, writes):
            if op1 is None:
                S.add(stream, "c", [lambda e: e.tensor_scalar(out=out_ap, in0=in_ap, scalar1=s1, scalar2=None, op0=op0)], reads, writes)
            else:
                S.add(stream, "c", [lambda e: e.tensor_scalar(out=out_ap, in0=in_ap, scalar1=s1, scalar2=s2, op0=op0, op1=op1)], reads, writes)

        def tt(stream, out_ap, a, b, op, reads, writes):
            S.add(stream, "c", [lambda e: e.tensor_tensor(out=out_ap, in0=a, in1=b, op=op)], reads, writes)

        def stt(out_ap, in0, scalar, in1, op0, op1, reads, writes):
            S.add("dve", "c", [lambda e: e.scalar_tensor_tensor(out=out_ap, in0=in0, scalar=scalar, in1=in1, op0=op0, op1=op1)], reads, writes)

        def copy(stream, out_ap, in_ap, reads, writes):
            if stream == "act":
                S.add("act", "c", [lambda e: e.copy(out=out_ap, in_=in_ap)], reads, writes)
            else:
                S.add(stream, "c", [lambda e: e.tensor_copy(out=out_ap, in_=in_ap)], reads, writes)

        def mm_group(specs, reads, writes):
            fns = []
            for (o_, l_, r_, st_, sp_) in specs:
                fns.append(lambda e, o_=o_, l_=l_, r_=r_, st_=st_, sp_=sp_: e.matmul(o_, lhsT=l_, rhs=r_, start=st_, stop=sp_))
            S.add("pe", "c", fns, reads, writes)

        def tr_group(specs, reads, writes):
            fns = []
            for (o_, i_) in specs:
                fns.append(lambda e, o_=o_, i_=i_: e.transpose(o_, i_, ident_b[:]))
            S.add("pe", "c", fns, list(reads) + [ident_bb], writes)

        def dump(name, ap, buf, F):
            if not dbg:
                return
            t = nc.dram_tensor("dmp_" + name, [128, F], F32, kind="ExternalOutput").ap()
            dumps[name] = t
            dma_pool(t[:, :], ap, [buf], [DUMP_B])

        dma_sp(cst[:], consts[:, :], [], [cst_b])
        dma_pool(ident_b[:], consts[:, 0:128], [], [ident_bb])
        dma_sp(ccol[:], c_col[:, :], [], [act_b])
        dma_sp(modb_t, mod_b_col[:, :], [], [small_b])
        dma_sp(npre_t, npre_col[:, :], [], [small_b])
        dma_sp(npost_t, npost_col[:, :], [], [small_b])
        dma_sp(lg_t, ret_lg[:, :], [], [small_b])
        dma_sp(gn_t, ret_gn_col[:, :], [], [small_b])
        dma_sp(cw_t, lru_cw[:, :], [], [small_b])
        dma_sp(cb_t, lru_cb[:, :], [], [small_b])
        dma_sp(ba_t, lru_ba[:, :], [], [small_b])
        dma_sp(bx_t, lru_bx[:, :], [], [small_b])
        dma_sp(lam_t[:], lru_lam[:, :], [], [small_b])
        act(act_bf[:], ccol[:], AF.Silu, [act_b], [act_b])

        def prep_layer(l, les):
            lsb = lambda n, s, d=F32: les.enter_context(nc.sbuf_tensor(n, s, d))
            dg = Ring(nc, les, f"dg{l}_", [128, 128], F32, 2)
            actv = act_bf[:].rearrange("p (k v) -> p k v", v=2)
            for cg in range(6):
                s = cg % 2
                wv = wslot(s).rearrange("p (k c) -> p k c", k=8)
                dma_pool(wv, mod_w[l][:, cg * 512:(cg + 1) * 512].rearrange("(k p) c -> p k c", p=128), [], [Wb[s]])
                specs = []
                for cc in range(4):
                    ck = cg * 4 + cc
                    for k in range(8):
                        specs.append((ps_st[:, ck * 2:ck * 2 + 2], wv[:, k, cc * 128:(cc + 1) * 128], actv[:, k, :], k == 0, k == 7))
                mm_group(specs, [Wb[s], act_b], [ps_st_b])
            tt("dve", modT[:].rearrange("p (c v) -> p c v", v=2), ps_st[:, 0:48].rearrange("p (c v) -> p c v", v=2),
               modb_t[:, l * 24:(l + 1) * 24].unsqueeze(2).to_broadcast([128, 24, 2]), ALU.add, [ps_st_b, small_b], [modT_b])
            npre_bc = npre_t[:, l * 8:(l + 1) * 8].unsqueeze(2).to_broadcast([128, 8, 2])
            npost_bc = npost_t[:, l * 8:(l + 1) * 8].unsqueeze(2).to_broadcast([128, 8, 2])
            v3 = lambda t: t.rearrange("p (c v) -> p c v", v=2)
            stt(v3(A_col[:]), v3(modT[:, 16:32]), 1.0, npre_bc, ALU.add, ALU.mult, [modT_b, small_b], [col_b])
            tt("dve", v3(G2col[:]), v3(modT[:, 32:48]), npost_bc, ALU.mult, [modT_b, small_b, col_b], [col_b])
            for v in range(2):
                for half in range(2):
                    specs = []
                    dbs = []
                    for kk in range(4):
                        k = half * 4 + kk
                        dt_, db_ = dg.next()
                        ts("dve", dt_[:], ident_f, G2col[:, k * 2 + v:k * 2 + v + 1], None, ALU.mult, None, [cst_b, col_b], [db_])
                        mm_group([(ps_o[half][:, kk * 128:(kk + 1) * 128], ones_f, dt_[:], True, True)], [db_, cst_b], [ps_o_b[half]])
                    copy("act", G2bc[v][:, half * 512:(half + 1) * 512], ps_o[half][:], [ps_o_b[half]], [G2bc_b])

        def norm_a(R, src_ap, src_buf, row0):
            xt, xb = R["x"].next()
            dma_sp(xt[:], src_ap[row0:row0 + 128, :], [src_buf], [xb])
            jt, jb = R["junk"].next()
            st_t, st_b = R["stat"].next()
            act(jt[:], xt[:], AF.Square, [xb], [jb])
            S.add("dve", "c", [lambda e: e.tensor_reduce(out=st_t[:, 0:1], in_=jt[:], axis=AX.X, op=ALU.add)], [jb], [st_b])
            act(st_t[:, 1:2], st_t[:, 0:1], AF.Sqrt, [st_b, cst_b], [st_b], scale=1.0 / D, bias=eps_col)
            S.add("dve", "c", [lambda e: e.reciprocal(out=st_t[:, 2:3], in_=st_t[:, 1:2])], [st_b], [st_b])
            xh, xhb = R["xhat"].next()
            ts("dve", xh[:], xt[:], st_t[:, 2:3], None, ALU.mult, None, [xb, st_b], [xhb])
            return xt, xb, xh, xhb

        def norm_b(xh, xhb, v, hT_ap_fn, hT_buf):
            tr_group([(ps_tr[:, k * 128:(k + 1) * 128], xh[:, k * 128:(k + 1) * 128]) for k in range(8)], [xhb], [ps_tr_b])
            for k in range(8):
                act(hT_ap_fn(k), ps_tr[:, k * 128:(k + 1) * 128], AF.Identity, [ps_tr_b, col_b, modT_b], [hT_buf],
                    scale=A_col[:, k * 2 + v:k * 2 + v + 1], bias=modT[:, k * 2 + v:k * 2 + v + 1])

        def norm_chunk(R, src_ap, src_buf, row0, v, hT_ap_fn, hT_buf, keep_x=False):
            xt, xb, xh, xhb = norm_a(R, src_ap, src_buf, row0)
            norm_b(xh, xhb, v, hT_ap_fn, hT_buf)
            return xt, xb

        def post_chunk(R, xt, xb, v, dst_ap, dst_buf, dst_row0):
            jt, jb = R["junk"].next()
            st_t, st_b = R["stat"].next()
            for g in range(2):
                act(jt[:, g * 512:(g + 1) * 512], ps_proj[g][:], AF.Square, [ps_proj_b[g]], [jb])
            S.add("dve", "c", [lambda e: e.tensor_reduce(out=st_t[:, 0:1], in_=jt[:], axis=AX.X, op=ALU.add)], [jb], [st_b])
            act(st_t[:, 1:2], st_t[:, 0:1], AF.Sqrt, [st_b, cst_b], [st_b], scale=1.0 / D, bias=eps_col)
            S.add("dve", "c", [lambda e: e.reciprocal(out=st_t[:, 2:3], in_=st_t[:, 1:2])], [st_b], [st_b])
            tm, tmb = R["tmp"].next()
            xn, xnb = R["xn"].next()
            for g in range(2):
                stt(tm[:, g * 512:(g + 1) * 512], ps_proj[g][:], st_t[:, 2:3], G2bc[v][:, g * 512:(g + 1) * 512],
                    ALU.mult, ALU.mult, [ps_proj_b[g], st_b, G2bc_b], [tmb])
            tt("pool", xn[:], tm[:], xt[:], ALU.add, [tmb, xb], [xnb])
            if dst_row0 == CTX and v == 0 and "tm" not in dumps:
                dump("tm", tm[:], tmb, 1024); dump("ysq", jt[:], jb, 1024); dump("stat", st_t[:], st_b, 4)
            dma_pool(dst_ap[dst_row0:dst_row0 + 128, :], xn[:], [xnb], [dst_buf])

        def exchange_start(idx_big, src_ap, src_buf, F):
            if F > 32:
                cin, cout = cc_state_in[idx_big], cc_state_out[idx_big]
            else:
                cin, cout = cc_small_in[idx_big], cc_small_out[idx_big]
            cb = Buf("ccin")
            cob = Buf("ccout")
            dma_pool(cin[:, 0:F], src_ap, (src_buf if isinstance(src_buf, list) else [src_buf]), [cb])
            S.add("pool", "cc", [lambda e: e.collective_compute(
                "AllGather", ALU.bypass, replica_groups=rgroups,
                ins=[cin[:, :]], outs=[cout[:, :]])], [cb], [cob])
            return cout, cob

        def exchange_finish(R, cout, cob, dst_ap, dst_buf, F):
            step = min(F, 1024)
            for p0 in range(0, F, step):
                g0, g0b = R["xg"].next()
                g1, g1b = R["xg"].next()
                dma_sp(g0[:, 0:step], cout[0:128, p0:p0 + step], [cob], [g0b])
                dma_sp(g1[:, 0:step], cout[128:256, p0:p0 + step], [cob], [g1b])
                ts("dve", g0[:, 0:step], g0[:, 0:step], sel0, None, ALU.mult, None, [g0b, cst_b], [g0b])
                stt(dst_ap[:, p0:p0 + step], g1[:, 0:step], sel1, g0[:, 0:step], ALU.mult, ALU.add, [g0b, g1b, cst_b],
                    (dst_buf if isinstance(dst_buf, list) else [dst_buf]))

        def retention_layer(l, src_ap, src_buf, dst_ap, dst_buf, last):
            j = l // 2
            les = contextlib.ExitStack()
            with les:
                lsb = lambda n, s, d=F32: les.enter_context(nc.sbuf_tensor(f"{n}_L{l}", s, d))
                prep_layer(l, les)
                state = lsb("state", [128, 4096])
                state_bf = lsb("state_bf", [128, 4096], BF16)
                state_hb = [Buf(f"state{h}") for h in range(4)]
                statebf_hb = [Buf(f"state_bf{h}") for h in range(4)]
                state_b = state_hb
                statebf_b = statebf_hb
                mask = [lsb("mask1", [128, 512]), lsb("mask2", [128, 512])]
                tab = lsb("tab", [128, 2 * 28])
                lgn = lsb("lgn", [128, 8])
                tab_b = Buf("tab")
                mask_b = Buf("mask")
                ts("dve", tab[:, 0:8], lg_t[:, j * 8:(j + 1) * 8], -1.0, None, ALU.mult, None, [small_b], [tab_b])
                tt("dve", lgn[:], lg_t[:, j * 8:(j + 1) * 8], tab[:, 0:8], ALU.min, [small_b, tab_b], [tab_b])
                for d in range(2):
                    for i in range(7):
                        ts("dve", tab[:, d * 28 + i * 4:d * 28 + i * 4 + 4], lgn[:, d * 4:(d + 1) * 4], coef[:, i:i + 1], None,
                           ALU.mult, None, [tab_b, cst_b], [tab_b])
                act(tab[:], tab[:], AF.Exp, [tab_b], [tab_b])
                for d in range(2):
                    for i in ((0, 2) if d == 0 else (3, 5)):
                        ts("dve", tab[:, d * 28 + i * 4:d * 28 + i * 4 + 4], tab[:, d * 28 + i * 4:d * 28 + i * 4 + 4], 0.0625, None,
                           ALU.mult, None, [tab_b], [tab_b])
                T = lambda d, i, h: tab[:, d * 28 + i * 4 + h:d * 28 + i * 4 + h + 1]
                for d in range(2):
                    for h in range(4):
                        ts("dve", mask[d][:, h * 128:(h + 1) * 128], tri1 if d == 0 else tri2, T(d, 2 if d == 0 else 5, h), None,
                           ALU.mult, None, [tab_b, cst_b], [mask_b])
                KD = (0, 3)
                QD = (1, 4)

                def chunk_rows(n):
                    return n * 128, (1 if n < NCC else 0)

                def scan_part(R, d, n, QT, QTb, KT, KTb, Kr, Krb, V, Vb, o_t, o_b, o1_t, o1_b, need_o):
                    if need_o:
                        specs = []
                        for h in range(4):
                            for dc in range(2):
                                specs.append((ps_st[:, h * 128:(h + 1) * 128], KT[:, (2 * h + dc) * 128:(2 * h + dc + 1) * 128],
                                              QT[:, (2 * h + dc) * 128:(2 * h + dc + 1) * 128], dc == 0, dc == 1))
                        mm_group(specs, [KTb, QTb], [ps_st_b])
                        P, Pb = R["P"].next()
                        tt("dve", P[:], ps_st[:], mask[d][:], ALU.mult, [ps_st_b, mask_b], [Pb])
                    Kd, Kdb = R["Kdec"].next()
                    ts_bc = tab[:, d * 28 + KD[d] * 4:d * 28 + KD[d] * 4 + 4].unsqueeze(2).to_broadcast([128, 4, 256])
                    tt("pool", Kd[:].rearrange("p (h c) -> p h c", h=4), Kr[:].rearrange("p (h c) -> p h c", h=4), ts_bc, ALU.mult, [Krb, tab_b], [Kdb])
                    for h in range(4):
                        if need_o:
                            po, pob = ps_o[h % 2], ps_o_b[h % 2]
                            specs = [(po[:], P[:, h * 128:(h + 1) * 128], V[:, h * 512:(h + 1) * 512], True, False)]
                            for dc in range(2):
                                specs.append((po[:], QT[:, (2 * h + dc) * 128:(2 * h + dc + 1) * 128],
                                              state_bf[:, (h * 2 + dc) * 512:(h * 2 + dc + 1) * 512], False, dc == 1))
                            mm_group(specs, [Pb, Vb, QTb, statebf_hb[h]], [pob])
                            if d == 0:
                                act(o_t[:, h * 512:(h + 1) * 512], po[:], AF.Identity, [pob, tab_b], [o_b], scale=T(d, QD[d], h))
                            else:
                                stt(o_t[:, h * 512:(h + 1) * 512], po[:], T(d, QD[d], h), o1_t[:, h * 512:(h + 1) * 512],
                                    ALU.mult, ALU.add, [pob, tab_b, o1_b], [o_b])
                        for dc in range(2):
                            mm_group([(ps_su[dc][:], Kd[:, h * 256 + dc * 128:h * 256 + (dc + 1) * 128], V[:, h * 512:(h + 1) * 512], True, True)],
                                     [Kdb, Vb], [ps_su_b[dc]])
                            sl = slice((h * 2 + dc) * 512, (h * 2 + dc + 1) * 512)
                            stt(state[:, sl], state[:, sl], T(d, 6, h), ps_su[dc][:], ALU.mult, ALU.add,
                                [ps_su_b[dc], tab_b, state_hb[h]], [state_hb[h]])
                        for dc in range(2):
                            sl = slice((h * 2 + dc) * 512, (h * 2 + dc + 1) * 512)
                            copy("act", state_bf[:, sl], state[:, sl], [state_hb[h]], [statebf_hb[h]])

                def zero_state():
                    S.add("dve", "c", [lambda e: e.memset(state[:], 0.0)], [], state_hb)
                    S.add("pool", "c", [lambda e: e.memset(state_bf[:], 0.0)], [], statebf_hb)

                Qb_d = [Buf(f"QTd{n}") for n in range(NCHUNK)]
                Kb_d = [Buf(f"KTd{n}") for n in range(NCHUNK)]
                Krb_d = [Buf(f"Krd{n}") for n in range(NCHUNK)]
                Vb_d = [Buf(f"Vd{n}") for n in range(NCHUNK)]
                O1b_d = [Buf(f"O1d{n}") for n in range(NCHUNK)]

                pes = contextlib.ExitStack()
                with pes:
                    R = {
                        "x": Ring(nc, pes, "r1x", [128, D], F32, 2), "junk": Ring(nc, pes, "r1j", [128, D], F32, 1),
                        "stat": Ring(nc, pes, "r1s", [128, 4], F32, 3), "xhat": Ring(nc, pes, "r1xh", [128, D], BF16, 3),
                        "hT": Ring(nc, pes, "r1hT", [128, D], BF16, 2), "cs": Ring(nc, pes, "r1cs", [128, 256], F32, 2),
                        "qkf": Ring(nc, pes, "r1qkf", [128, 1024], F32, 2), "rt": Ring(nc, pes, "r1rt", [128, 2048], F32, 2),
                        "Qr": Ring(nc, pes, "r1Qr", [128, D], BF16, 2), "Kr": Ring(nc, pes, "r1Kr", [128, D], BF16, 2),
                        "QT": Ring(nc, pes, "r1QT", [128, D], BF16, 2), "KT": Ring(nc, pes, "r1KT", [128, D], BF16, 2),
                        "V": Ring(nc, pes, "r1V", [128, 2048], BF16, 2), "P": Ring(nc, pes, "r1P", [128, 512], BF16, 2),
                        "Kdec": Ring(nc, pes, "r1Kd", [128, D], BF16, 2), "o1": Ring(nc, pes, "r1o1", [128, 2048], F32, 2),
                    }
                    for cg in range(8):
                        dma_pool(wslot(cg).rearrange("p (k c) -> p k c", k=8),
                                 ret_w_in[j][:, cg * 512:(cg + 1) * 512].rearrange("(k p) c -> p k c", p=128), [], [Wb[cg]])
                    zero_state()
                    def stageA0a(n):
                        c = {"n": n}
                        row0, v = chunk_rows(n)
                        _, _, c["xh"], c["xhb"] = norm_a(R, src_ap, src_buf, row0)
                        return c

                    def stageA0b(c):
                        row0, v = chunk_rows(c["n"])
                        hT, hTb = R["hT"].next()
                        norm_b(c["xh"], c["xhb"], v, lambda k, hT=hT: hT[:, k * 128:(k + 1) * 128], hTb)
                        c.update(hT=hT, hTb=hTb)
                        return c

                    def stageA1(c, inject=None):
                        n = c["n"]
                        row0, v = chunk_rows(n)
                        hT, hTb = c["hT"], c["hTb"]
                        cs, csb = R["cs"].next()
                        dma_sp(cs[:, 0:128], ropec[row0:row0 + 128, :], [], [csb])
                        dma_sp(cs[:, 128:256], ropes[row0:row0 + 128, :], [], [csb])
                        cosb = cs[:, 0:128].unsqueeze(1).to_broadcast([128, 4, 128])
                        sinb = cs[:, 128:256].unsqueeze(1).to_broadcast([128, 4, 128])
                        Qr, Qrb = R["Qr"].next()
                        Kr, Krb = R["Kr"].next()
                        V, Vb = R["V"].next()
                        qf = None
                        inj = None
                        for ci, cg in enumerate((2, 3, 0, 1, 4, 5, 6, 7)):
                            pp, ppb = ps_proj[ci % 2], ps_proj_b[ci % 2]
                            wv = wslot(cg).rearrange("p (k c) -> p k c", k=8)
                            mm_group([(pp[:], hT[:, k * 128:(k + 1) * 128], wv[:, k, :], k == 0, k == 7) for k in range(8)],
                                     [hTb, Wb[cg]], [ppb])
                            if cg < 4:
                                if cg % 2 == 0:
                                    qf, qfb = R["qkf"].next()
                                copy("act", qf[:, (cg % 2) * 512:(cg % 2 + 1) * 512], pp[:], [ppb], [qfb])
                                if cg % 2 == 1:
                                    dst, dstb = (Qr, Qrb) if cg < 2 else (Kr, Krb)
                                    eng = "dve" if cg < 2 else "pool"
                                    rt, rtb = R["rt"].next()
                                    q4 = qf[:].rearrange("p (h e j) -> p h e j", h=4, e=2)
                                    te, to = q4[:, :, 0, :], q4[:, :, 1, :]
                                    r4 = rt[:].rearrange("p (a h j) -> p a h j", a=4, h=4)
                                    d4 = dst[:].rearrange("p (h e j) -> p h e j", h=4, e=2)
                                    tt(eng, r4[:, 0], te, cosb, ALU.mult, [qfb, csb], [rtb])
                                    tt(eng, r4[:, 1], to, sinb, ALU.mult, [qfb, csb], [rtb])
                                    tt(eng, r4[:, 2], te, sinb, ALU.mult, [qfb, csb], [rtb])
                                    tt(eng, r4[:, 3], to, cosb, ALU.mult, [qfb, csb], [rtb])
                                    tt(eng, d4[:, :, 0, :], r4[:, 0], r4[:, 1], ALU.subtract, [rtb], [dstb])
                                    tt(eng, d4[:, :, 1, :], r4[:, 2], r4[:, 3], ALU.add, [rtb], [dstb])
                            else:
                                copy("act", V[:, (cg - 4) * 512:(cg - 3) * 512], pp[:], [ppb], [Vb])
                            if ci == 3 and inject is not None:
                                inj = stageA0a(inject)
                        QT, QTb = R["QT"].next()
                        KT, KTb = R["KT"].next()
                        tr_group([(ps_tr[:, k * 128:(k + 1) * 128], Qr[:, k * 128:(k + 1) * 128]) for k in range(8)], [Qrb], [ps_tr_b])
                        copy("dve", QT[:], ps_tr[:], [ps_tr_b], [QTb])
                        tr_group([(ps_tr[:, k * 128:(k + 1) * 128], Kr[:, k * 128:(k + 1) * 128]) for k in range(8)], [Krb], [ps_tr_b])
                        copy("act", KT[:], ps_tr[:], [ps_tr_b], [KTb])
                        c.update(QT=QT, QTb=QTb, KT=KT, KTb=KTb, Kr=Kr, Krb=Krb, V=V, Vb=Vb)
                        if inj is not None:
                            inj = stageA0b(inj)
                        return c, inj

                    def stageB1(c):
                        n = c["n"]
                        o1, o1b = R["o1"].next()
                        scan_part(R, 0, n, c["QT"], c["QTb"], c["KT"], c["KTb"], c["Kr"], c["Krb"], c["V"], c["Vb"], o1, o1b, None, None, True)
                        dma_sp(QTd[n], c["QT"][:], [c["QTb"]], [Qb_d[n]])
                        dma_act(KTd[n], c["KT"][:], [c["KTb"]], [Kb_d[n]])
                        dma_pool(Krd[n], c["Kr"][:], [c["Krb"]], [Krb_d[n]])
                        dma_act(Vd[n], c["V"][:], [c["Vb"]], [Vb_d[n]])
                        dma_act(O1d[n], o1[:], [o1b], [O1b_d[n]])

                    c0 = stageA0b(stageA0a(0))
                    prev, nxt0 = stageA1(c0, inject=1 if NCHUNK > 1 else None)
                    for n in range(NCHUNK):
                        if n + 1 < NCHUNK:
                            nxt, nxt0 = stageA1(nxt0, inject=(n + 2) if n + 2 < NCHUNK else None)
                        else:
                            nxt = None
                        stageB1(prev)
                        prev = nxt
                    S.barrier()
                pes = contextlib.ExitStack()
                with pes:
                    R = {
                        "x": Ring(nc, pes, "r2x", [128, D], F32, 4), "junk": Ring(nc, pes, "r2j", [128, D], F32, 1),
                        "stat": Ring(nc, pes, "r2s", [128, 4], F32, 2), "xhat": Ring(nc, pes, "r2xh", [128, D], BF16, 2),
                        "hT": Ring(nc, pes, "r2hT", [128, D], BF16, 1),
                        "QT": Ring(nc, pes, "r2QT", [128, D], BF16, 2), "KT": Ring(nc, pes, "r2KT", [128, D], BF16, 2),
                        "Kr": Ring(nc, pes, "r2Kr", [128, D], BF16, 2),
                        "V": Ring(nc, pes, "r2V", [128, 2048], BF16, 2), "P": Ring(nc, pes, "r2P", [128, 512], BF16, 2),
                        "Kdec": Ring(nc, pes, "r2Kd", [128, D], BF16, 2), "o1": Ring(nc, pes, "r2o1", [128, 2048], F32, 2),
                        "SG": Ring(nc, pes, "r2SG", [128, 2048], BF16, 2), "Z": Ring(nc, pes, "r2Z", [128, 2048], BF16, 2),
                        "ZT": Ring(nc, pes, "r2ZT", [128, 2048], BF16, 1),
                        "xn": Ring(nc, pes, "r2xn", [128, D], F32, 1),
                        "bn": Ring(nc, pes, "r2bn", [128, 48], F32, 2),
                    }
                    R["xg"] = Ring(nc, pes, "r2xg", [128, 1024], F32, 2)
                    R["wst"] = R["x"]
                    R["tmp"] = R["junk"]
                    R["stat"] = Ring(nc, pes, "r2s2", [128, 4], F32, 6)
                    if PAIR:
                        ex_cout, ex_cob = exchange_start(j, state[:], state_hb, 4096)
                    for cg in range(4):
                        dma_pool(wslot(cg).rearrange("p (k c) -> p k c", k=8),
                                 ret_w_in[j][:, (12 - 4 + cg) * 512:(12 - 3 + cg) * 512].rearrange("(k p) c -> p k c", p=128), [], [Wb[cg]])
                    Wo = W[:, 4 * 4096:8 * 4096].rearrange("p (k c) -> p k c", k=16)
                    for k in range(16):
                        wt_, wtb = R["wst"].next()
                        dma_sp(wt_[:], ret_w_out[j][k * 128:(k + 1) * 128, :], [], [wtb])
                        ts("dve", Wo[:, k, :], wt_[:], gn_t[:, j * 16 + k:j * 16 + k + 1], None, ALU.mult, None, [wtb, small_b], [Wb[4 + k // 4]])
                    zero_state()
                    order = list(range(NCC - 1, -1, -1)) + list(range(NCHUNK - 1, NCC - 1, -1))

                    def mk2(n):
                        c = {"n": n}
                        c["row0"], c["v"] = chunk_rows(n)
                        c["skip"] = last and c["v"] == 1 and not dbg
                        return c

                    def loads2(c):
                        n = c["n"]
                        c["Kr"], c["Krb"] = R["Kr"].next()
                        c["V"], c["Vb"] = R["V"].next()
                        dma_sp(c["Kr"][:], Krd[n], [Krb_d[n]], [c["Krb"]])
                        dma_sp(c["V"][:], Vd[n], [Vb_d[n]], [c["Vb"]])
                        if not c["skip"]:
                            c["QT"], c["QTb"] = R["QT"].next()
                            c["KT"], c["KTb"] = R["KT"].next()
                            dma_sp(c["QT"][:], QTd[n], [Qb_d[n]], [c["QTb"]])
                            dma_sp(c["KT"][:], KTd[n], [Kb_d[n]], [c["KTb"]])
                        return c

                    def stageB2(c):
                        n = c["n"]
                        if n == NCHUNK - 1 and PAIR:
                            exchange_finish(R, ex_cout, ex_cob, state, state_hb, 4096)
                            copy("pool", state_bf[:], state[:], state_hb, statebf_hb)
                        if c["skip"]:
                            scan_part(R, 1, n, None, None, None, None, c["Kr"], c["Krb"], c["V"], c["Vb"], None, None, None, None, False)
                            return
                        o1, o1b = c["o1"], c["o1b"]
                        scan_part(R, 1, n, c["QT"], c["QTb"], c["KT"], c["KTb"], c["Kr"], c["Krb"], c["V"], c["Vb"], o1, o1b, o1, o1b, True)

                    def c2a_norm(c):
                        if c["skip"]:
                            return
                        c["xt"], c["xb"], c["xh"], c["xhb"] = norm_a(R, src_ap, src_buf, c["row0"])

                    def c2a_pe(c):
                        if c["skip"]:
                            return
                        hT, hTb = R["hT"].next()
                        norm_b(c["xh"], c["xhb"], c["v"], lambda k, hT=hT: hT[:, k * 128:(k + 1) * 128], hTb)
                        SG, SGb = R["SG"].next()
                        for cg in range(4):
                            pp, ppb = ps_proj[cg % 2], ps_proj_b[cg % 2]
                            wv = wslot(cg).rearrange("p (k c) -> p k c", k=8)
                            mm_group([(pp[:], hT[:, k * 128:(k + 1) * 128], wv[:, k, :], k == 0, k == 7) for k in range(8)],
                                     [hTb, Wb[cg]], [ppb])
                            act(SG[:, cg * 512:(cg + 1) * 512], pp[:], AF.Silu, [ppb], [SGb])
                        c["SG"], c["SGb"] = SG, SGb

                    def c2b(c):
                        if c["skip"]:
                            return
                        o1, o1b = c["o1"], c["o1b"]
                        bn, bnb = R["bn"].next()
                        for h in range(4):
                            S.add("dve", "c", [lambda e, h=h, bn=bn, o1=o1: e.bn_stats(out=bn[:, h * 6:(h + 1) * 6], in_=o1[:, h * 512:(h + 1) * 512])], [o1b], [bnb])
                        for h in range(4):
                            S.add("dve", "c", [lambda e, h=h, bn=bn: e.bn_aggr(out=bn[:, 24 + h * 2:24 + h * 2 + 2], in_=bn[:, h * 6:(h + 1) * 6])], [bnb], [bnb])
                        mv = bn[:, 24:32].rearrange("p (h t) -> p h t", t=2)
                        act(bn[:, 32:36], mv[:, :, 1], AF.Sqrt, [bnb, cst_b], [bnb], scale=1.0, bias=eps_col)
                        S.add("dve", "c", [lambda e, bn=bn: e.reciprocal(out=bn[:, 36:40], in_=bn[:, 32:36])], [bnb], [bnb])
                        for h in range(4):
                            ts("dve", o1[:, h * 512:(h + 1) * 512], o1[:, h * 512:(h + 1) * 512], bn[:, 24 + h * 2:24 + h * 2 + 1],
                               bn[:, 36 + h:37 + h], ALU.subtract, ALU.mult, [o1b, bnb], [o1b])
                        Z, Zb = R["Z"].next()
                        tt("pool", Z[:], o1[:], c["SG"][:], ALU.mult, [o1b, c["SGb"]], [Zb])
                        c["Z"], c["Zb"] = Z, Zb

                    def c2c(c):
                        if c["skip"]:
                            return
                        n, row0, v = c["n"], c["row0"], c["v"]
                        Z, Zb = c["Z"], c["Zb"]
                        ZT, ZTb = R["ZT"].next()
                        for half in range(2):
                            tr_group([(ps_tr[:, k * 128:(k + 1) * 128], Z[:, (half * 8 + k) * 128:(half * 8 + k + 1) * 128]) for k in range(8)],
                                     [Zb], [ps_tr_b])
                            copy("act" if half == 0 else "dve", ZT[:, half * 1024:(half + 1) * 1024], ps_tr[:], [ps_tr_b], [ZTb])
                        for g in range(2):
                            mm_group([(ps_proj[g][:], ZT[:, k * 128:(k + 1) * 128], Wo[:, k, g * 512:(g + 1) * 512], k == 0, k == 15) for k in range(16)],
                                     [ZTb] + Wb[4:8], [ps_proj_b[g]])
                        xt, xb = c["xt"], c["xb"]
                        if last and v == 1:
                            post_chunk(R, xt, xb, v, dbg_ctx, dst_buf, row0)
                        elif last:
                            post_chunk(R, xt, xb, v, out, dst_buf, row0 - CTX)
                        else:
                            post_chunk(R, xt, xb, v, dst_ap, dst_buf, row0)

                    NO = len(order)
                    cs2 = {0: loads2(mk2(order[0]))}
                    c2a_norm(cs2[0])
                    c2a_pe(cs2[0])
                    if NO > 1:
                        cs2[1] = mk2(order[1])
                        c2a_norm(cs2[1])
                    for i, n in enumerate(order):
                        c = cs2[i]
                        if not c["skip"]:
                            c["o1"], c["o1b"] = R["o1"].next()
                            dma_sp(c["o1"][:], O1d[n], [O1b_d[n]], [c["o1b"]])
                        if i + 2 < NO:
                            cs2[i + 2] = mk2(order[i + 2])
                        if i + 1 < NO:
                            loads2(cs2[i + 1])
                            c2a_pe(cs2[i + 1])
                        stageB2(c)
                        if i >= 1:
                            c2c(cs2[i - 1])
                            del cs2[i - 1]
                        c2b(c)
                        if i + 2 < NO:
                            c2a_norm(cs2[i + 2])
                    c2c(cs2[NO - 1])
                    S.barrier()

        def lru_layer(l, src_ap, src_buf, dst_ap, dst_buf, last):
            j = l // 2
            tiles = [("c", 0, CTX)] + [("l", i * 512, 512) for i in range(NL // 512)]
            XR = {"c": XRc, "l": XRl}
            XC = {"c": XCc, "l": XCl}
            H1 = {"c": H1c, "l": H1l}
            SGD = {"c": SGc, "l": SGl}
            les = contextlib.ExitStack()
            with les:
                lsb = lambda n, s, d=F32: les.enter_context(nc.sbuf_tensor(f"{n}_L{l}", s, d))
                prep_layer(l, les)
                cl = lsb("cl", [128, 20])
                sp_t = lsb("sp_t", [128, 120])
                cl_b = Buf("cl")
                hprev = lsb("hprev", [128, 2 * NCH])
                hprev_b = Buf("hprev")
                lam_j = lam_t[:, j * 20:(j + 1) * 20]
                A0 = lambda i: sp_t[:, i * 20:(i + 1) * 20]
                ts("dve", A0(1), lam_j, -1.0, None, ALU.mult, None, [small_b], [cl_b])
                tt("dve", A0(0), lam_j, A0(1), ALU.min, [small_b, cl_b], [cl_b])
                act(A0(1), A0(0), AF.Exp, [cl_b], [cl_b])
                ts("dve", A0(2), A0(1), 2.0, None, ALU.add, None, [cl_b], [cl_b])
                S.add("dve", "c", [lambda e: e.reciprocal(out=A0(3), in_=A0(2))], [cl_b], [cl_b])
                tt("dve", A0(2), A0(1), A0(3), ALU.mult, [cl_b], [cl_b])
                tt("dve", A0(3), A0(2), A0(2), ALU.mult, [cl_b], [cl_b])
                ts("dve", A0(4), A0(3), 1.0 / 17.0, 1.0 / 15.0, ALU.mult, ALU.add, [cl_b], [cl_b])
                for cden in (13.0, 11.0, 9.0, 7.0, 5.0, 3.0, 1.0):
                    tt("dve", A0(4), A0(4), A0(3), ALU.mult, [cl_b], [cl_b])
                    ts("dve", A0(4), A0(4), 1.0 / cden, None, ALU.add, None, [cl_b], [cl_b])
                tt("dve", A0(4), A0(4), A0(2), ALU.mult, [cl_b], [cl_b])
                ts("dve", A0(5), lam_j, -1.0, 0.0, ALU.mult, ALU.max, [small_b, cl_b], [cl_b])
                stt(A0(5), A0(4), 2.0, A0(5), ALU.mult, ALU.add, [cl_b], [cl_b])
                ts("dve", cl[:], A0(5), -8.0, None, ALU.mult, None, [cl_b], [cl_b])

                XRb = {("c", 0): Buf("xrc")}
                XCb, H1b, SGb_d = {}, {}, {}
                for (sq, t0, nt) in tiles:
                    XRb[(sq, t0)] = Buf(f"xr{sq}{t0}")
                    XCb[(sq, t0)] = Buf(f"xc{sq}{t0}")
                    H1b[(sq, t0)] = Buf(f"h1{sq}{t0}")
                    SGb_d[(sq, t0)] = Buf(f"sg{sq}{t0}")
                pad_b = {("c", "L"): Buf("padcL"), ("c", "R"): Buf("padcR"), ("l", "L"): Buf("padlL"), ("l", "R"): Buf("padlR")}
                for s in range(5):
                    dma_pool(wslot(s).rearrange("p (k c) -> p k c", k=8),
                             lru_w_in[j][:, s * 512:(s + 1) * 512].rearrange("(k p) c -> p k c", p=128), [], [Wb[s]])
                GW = W[:, 5 * 4096:5 * 4096 + 40 * 128].rearrange("p (d g c j) -> p d g c j", d=2, g=2, c=NCH)
                for d in range(2):
                    dma_pool(GW[:, d, 0], lru_wa[j, d].rearrange("c i j -> i c j"), [], [Wb[5], Wb[6]])
                    dma_pool(GW[:, d, 1], lru_wx[j, d].rearrange("c i j -> i c j"), [], [Wb[5], Wb[6]])
                z3 = zeros_f[:, 0:20].rearrange("p (c t) -> p c t", t=2)
                for sq, ln in (("c", CTX), ("l", NL)):
                    dma_pool(XR[sq][:, :, 0:2].rearrange("c p t -> p c t"), z3, [cst_b], [pad_b[(sq, "L")]])
                    if sq == "c" or not PAIR:
                        dma_pool(XR[sq][:, :, ln + 2:ln + 4].rearrange("c p t -> p c t"), z3, [cst_b], [pad_b[(sq, "R")]])

                GS = 5
                hp1_cb = [Buf(f"hp1_{c}") for c in range(NCH)]
                hp2_cb = [Buf(f"hp2_{c}") for c in range(NCH)]

                def gates_p1(R, d, cc, nt, xc, xcb):
                    xb16, xb16b = R["xcb"].next()
                    copy("act", xb16[:, 0:nt], xc[:, 0:nt], [xcb], [xb16b])
                    pr, prb = ps_o[cc % 2], ps_o_b[cc % 2]
                    pg, pgb = ps_su[cc % 2], ps_su_b[cc % 2]
                    mm_group([(pr[:, 0:nt], GW[:, d, 0, cc, :], xb16[:, 0:nt], True, True)], [xb16b, Wb[5], Wb[6]], [prb])
                    mm_group([(pg[:, 0:nt], GW[:, d, 1, cc, :], xb16[:, 0:nt], True, True)], [xb16b, Wb[5], Wb[6]], [pgb])
                    r_, rb = R["r"].next()
                    gi, gib = R["gi"].next()
                    bcol = (j * 2 + d) * NCH + cc
                    act(r_[:, 0:nt], pr[:, 0:nt], AF.Sigmoid, [prb, small_b], [rb], bias=ba_t[:, bcol:bcol + 1])
                    act(gi[:, 0:nt], pg[:, 0:nt], AF.Sigmoid, [pgb, small_b], [gib], bias=bx_t[:, bcol:bcol + 1])
                    return dict(cc=cc, xc=xc, xcb=xcb, r=r_, rb=rb, gi=gi, gib=gib)

                def gates_rest(R, d, nt, cx):
                    for c_ in cx:
                        c_["a"], c_["ab"] = R["a"].next()
                        cc = c_["cc"]
                        act(c_["a"][:, 0:nt], c_["r"][:, 0:nt], AF.Exp, [c_["rb"], cl_b], [c_["ab"]], scale=cl[:, d * NCH + cc:d * NCH + cc + 1])
                    for c_ in cx:
                        c_["q"], c_["qb"] = R["q"].next()
                        tt("pool", c_["q"][:, 0:nt], c_["a"][:, 0:nt], c_["a"][:, 0:nt], ALU.mult, [c_["ab"]], [c_["qb"]])
                    for c_ in cx:
                        act(c_["q"][:, 0:nt], c_["q"][:, 0:nt], AF.Sqrt, [c_["qb"], cst_b], [c_["qb"]], scale=-1.0, bias=one_col)
                    for c_ in cx:
                        c_["u"], c_["ub"] = R["u"].next()
                        tt("pool", c_["u"][:, 0:nt], c_["q"][:, 0:nt], c_["gi"][:, 0:nt], ALU.mult, [c_["qb"], c_["gib"]], [c_["ub"]])
                        tt("dve", c_["u"][:, 0:nt], c_["u"][:, 0:nt], c_["xc"][:, 0:nt], ALU.mult, [c_["ub"], c_["xcb"]], [c_["ub"]])


                pes = contextlib.ExitStack()
                with pes:
                    R0 = {
                        "x": Ring(nc, pes, "l0x", [128, D], F32, 2), "junk": Ring(nc, pes, "l0j", [128, D], F32, 1),
                        "stat": Ring(nc, pes, "l0s", [128, 4], F32, 2), "xhat": Ring(nc, pes, "l0xh", [128, D], BF16, 2),
                        "hT": Ring(nc, pes, "l0hT", [128, 8 * 512], BF16, 2), "xr": Ring(nc, pes, "l0xr", [128, 512], F32, 2),
                        "sg": Ring(nc, pes, "l0sg", [128, 512], BF16, 2),
                        "halo": Ring(nc, pes, "l0halo", [128, 64], F32, 2),
                    }
                    R0["xg"] = R0["x"]

                    def L0_tile(sq, t0, nt):
                        R = R0
                        base = 0 if sq == "c" else CTX
                        v = 1 if sq == "c" else 0
                        hT, hTb = R["hT"].next()
                        hv = hT[:].rearrange("p (k t) -> p k t", k=8)
                        for c in range(nt // 128):
                            norm_chunk(R, src_ap, src_buf, base + t0 + c * 128, v, lambda k, hv=hv, c=c: hv[:, k, c * 128:(c + 1) * 128], hTb)
                        for cc in range(2 * NCH):
                            pp, ppb = ps_proj[cc % 2], ps_proj_b[cc % 2]
                            wv = wslot(cc // 4).rearrange("p (k c) -> p k c", k=8)
                            mm_group([(pp[:, 0:nt], wv[:, k, (cc % 4) * 128:(cc % 4 + 1) * 128], hv[:, k, 0:nt], k == 0, k == 7) for k in range(8)],
                                     [hTb, Wb[cc // 4]], [ppb])
                            if cc < NCH:
                                xr, xrb = R["xr"].next()
                                copy("act", xr[:, 0:nt], pp[:, 0:nt], [ppb], [xrb])
                                dma_act(XR[sq][cc, :, 2 + t0:2 + t0 + nt], xr[:, 0:nt], [xrb], [XRb[(sq, t0)]])
                            else:
                                sg, sgb = R["sg"].next()
                                act(sg[:, 0:nt], pp[:, 0:nt], AF.Silu, [ppb], [sgb])
                                dma_act(SGD[sq][cc - NCH, :, t0:t0 + nt], sg[:, 0:nt], [sgb], [SGb_d[(sq, t0)]])
                    def L0_halo():
                        R = R0
                        lastb = XRb[("l", NL - 512)]
                        hl, hlb = R["halo"].next()
                        dma_sp(hl[:, 0:20].rearrange("p (c t) -> p c t", t=2), XRl[:, :, NL:NL + 2].rearrange("c p t -> p c t"), [lastb], [hlb])
                        hr, hrb = R["halo"].next()
                        hc_, hcb_ = exchange_start(l, hl[:, 0:32], hlb, 32)
                        exchange_finish(R, hc_, hcb_, hr, hrb, 32)
                        h3 = hr[:, 0:20].rearrange("p (c t) -> p c t", t=2)
                        dma_pool(XRl[:, :, NL + 2:NL + 3].rearrange("c p t -> p c t"), h3[:, :, 1:2], [hrb], [pad_b[("l", "R")]], slow=True)
                        dma_pool(XRl[:, :, NL + 3:NL + 4].rearrange("c p t -> p c t"), h3[:, :, 0:1], [hrb], [pad_b[("l", "R")]], slow=True)

                    R = {
                        "win": Ring(nc, pes, "l1w", [128, 516], F32, 5), "xc": Ring(nc, pes, "l1xc", [128, 512], F32, 5),
                        "xcb": Ring(nc, pes, "l1xcb", [128, 512], BF16, 2), "r": Ring(nc, pes, "l1r", [128, 512], F32, 5),
                        "gi": Ring(nc, pes, "l1gi", [128, 512], F32, 5), "a": Ring(nc, pes, "l1a", [128, 512], F32, 5),
                        "q": Ring(nc, pes, "l1q", [128, 512], F32, 5), "u": Ring(nc, pes, "l1u", [128, 512], F32, 5),
                        "h": Ring(nc, pes, "l1h", [128, 512], F32, 3),
                    }
                    S.add("dve", "c", [lambda e: e.memset(hprev[:], 0.0)], [], hp1_cb + hp2_cb)
                    def L1_tile(sq, t0, nt):
                        ln = CTX if sq == "c" else NL
                        rd = [XRb[(sq, t0)]]
                        if t0 >= 512:
                            rd.append(XRb[(sq, t0 - 512)])
                        else:
                            rd.append(pad_b[(sq, "L")])
                        if t0 + nt < ln:
                            rd.append(XRb[(sq, t0 + nt)])
                        else:
                            rd.append(pad_b[(sq, "R")])
                        for g0 in range(0, NCH, GS):
                            cx = []
                            wins = {}
                            for cc in range(g0, g0 + GS):
                                win, winb = R["win"].next()
                                dma_sp(win[:, 0:nt + 4], XR[sq][cc, :, t0:t0 + nt + 4], rd, [winb])
                                wins[cc] = (win, winb)
                            for cc in range(g0, g0 + GS):
                                win, winb = wins[cc]
                                xc, xcb = R["xc"].next()
                                wc = lambda tap, cc=cc: cw_t[:, (j * NCH + cc) * 5 + tap:(j * NCH + cc) * 5 + tap + 1]
                                ts("dve", xc[:, 0:nt], win[:, 0:nt], wc(0), cb_t[:, j * NCH + cc:j * NCH + cc + 1], ALU.mult, ALU.add,
                                   [winb, small_b], [xcb])
                                for tap in range(1, 5):
                                    stt(xc[:, 0:nt], win[:, tap:tap + nt], wc(tap), xc[:, 0:nt], ALU.mult, ALU.add, [winb, small_b, xcb], [xcb])
                                dma_sp(XC[sq][cc, :, t0:t0 + nt], xc[:, 0:nt], [xcb], [XCb[(sq, t0)]])
                                cx.append(gates_p1(R, 0, cc, nt, xc, xcb))
                            gates_rest(R, 0, nt, cx)
                            for c_ in cx:
                                cc, a_, ab, u_, ub = c_["cc"], c_["a"], c_["ab"], c_["u"], c_["ub"]
                                h_, hb = R["h"].next()
                                S.add("dve", "c", [lambda e, h_=h_, a_=a_, u_=u_, cc=cc, nt=nt: e.tensor_tensor_scan(
                                    out=h_[:, 0:nt], data0=a_[:, 0:nt], data1=u_[:, 0:nt], initial=hprev[:, cc:cc + 1], op0=ALU.mult, op1=ALU.add)],
                                    [ab, ub, hp1_cb[cc]], [hb])
                                copy("act", hprev[:, cc:cc + 1], h_[:, nt - 1:nt], [hb], [hp1_cb[cc]])
                                dma_pool(H1[sq][cc, :, t0:t0 + nt], h_[:, 0:nt], [hb], [H1b[(sq, t0)]])

                    NT = len(tiles)
                    for ti in range(NT + 2):
                        if ti < NT:
                            L0_tile(*tiles[ti])
                            if ti == NT - 1 and PAIR:
                                L0_halo()
                        if 2 <= ti:
                            L1_tile(*tiles[ti - 2])
                    S.barrier()

                pes = contextlib.ExitStack()
                with pes:
                    R = {
                        "xc": Ring(nc, pes, "l2xc", [128, 512], F32, 5),
                        "xcb": Ring(nc, pes, "l2xcb", [128, 512], BF16, 2), "r": Ring(nc, pes, "l2r", [128, 512], F32, 5),
                        "gi": Ring(nc, pes, "l2gi", [128, 512], F32, 5), "a": Ring(nc, pes, "l2a", [128, 512], F32, 5),
                        "q": Ring(nc, pes, "l2q", [128, 512], F32, 5), "u": Ring(nc, pes, "l2u", [128, 512], F32, 5),
                        "h": Ring(nc, pes, "l2h", [128, 512], F32, 2), "h1": Ring(nc, pes, "l2h1", [128, 512], F32, 2),
                        "sg": Ring(nc, pes, "l2sg", [128, 512], BF16, 2), "Z": Ring(nc, pes, "l2Z", [128, NCH * 512], BF16, 1),
                        "x": Ring(nc, pes, "l2x", [128, D], F32, 2), "junk": Ring(nc, pes, "l2j", [128, D], F32, 1),
                        "stat": Ring(nc, pes, "l2s", [128, 4], F32, 2), "tmp": Ring(nc, pes, "l2tmp", [128, D], F32, 1),
                        "xn": Ring(nc, pes, "l2xn", [128, D], F32, 2),
                    }
                    R["xg"] = R["xn"]
                    Wo = W[:, 0:NCH * 1024].rearrange("p (k c) -> p k c", k=NCH)
                    dma_pool(Wo, lru_w_out[j].rearrange("(k p) c -> p k c", p=128), [], [Wb[0], Wb[1], Wb[2]])
                    hp2 = hprev[:, NCH:2 * NCH]
                    if PAIR:
                        pst_ = pes.enter_context(nc.sbuf_tensor(f"lpstate{l}", [128, 32], F32))
                        pst_b = Buf("lpstate")
                        hx, hxb = R["xg"].next()
                        copy("dve", hx[:, 0:NCH], hprev[:, 0:NCH], hp1_cb, [hxb])
                        sc_, scb_ = exchange_start(l - 1, hx[:, 0:32], hxb, 32)
                        exchange_finish(R, sc_, scb_, pst_, pst_b, 32)
                    order = [tiles[0]] + tiles[:0:-1]
                    for (sq, t0, nt) in order:
                        base = 0 if sq == "c" else CTX
                        v = 1 if sq == "c" else 0
                        if PAIR and sq == "l" and t0 == NL - 512:
                            copy("dve", hp2, pst_[:, 0:NCH], [pst_b] + hp2_cb, hp2_cb)
                        skip_out = last and sq == "c" and not dbg
                        Z, Zb = R["Z"].next()
                        zv = Z[:].rearrange("p (c t) -> p c t", c=NCH)
                        for g0 in range(0, NCH, GS):
                            cx = []
                            for cc in range(g0, g0 + GS):
                                xc, xcb = R["xc"].next()
                                dma_sp(xc[:, 0:nt], XC[sq][cc, :, t0:t0 + nt], [XCb[(sq, t0)]], [xcb])
                                cx.append(gates_p1(R, 1, cc, nt, xc, xcb))
                            gates_rest(R, 1, nt, cx)
                            for c_ in cx:
                                cc, a_, ab, u_, ub = c_["cc"], c_["a"], c_["ab"], c_["u"], c_["ub"]
                                h_, hb = R["h"].next()
                                S.add("dve", "c", [lambda e, h_=h_, a_=a_, u_=u_, cc=cc, nt=nt: e.tensor_tensor_scan(
                                    out=h_[:, 0:nt][:, ::-1], data0=a_[:, 0:nt][:, ::-1], data1=u_[:, 0:nt][:, ::-1],
                                    initial=hp2[:, cc:cc + 1], op0=ALU.mult, op1=ALU.add)], [ab, ub, hp2_cb[cc]], [hb])
                                copy("act", hp2[:, cc:cc + 1], h_[:, 0:1], [hb], [hp2_cb[cc]])
                                if skip_out:
                                    continue
                                h1, h1b = R["h1"].next()
                                dma_sp(h1[:, 0:nt], H1[sq][cc, :, t0:t0 + nt], [H1b[(sq, t0)]], [h1b])
                                sg, sgb = R["sg"].next()
                                dma_sp(sg[:, 0:nt], SGD[sq][cc, :, t0:t0 + nt], [SGb_d[(sq, t0)]], [sgb])
                                tt("pool", h1[:, 0:nt], h1[:, 0:nt], h_[:, 0:nt], ALU.add, [h1b, hb], [h1b])
                                tt("dve", zv[:, cc, 0:nt], h1[:, 0:nt], sg[:, 0:nt], ALU.mult, [h1b, sgb], [Zb])
                        if skip_out:
                            continue
                        for c in range(nt // 128):
                            row0 = base + t0 + c * 128
                            xt, xb = R["x"].next()
                            dma_sp(xt[:], src_ap[row0:row0 + 128, :], [src_buf], [xb])
                            for g in range(2):
                                mm_group([(ps_proj[g][:], zv[:, cc, c * 128:(c + 1) * 128], Wo[:, cc, g * 512:(g + 1) * 512], cc == 0, cc == NCH - 1)
                                          for cc in range(NCH)], [Zb, Wb[0], Wb[1], Wb[2]], [ps_proj_b[g]])
                            if last and v == 1:
                                post_chunk(R, xt, xb, v, dbg_ctx, dst_buf, row0)
                            elif last:
                                post_chunk(R, xt, xb, v, out, dst_buf, row0 - CTX)
                            else:
                                post_chunk(R, xt, xb, v, dst_ap, dst_buf, row0)
                    S.barrier()

        xin_b = Buf("xin")
        Xb = [Buf("Xs0"), Buf("Xs1")]
        out_b = Buf("out")
        src_ap, src_buf = xin, xin_b
        for l in range(depth):
            last = l == depth - 1
            dst_ap, dst_buf = (Xs[l % 2], Xb[l % 2])
            if last:
                dst_buf = out_b
            if l % 2 == 0:
                retention_layer(l, src_ap, src_buf, dst_ap, dst_buf, last)
            else:
                lru_layer(l, src_ap, src_buf, dst_ap, dst_buf, last)
            src_ap, src_buf = dst_ap, dst_buf
            if not last:
                S.rotate()
        S.final_wait("sp")

        block = es.enter_context(nc.Block())

        @block.tensor
        def _(e):
            S.emit("pe", e)

        @block.scalar
        def _(e):
            S.emit("act", e)

        @block.vector
        def _(e):
            S.emit("dve", e)

        @block.gpsimd
        def _(e):
            S.emit("pool", e)

        @block.sync
        def _(e):
            S.emit("sp", e)
    return nc


def _col(vec, nk):
    return np.ascontiguousarray(np.asarray(vec, np.float32).reshape(nk, 128).T)


def _rope_tables():
    n_rows = SEQ // GRID_W
    row = np.repeat(np.arange(n_rows, dtype=np.float32), GRID_W)
    col = np.tile(np.arange(GRID_W, dtype=np.float32), n_rows)
    n_freq = 64
    inv = (np.float32(10000.0) ** (-np.arange(n_freq, dtype=np.float32) / np.float32(n_freq))).astype(np.float32)
    ang = np.concatenate([row[:, None] * inv, col[:, None] * inv], axis=-1).astype(np.float32)
    return np.cos(ang).astype(np.float32), np.sin(ang).astype(np.float32)


def _core_inputs(core, inp, shared):
    if PAIR:
        b, half = core // 2, core % 2
    else:
        b, half = core, 0
    flip = half == 1
    dirs = (1, 0) if flip else (0, 1)
    x = inp["x"][b]
    ctx = inp["ctx"][b]
    cos, sin = shared["rope"]
    if PAIR:
        xs = x[half * NL:(half + 1) * NL]
        cs, sn = cos[half * NL:(half + 1) * NL], sin[half * NL:(half + 1) * NL]
    else:
        xs, cs, sn = x, cos, sin
    if flip:
        xs, cs, sn, ctx = xs[::-1], cs[::-1], sn[::-1], ctx[::-1]
    m = {}
    m["xin"] = np.ascontiguousarray(np.concatenate([ctx, xs], 0), dtype=np.float32)
    m["ropec"] = np.ascontiguousarray(np.concatenate([np.ones((CTX, 128), np.float32), cs], 0))
    m["ropes"] = np.ascontiguousarray(np.concatenate([np.zeros((CTX, 128), np.float32), sn], 0))
    cc = np.stack([_col(inp["c"][b], 8), _col(inp["c_ctx"], 8)], -1).reshape(128, 16)
    m["c_col"] = np.ascontiguousarray(cc)
    lg = np.asarray(inp["ret_log_decay"], np.float32)[:, list(dirs), :]
    m["ret_lg"] = np.ascontiguousarray(np.broadcast_to(lg.reshape(1, 16), (128, 16)))
    sel_d = list(dirs)
    m["lru_wa"] = np.ascontiguousarray(np.asarray(inp["lru_w_a"], np.float32)[:, sel_d])
    m["lru_wx"] = np.ascontiguousarray(np.asarray(inp["lru_w_x"], np.float32)[:, sel_d])
    def dcol(a):
        a = np.asarray(a, np.float32)[:, sel_d]
        return np.ascontiguousarray(np.concatenate([_col(a[j, d], NCH) for j in range(2) for d in range(2)], 1))
    m["lru_ba"] = dcol(inp["lru_b_a"])
    m["lru_bx"] = dcol(inp["lru_b_x"])
    m["lru_lam"] = dcol(inp["lru_lambda"])
    cw = np.asarray(inp["lru_conv_w"], np.float32)
    z = np.zeros_like(cw[:, :1])
    cw5 = np.concatenate([z, cw[:, ::-1]], 1) if flip else np.concatenate([cw, z], 1)
    cwc = np.stack([np.stack([_col(cw5[j, t], NCH) for t in range(5)], -1) for j in range(2)], 1)
    m["lru_cw"] = np.ascontiguousarray(cwc.reshape(128, 100))
    cst = shared["consts"].copy()
    if PAIR:
        cst[:, 649] = 1.0 if half == 1 else 0.0
        cst[:, 650] = 1.0 if half == 0 else 0.0
    m["consts"] = cst
    for k in ("mod_w", "mod_b_col", "npre_col", "npost_col", "ret_w_in", "ret_gn_col", "ret_w_out", "lru_w_in", "lru_cb", "lru_w_out"):
        m[k] = shared[k]
    return m


def _shared_inputs(inp):
    sh = {}
    sh["rope"] = _rope_tables()
    sh["mod_w"] = np.ascontiguousarray(inp["mod_w"], dtype=np.float32)
    sh["mod_b_col"] = np.ascontiguousarray(np.concatenate([_col(inp["mod_b"][l], 24) for l in range(DEPTH)], 1))
    sh["npre_col"] = np.ascontiguousarray(np.concatenate([_col(inp["norm_pre"][l], 8) for l in range(DEPTH)], 1))
    sh["npost_col"] = np.ascontiguousarray(np.concatenate([_col(inp["norm_post"][l], 8) for l in range(DEPTH)], 1))
    perm = np.arange(6144)
    for blk in range(2):
        for h in range(RET_H):
            base = blk * 1024 + h * 256
            perm[base:base + 256] = np.concatenate([base + np.arange(0, 256, 2), base + np.arange(1, 256, 2)])
    sh["ret_w_in"] = np.ascontiguousarray(np.asarray(inp["ret_w_in"], np.float32)[:, :, perm])
    sh["ret_gn_col"] = np.ascontiguousarray(np.concatenate([_col(inp["ret_gn"][j], 16) for j in range(2)], 1))
    sh["ret_w_out"] = np.ascontiguousarray(inp["ret_w_out"], dtype=np.float32)
    sh["lru_w_in"] = np.ascontiguousarray(inp["lru_w_in"], dtype=np.float32)
    sh["lru_cb"] = np.ascontiguousarray(np.concatenate([_col(inp["lru_conv_b"][j], NCH) for j in range(2)], 1))
    sh["lru_w_out"] = np.ascontiguousarray(inp["lru_w_out"], dtype=np.float32)
    cst = np.zeros((128, 128 * 5 + 16), np.float32)
    p = np.arange(128, dtype=np.float32)
    cst[:, 0:128] = np.eye(128, dtype=np.float32)
    cst[:, 128:256] = 1.0
    cst[:, 256:384] = (p[:, None] <= p[None, :])
    cst[:, 384:512] = (p[:, None] >= p[None, :])
    coefs = np.stack([127 - p, p + 1, -(p + 1), p, 128 - p, p - 128, np.full(128, 128.0, np.float32)], 1)
    cst[:, 640:647] = coefs
    cst[:, 647] = EPS
    cst[:, 648] = 1.0
    sh["consts"] = cst
    return sh


_NC_CACHE = {}


def kernel(**inputs):
    inp = {k: np.asarray(v) for k, v in inputs.items()}
    if "nc" not in _NC_CACHE:
        _NC_CACHE["nc"] = build_program()
    nc = _NC_CACHE["nc"]
    shared = _shared_inputs(inp)
    in_maps = [_core_inputs(c, inp, shared) for c in range(NCORES)]
    res = run_bass_kernel_spmd(nc, in_maps, core_ids=list(range(NCORES)))
    outp = np.empty((BATCH, SEQ, D), np.float32)
    for c in range(NCORES):
        o = np.asarray(res.results[c]["out"], np.float32)
        if PAIR:
            b, half = c // 2, c % 2
            outp[b, half * NL:(half + 1) * NL] = o[::-1] if half == 1 else o
        else:
            outp[c] = o
    return outp
```

```python
import contextlib
import numpy as np
import concourse.bass as bass
import concourse.mybir as mybir
from concourse.bass_utils import run_bass_kernel_spmd

F32 = mybir.dt.float32
BF16 = mybir.dt.bfloat16
AF = mybir.ActivationFunctionType
ALU = mybir.AluOpType
AX = mybir.AxisListType

D = 1024
DEPTH = 4
SEQ = 8192
BATCH = 4
CTX = 256
GRID_W = 64
EPS = 1e-6
RET_H = 4
LRU_W = 1280
NCH = 10
PAIR = True
NCORES = 8 if PAIR else 4
NL = SEQ // 2 if PAIR else SEQ
NTOK = CTX + NL
NCHUNK = NTOK // 128
NCC = CTX // 128


class Sem:
    def __init__(self, handle, stream, cls, step):
        self.h = handle
        self.stream = stream
        self.cls = cls
        self.step = step
        self.count = 0


class Buf:
    __slots__ = ("name", "w", "r")

    def __init__(self, name=""):
        self.name = name
        self.w = None
        self.r = {}


class Stream:
    def __init__(self, name):
        self.name = name
        self.items = []
        self.waited = {}


class Sched:
    NDMA = 14

    def __init__(self, nc, es):
        self.nc = nc
        self.es = es
        self.streams = {n: Stream(n) for n in ("pe", "act", "dve", "pool", "sp")}
        self.sems = {}
        self.all_sems = []
        self.nsem = 0
        self.dma_sems = {}
        self.dma_rr = {}
        for st in ("pool", "sp", "act"):
            self.dma_sems[st] = [self._new_sem(st, "d", 16) for _ in range(self.NDMA)]
            self.dma_rr[st] = 0
        self.rotate()

    def _new_sem(self, st, cls, step):
        h = self.es.enter_context(self.nc.semaphore(f"s{self.nsem}_{st}_{cls}"))
        self.nsem += 1
        sm = Sem(h, st, cls, step)
        self.all_sems.append(sm)
        return sm

    def rotate(self):
        for st in ("pe", "act", "dve", "pool"):
            self.sems[(st, "c")] = self._new_sem(st, "c", 1)

    def add(self, stream, cls, fns, reads=(), writes=()):
        st = self.streams[stream]
        need = {}
        if cls == "d":
            i = self.dma_rr[stream]
            self.dma_rr[stream] = (i + 1) % self.NDMA
            sm = self.dma_sems[stream][i]
            if sm.count > 0 and st.waited.get(sm, 0) < sm.count:
                need[sm] = sm.count
        elif cls == "cc":
            sm = self._new_sem(stream, "cc", 1)
        else:
            sm = self.sems[(stream, cls)]
        raw = set()
        oth = set()
        for b in reads:
            if b.w is not None:
                raw.add(b.w)
        for b in writes:
            if b.w is not None:
                oth.add(b.w)
            for k, v in b.r.items():
                oth.add((k, v))
        for (dsm, v) in raw | oth:
            if dsm.stream == stream:
                if stream == "pe":
                    continue
                if dsm.cls == "c" and cls == "c" and (dsm, v) not in raw:
                    continue
            if st.waited.get(dsm, 0) >= v:
                continue
            if need.get(dsm, 0) < v:
                need[dsm] = v
        for k, v in need.items():
            st.waited[k] = v
        sm.count += sm.step
        tok = (sm, sm.count)
        st.items.append((list(need.items()), fns, sm))
        for b in reads:
            if b.r.get(sm, 0) < sm.count:
                b.r[sm] = sm.count
        for b in writes:
            b.w = tok
            b.r = {}
        return tok

    def barrier(self):
        for st in self.streams.values():
            need = []
            for sm in self.all_sems:
                if sm.count > 0 and st.waited.get(sm, 0) < sm.count:
                    if sm.stream == st.name and st.name == "pe":
                        continue
                    need.append((sm, sm.count))
                    st.waited[sm] = sm.count
            if need:
                st.items.append((need, [], None))

    def final_wait(self, stream="sp"):
        st = self.streams[stream]
        need = [(sm, sm.count) for sm in self.all_sems if sm.count > 0 and st.waited.get(sm, 0) < sm.count]
        st.items.append((need, [], None))

    def emit(self, stream, eng):
        for waits, fns, sm in self.streams[stream].items:
            for (wsm, v) in waits:
                eng.wait_ge(wsm.h, v)
            ins = None
            for f in fns:
                ins = f(eng)
            if ins is not None and sm is not None:
                ins.then_inc(sm.h, sm.step)


class Ring:
    _uid = [0]

    def __init__(self, nc, es, name, shape, dtype, n):
        Ring._uid[0] += 1
        name = f"{name}u{Ring._uid[0]}_"
        self.t = [es.enter_context(nc.sbuf_tensor(f"{name}{i}", shape, dtype)) for i in range(n)]
        self.b = [Buf(f"{name}{i}") for i in range(n)]
        self.i = 0

    def next(self):
        i = self.i
        self.i = (i + 1) % len(self.t)
        return self.t[i], self.b[i]


def build_program(depth=DEPTH, ncores=NCORES, dbg=False):
    nc = bass.Bass("TRN2", target_bir_lowering=False)
    dt_in = lambda n, s: nc.dram_tensor(n, s, F32, kind="ExternalInput").ap()
    xin = dt_in("xin", [NTOK, D])
    ropec = dt_in("ropec", [NTOK, 128])
    ropes = dt_in("ropes", [NTOK, 128])
    c_col = dt_in("c_col", [128, 16])
    mod_w = dt_in("mod_w", [DEPTH, D, 3 * D])
    mod_b_col = dt_in("mod_b_col", [128, DEPTH * 24])
    npre_col = dt_in("npre_col", [128, DEPTH * 8])
    npost_col = dt_in("npost_col", [128, DEPTH * 8])
    ret_w_in = dt_in("ret_w_in", [2, D, 6144])
    ret_lg = dt_in("ret_lg", [128, 16])
    ret_gn_col = dt_in("ret_gn_col", [128, 32])
    ret_w_out = dt_in("ret_w_out", [2, 2048, D])
    lru_w_in = dt_in("lru_w_in", [2, D, 2 * LRU_W])
    lru_cw = dt_in("lru_cw", [128, 2 * NCH * 5])
    lru_cb = dt_in("lru_cb", [128, 2 * NCH])
    lru_wa = dt_in("lru_wa", [2, 2, NCH, 128, 128])
    lru_wx = dt_in("lru_wx", [2, 2, NCH, 128, 128])
    lru_ba = dt_in("lru_ba", [128, 2 * 2 * NCH])
    lru_bx = dt_in("lru_bx", [128, 2 * 2 * NCH])
    lru_lam = dt_in("lru_lam", [128, 2 * 2 * NCH])
    lru_w_out = dt_in("lru_w_out", [2, LRU_W, D])
    consts = dt_in("consts", [128, 128 * 5 + 16])
    out = nc.dram_tensor("out", [NL, D], F32, kind="ExternalOutput").ap()
    dbg_ctx = nc.dram_tensor("dbg_ctx", [CTX, D], F32, kind="ExternalOutput").ap() if dbg else None
    rgroups = [[2 * i, 2 * i + 1] for i in range(ncores // 2)]
    dumps = {}
    DUMP_B = Buf("dump")

    dram = lambda n, s, d=F32: nc.dram_tensor(n, s, d)
    Xs = [dram("Xs0", [NTOK, D]), dram("Xs1", [NTOK, D])]
    QTd = dram("QTd", [NCHUNK, 128, 1024], BF16)
    KTd = dram("KTd", [NCHUNK, 128, 1024], BF16)
    Krd = dram("Krd", [NCHUNK, 128, 1024], BF16)
    Vd = dram("Vd", [NCHUNK, 128, 2048], BF16)
    O1d = dram("O1d", [NCHUNK, 128, 2048])
    XRc = dram("XRc", [NCH, 128, CTX + 4])
    XRl = dram("XRl", [NCH, 128, NL + 4])
    XCc = dram("XCc", [NCH, 128, CTX])
    XCl = dram("XCl", [NCH, 128, NL])
    H1c = dram("H1c", [NCH, 128, CTX])
    H1l = dram("H1l", [NCH, 128, NL])
    SGc = dram("SGc", [NCH, 128, CTX], BF16)
    SGl = dram("SGl", [NCH, 128, NL], BF16)
    cc_state_in = [dram(f"ccsi{j}", [128, 4096]) for j in range(2)]
    cc_state_out = [dram(f"ccso{j}", [256, 4096]) for j in range(2)]
    cc_small_in = [dram(f"ccmi{j}", [128, 32]) for j in range(4)]
    cc_small_out = [dram(f"ccmo{j}", [256, 32]) for j in range(4)]

    es = contextlib.ExitStack()
    with es:
        S = Sched(nc, es)
        sb = lambda n, s, d=F32: es.enter_context(nc.sbuf_tensor(n, s, d))
        W = sb("W", [128, 8 * 4096], BF16)
        Wb = [Buf(f"W{i}") for i in range(8)]
        wslot = lambda s: W[:, s * 4096:(s + 1) * 4096]
        cst = sb("cst", [128, 128 * 5 + 16])
        cst_b = Buf("cst")
        ident_f = cst[:, 0:128]
        ones_f = cst[:, 128:256]
        tri1 = cst[:, 256:384]
        tri2 = cst[:, 384:512]
        zeros_f = cst[:, 512:640]
        coef = cst[:, 640:647]
        eps_col = cst[:, 647:648]
        one_col = cst[:, 648:649]
        sel0 = cst[:, 649:650]
        sel1 = cst[:, 650:651]
        ident_b = sb("ident_b", [128, 128], BF16)
        ident_bb = Buf("ident_b")
        ccol = sb("ccol", [128, 16])
        act_bf = sb("act_bf", [128, 16], BF16)
        act_b = Buf("act")
        small = sb("small", [128, DEPTH * 24 + DEPTH * 16 + 16 + 32 + 100 + 120])
        small_b = Buf("small")
        o = 0
        modb_t = small[:, o:o + DEPTH * 24]; o += DEPTH * 24
        npre_t = small[:, o:o + DEPTH * 8]; o += DEPTH * 8
        npost_t = small[:, o:o + DEPTH * 8]; o += DEPTH * 8
        lg_t = small[:, o:o + 16]; o += 16
        gn_t = small[:, o:o + 32]; o += 32
        cw_t = small[:, o:o + 100]; o += 100
        cb_t = small[:, o:o + 20]; o += 20
        ba_t = small[:, o:o + 40]; o += 40
        bx_t = small[:, o:o + 40]; o += 40
        lam_t = sb("lam_t", [128, 40])
        modT = sb("modT", [128, 48])
        modT_b = Buf("modT")
        A_col = sb("A_col", [128, 16])
        G2col = sb("G2col", [128, 16])
        col_b = Buf("cols")
        G2bc = [sb("G2bc0", [128, D]), sb("G2bc1", [128, D])]
        G2bc_b = Buf("G2bc")
        ps_proj = [es.enter_context(nc.psum_tensor(f"ps_proj{i}", [128, 512], F32)) for i in range(2)]
        ps_proj_b = [Buf("pp0"), Buf("pp1")]
        ps_tr = es.enter_context(nc.psum_tensor("ps_tr", [128, 1024], BF16))
        ps_tr_b = Buf("ptr")
        ps_st = es.enter_context(nc.psum_tensor("ps_st", [128, 512], F32))
        ps_st_b = Buf("pst")
        ps_o = [es.enter_context(nc.psum_tensor(f"ps_o{i}", [128, 512], F32)) for i in range(2)]
        ps_o_b = [Buf("po0"), Buf("po1")]
        ps_su = [es.enter_context(nc.psum_tensor(f"ps_su{i}", [128, 512], F32)) for i in range(2)]
        ps_su_b = [Buf("psu0"), Buf("psu1")]

        def dma_sp(out_ap, in_ap, reads, writes):
            S.add("sp", "d", [lambda e: e.dma_start(out=out_ap, in_=in_ap)], reads, writes)

        def dma_act(out_ap, in_ap, reads, writes):
            S.add("act", "d", [lambda e: e.dma_start(out=out_ap, in_=in_ap)], reads, writes)

        def dma_pool(out_ap, in_ap, reads, writes, slow=False):
            if slow:
                S.add("pool", "d", [lambda e: e.dma_start(out=out_ap, in_=in_ap, allow_slow_non_contiguous=True)], reads, writes)
            else:
                S.add("pool", "d", [lambda e: e.dma_start(out=out_ap, in_=in_ap)], reads, writes)

        def act(out_ap, in_ap, func, reads, writes, scale=None, bias=None):
            kw = {}
            if scale is not None:
                kw["scale"] = scale
            if bias is not None:
                kw["bias"] = bias
            S.add("act", "c", [lambda e: e.activation(out=out_ap, in_=in_ap, func=func, **kw)], reads, writes)

        def ts(stream, out_ap, in_ap, s1, s2, op0, op1, reads, writes):
            if op1 is None:
                S.add(stream, "c", [lambda e: e.tensor_scalar(out=out_ap, in0=in_ap, scalar1=s1, scalar2=None, op0=op0)], reads, writes)
            else:
                S.add(stream, "c", [lambda e: e.tensor_scalar(out=out_ap, in0=in_ap, scalar1=s1, scalar2=s2, op0=op0, op1=op1)], reads, writes)

        def tt(stream, out_ap, a, b, op, reads, writes):
            S.add(stream, "c", [lambda e: e.tensor_tensor(out=out_ap, in0=a, in1=b, op=op)], reads, writes)

        def stt(out_ap, in0, scalar, in1, op0, op1, reads, writes):
            S.add("dve", "c", [lambda e: e.scalar_tensor_tensor(out=out_ap, in0=in0, scalar=scalar, in1=in1, op0=op0, op1=op1)], reads, writes)

        def copy(stream, out_ap, in_ap, reads, writes):
            if stream == "act":
                S.add("act", "c", [lambda e: e.copy(out=out_ap, in_=in_ap)], reads, writes)
            else:
                S.add(stream, "c", [lambda e: e.tensor_copy(out=out_ap, in_=in_ap)], reads, writes)

        def mm_group(specs, reads, writes):
            fns = []
            for (o_, l_, r_, st_, sp_) in specs:
                fns.append(lambda e, o_=o_, l_=l_, r_=r_, st_=st_, sp_=sp_: e.matmul(o_, lhsT=l_, rhs=r_, start=st_, stop=sp_))
            S.add("pe", "c", fns, reads, writes)

        def tr_group(specs, reads, writes):
            fns = []
            for (o_, i_) in specs:
                fns.append(lambda e, o_=o_, i_=i_: e.transpose(o_, i_, ident_b[:]))
            S.add("pe", "c", fns, list(reads) + [ident_bb], writes)

        def dump(name, ap, buf, F):
            if not dbg:
                return
            t = nc.dram_tensor("dmp_" + name, [128, F], F32, kind="ExternalOutput").ap()
            dumps[name] = t
            dma_pool(t[:, :], ap, [buf], [DUMP_B])

        dma_sp(cst[:], consts[:, :], [], [cst_b])
        dma_pool(ident_b[:], consts[:, 0:128], [], [ident_bb])
        dma_sp(ccol[:], c_col[:, :], [], [act_b])
        dma_sp(modb_t, mod_b_col[:, :], [], [small_b])
        dma_sp(npre_t, npre_col[:, :], [], [small_b])
        dma_sp(npost_t, npost_col[:, :], [], [small_b])
        dma_sp(lg_t, ret_lg[:, :], [], [small_b])
        dma_sp(gn_t, ret_gn_col[:, :], [], [small_b])
        dma_sp(cw_t, lru_cw[:, :], [], [small_b])
        dma_sp(cb_t, lru_cb[:, :], [], [small_b])
        dma_sp(ba_t, lru_ba[:, :], [], [small_b])
        dma_sp(bx_t, lru_bx[:, :], [], [small_b])
        dma_sp(lam_t[:], lru_lam[:, :], [], [small_b])
        act(act_bf[:], ccol[:], AF.Silu, [act_b], [act_b])

        def prep_layer(l, les):
            lsb = lambda n, s, d=F32: les.enter_context(nc.sbuf_tensor(n, s, d))
            dg = Ring(nc, les, f"dg{l}_", [128, 128], F32, 2)
            actv = act_bf[:].rearrange("p (k v) -> p k v", v=2)
            for cg in range(6):
                s = cg % 2
                wv = wslot(s).rearrange("p (k c) -> p k c", k=8)
                dma_pool(wv, mod_w[l][:, cg * 512:(cg + 1) * 512].rearrange("(k p) c -> p k c", p=128), [], [Wb[s]])
                specs = []
                for cc in range(4):
                    ck = cg * 4 + cc
                    for k in range(8):
                        specs.append((ps_st[:, ck * 2:ck * 2 + 2], wv[:, k, cc * 128:(cc + 1) * 128], actv[:, k, :], k == 0, k == 7))
                mm_group(specs, [Wb[s], act_b], [ps_st_b])
            tt("dve", modT[:].rearrange("p (c v) -> p c v", v=2), ps_st[:, 0:48].rearrange("p (c v) -> p c v", v=2),
               modb_t[:, l * 24:(l + 1) * 24].unsqueeze(2).to_broadcast([128, 24, 2]), ALU.add, [ps_st_b, small_b], [modT_b])
            npre_bc = npre_t[:, l * 8:(l + 1) * 8].unsqueeze(2).to_broadcast([128, 8, 2])
            npost_bc = npost_t[:, l * 8:(l + 1) * 8].unsqueeze(2).to_broadcast([128, 8, 2])
            v3 = lambda t: t.rearrange("p (c v) -> p c v", v=2)
            stt(v3(A_col[:]), v3(modT[:, 16:32]), 1.0, npre_bc, ALU.add, ALU.mult, [modT_b, small_b], [col_b])
            tt("dve", v3(G2col[:]), v3(modT[:, 32:48]), npost_bc, ALU.mult, [modT_b, small_b, col_b], [col_b])
            for v in range(2):
                for half in range(2):
                    specs = []
                    dbs = []
                    for kk in range(4):
                        k = half * 4 + kk
                        dt_, db_ = dg.next()
                        ts("dve", dt_[:], ident_f, G2col[:, k * 2 + v:k * 2 + v + 1], None, ALU.mult, None, [cst_b, col_b], [db_])
                        mm_group([(ps_o[half][:, kk * 128:(kk + 1) * 128], ones_f, dt_[:], True, True)], [db_, cst_b], [ps_o_b[half]])
                    copy("act", G2bc[v][:, half * 512:(half + 1) * 512], ps_o[half][:], [ps_o_b[half]], [G2bc_b])

        def norm_a(R, src_ap, src_buf, row0):
            xt, xb = R["x"].next()
            dma_sp(xt[:], src_ap[row0:row0 + 128, :], [src_buf], [xb])
            jt, jb = R["junk"].next()
            st_t, st_b = R["stat"].next()
            act(jt[:], xt[:], AF.Square, [xb], [jb])
            S.add("dve", "c", [lambda e: e.tensor_reduce(out=st_t[:, 0:1], in_=jt[:], axis=AX.X, op=ALU.add)], [jb], [st_b])
            act(st_t[:, 1:2], st_t[:, 0:1], AF.Sqrt, [st_b, cst_b], [st_b], scale=1.0 / D, bias=eps_col)
            S.add("dve", "c", [lambda e: e.reciprocal(out=st_t[:, 2:3], in_=st_t[:, 1:2])], [st_b], [st_b])
            xh, xhb = R["xhat"].next()
            ts("dve", xh[:], xt[:], st_t[:, 2:3], None, ALU.mult, None, [xb, st_b], [xhb])
            return xt, xb, xh, xhb

        def norm_b(xh, xhb, v, hT_ap_fn, hT_buf):
            tr_group([(ps_tr[:, k * 128:(k + 1) * 128], xh[:, k * 128:(k + 1) * 128]) for k in range(8)], [xhb], [ps_tr_b])
            for k in range(8):
                act(hT_ap_fn(k), ps_tr[:, k * 128:(k + 1) * 128], AF.Identity, [ps_tr_b, col_b, modT_b], [hT_buf],
                    scale=A_col[:, k * 2 + v:k * 2 + v + 1], bias=modT[:, k * 2 + v:k * 2 + v + 1])

        def norm_chunk(R, src_ap, src_buf, row0, v, hT_ap_fn, hT_buf, keep_x=False):
            xt, xb, xh, xhb = norm_a(R, src_ap, src_buf, row0)
            norm_b(xh, xhb, v, hT_ap_fn, hT_buf)
            return xt, xb

        def post_chunk(R, xt, xb, v, dst_ap, dst_buf, dst_row0):
            jt, jb = R["junk"].next()
            st_t, st_b = R["stat"].next()
            for g in range(2):
                act(jt[:, g * 512:(g + 1) * 512], ps_proj[g][:], AF.Square, [ps_proj_b[g]], [jb])
            S.add("dve", "c", [lambda e: e.tensor_reduce(out=st_t[:, 0:1], in_=jt[:], axis=AX.X, op=ALU.add)], [jb], [st_b])
            act(st_t[:, 1:2], st_t[:, 0:1], AF.Sqrt, [st_b, cst_b], [st_b], scale=1.0 / D, bias=eps_col)
            S.add("dve", "c", [lambda e: e.reciprocal(out=st_t[:, 2:3], in_=st_t[:, 1:2])], [st_b], [st_b])
            tm, tmb = R["tmp"].next()
            xn, xnb = R["xn"].next()
            for g in range(2):
                stt(tm[:, g * 512:(g + 1) * 512], ps_proj[g][:], st_t[:, 2:3], G2bc[v][:, g * 512:(g + 1) * 512],
                    ALU.mult, ALU.mult, [ps_proj_b[g], st_b, G2bc_b], [tmb])
            tt("pool", xn[:], tm[:], xt[:], ALU.add, [tmb, xb], [xnb])
            if dst_row0 == CTX and v == 0 and "tm" not in dumps:
                dump("tm", tm[:], tmb, 1024); dump("ysq", jt[:], jb, 1024); dump("stat", st_t[:], st_b, 4)
            dma_pool(dst_ap[dst_row0:dst_row0 + 128, :], xn[:], [xnb], [dst_buf])

        def exchange_start(idx_big, src_ap, src_buf, F):
            if F > 32:
                cin, cout = cc_state_in[idx_big], cc_state_out[idx_big]
            else:
                cin, cout = cc_small_in[idx_big], cc_small_out[idx_big]
            cb = Buf("ccin")
            cob = Buf("ccout")
            dma_pool(cin[:, 0:F], src_ap, (src_buf if isinstance(src_buf, list) else [src_buf]), [cb])
            S.add("pool", "cc", [lambda e: e.collective_compute(
                "AllGather", ALU.bypass, replica_groups=rgroups,
                ins=[cin[:, :]], outs=[cout[:, :]])], [cb], [cob])
            return cout, cob

        def exchange_finish(R, cout, cob, dst_ap, dst_buf, F):
            step = min(F, int(R["xg"].t[0].shape[1]))
            for p0 in range(0, F, step):
                g0, g0b = R["xg"].next()
                g1, g1b = R["xg"].next()
                dma_sp(g0[:, 0:step], cout[0:128, p0:p0 + step], [cob], [g0b])
                dma_sp(g1[:, 0:step], cout[128:256, p0:p0 + step], [cob], [g1b])
                ts("dve", g0[:, 0:step], g0[:, 0:step], sel0, None, ALU.mult, None, [g0b, cst_b], [g0b])
                stt(dst_ap[:, p0:p0 + step], g1[:, 0:step], sel1, g0[:, 0:step], ALU.mult, ALU.add, [g0b, g1b, cst_b],
                    (dst_buf if isinstance(dst_buf, list) else [dst_buf]))

        def retention_layer(l, src_ap, src_buf, dst_ap, dst_buf, last):
            j = l // 2
            les = contextlib.ExitStack()
            with les:
                lsb = lambda n, s, d=F32: les.enter_context(nc.sbuf_tensor(f"{n}_L{l}", s, d))
                prep_layer(l, les)
                state = lsb("state", [128, 4096])
                state_bf = lsb("state_bf", [128, 4096], BF16)
                state_hb = [Buf(f"state{h}") for h in range(4)]
                statebf_hb = [Buf(f"state_bf{h}") for h in range(4)]
                state_b = state_hb
                statebf_b = statebf_hb
                mask = [lsb("mask1", [128, 512]), lsb("mask2", [128, 512])]
                tab = lsb("tab", [128, 2 * 28])
                lgn = lsb("lgn", [128, 8])
                tab_b = Buf("tab")
                mask_b = Buf("mask")
                ts("dve", tab[:, 0:8], lg_t[:, j * 8:(j + 1) * 8], -1.0, None, ALU.mult, None, [small_b], [tab_b])
                tt("dve", lgn[:], lg_t[:, j * 8:(j + 1) * 8], tab[:, 0:8], ALU.min, [small_b, tab_b], [tab_b])
                for d in range(2):
                    for i in range(7):
                        ts("dve", tab[:, d * 28 + i * 4:d * 28 + i * 4 + 4], lgn[:, d * 4:(d + 1) * 4], coef[:, i:i + 1], None,
                           ALU.mult, None, [tab_b, cst_b], [tab_b])
                act(tab[:], tab[:], AF.Exp, [tab_b], [tab_b])
                for d in range(2):
                    for i in ((0, 2) if d == 0 else (3, 5)):
                        ts("dve", tab[:, d * 28 + i * 4:d * 28 + i * 4 + 4], tab[:, d * 28 + i * 4:d * 28 + i * 4 + 4], 0.0625, None,
                           ALU.mult, None, [tab_b], [tab_b])
                T = lambda d, i, h: tab[:, d * 28 + i * 4 + h:d * 28 + i * 4 + h + 1]
                for d in range(2):
                    for h in range(4):
                        ts("dve", mask[d][:, h * 128:(h + 1) * 128], tri1 if d == 0 else tri2, T(d, 2 if d == 0 else 5, h), None,
                           ALU.mult, None, [tab_b, cst_b], [mask_b])
                KD = (0, 3)
                QD = (1, 4)

                def chunk_rows(n):
                    return n * 128, (1 if n < NCC else 0)

                def scan_part(R, d, n, QT, QTb, KT, KTb, Kr, Krb, V, Vb, o_t, o_b, o1_t, o1_b, need_o):
                    if need_o:
                        specs = []
                        for h in range(4):
                            for dc in range(2):
                                specs.append((ps_st[:, h * 128:(h + 1) * 128], KT[:, (2 * h + dc) * 128:(2 * h + dc + 1) * 128],
                                              QT[:, (2 * h + dc) * 128:(2 * h + dc + 1) * 128], dc == 0, dc == 1))
                        mm_group(specs, [KTb, QTb], [ps_st_b])
                        P, Pb = R["P"].next()
                        tt("dve", P[:], ps_st[:], mask[d][:], ALU.mult, [ps_st_b, mask_b], [Pb])
                    Kd, Kdb = R["Kdec"].next()
                    ts_bc = tab[:, d * 28 + KD[d] * 4:d * 28 + KD[d] * 4 + 4].unsqueeze(2).to_broadcast([128, 4, 256])
                    tt("pool", Kd[:].rearrange("p (h c) -> p h c", h=4), Kr[:].rearrange("p (h c) -> p h c", h=4), ts_bc, ALU.mult, [Krb, tab_b], [Kdb])
                    for h in range(4):
                        if need_o:
                            po, pob = ps_o[h % 2], ps_o_b[h % 2]
                            specs = [(po[:], P[:, h * 128:(h + 1) * 128], V[:, h * 512:(h + 1) * 512], True, False)]
                            for dc in range(2):
                                specs.append((po[:], QT[:, (2 * h + dc) * 128:(2 * h + dc + 1) * 128],
                                              state_bf[:, (h * 2 + dc) * 512:(h * 2 + dc + 1) * 512], False, dc == 1))
                            mm_group(specs, [Pb, Vb, QTb, statebf_hb[h]], [pob])
                            if d == 0:
                                act(o_t[:, h * 512:(h + 1) * 512], po[:], AF.Identity, [pob, tab_b], [o_b], scale=T(d, QD[d], h))
                            else:
                                stt(o_t[:, h * 512:(h + 1) * 512], po[:], T(d, QD[d], h), o1_t[:, h * 512:(h + 1) * 512],
                                    ALU.mult, ALU.add, [pob, tab_b, o1_b], [o_b])
                        for dc in range(2):
                            mm_group([(ps_su[dc][:], Kd[:, h * 256 + dc * 128:h * 256 + (dc + 1) * 128], V[:, h * 512:(h + 1) * 512], True, True)],
                                     [Kdb, Vb], [ps_su_b[dc]])
                            sl = slice((h * 2 + dc) * 512, (h * 2 + dc + 1) * 512)
                            stt(state[:, sl], state[:, sl], T(d, 6, h), ps_su[dc][:], ALU.mult, ALU.add,
                                [ps_su_b[dc], tab_b, state_hb[h]], [state_hb[h]])
                        for dc in range(2):
                            sl = slice((h * 2 + dc) * 512, (h * 2 + dc + 1) * 512)
                            copy("act", state_bf[:, sl], state[:, sl], [state_hb[h]], [statebf_hb[h]])

                def zero_state():
                    S.add("dve", "c", [lambda e: e.memset(state[:], 0.0)], [], state_hb)
                    S.add("pool", "c", [lambda e: e.memset(state_bf[:], 0.0)], [], statebf_hb)

                Qb_d = [Buf(f"QTd{n}") for n in range(NCHUNK)]
                Kb_d = [Buf(f"KTd{n}") for n in range(NCHUNK)]
                Krb_d = [Buf(f"Krd{n}") for n in range(NCHUNK)]
                Vb_d = [Buf(f"Vd{n}") for n in range(NCHUNK)]
                O1b_d = [Buf(f"O1d{n}") for n in range(NCHUNK)]

                pes = contextlib.ExitStack()
                with pes:
                    R = {
                        "x": Ring(nc, pes, "r1x", [128, D], F32, 2), "junk": Ring(nc, pes, "r1j", [128, D], F32, 1),
                        "stat": Ring(nc, pes, "r1s", [128, 4], F32, 3), "xhat": Ring(nc, pes, "r1xh", [128, D], BF16, 3),
                        "hT": Ring(nc, pes, "r1hT", [128, D], BF16, 2), "cs": Ring(nc, pes, "r1cs", [128, 256], F32, 2),
                        "qkf": Ring(nc, pes, "r1qkf", [128, 1024], F32, 2), "rt": Ring(nc, pes, "r1rt", [128, 2048], F32, 2),
                        "Qr": Ring(nc, pes, "r1Qr", [128, D], BF16, 2), "Kr": Ring(nc, pes, "r1Kr", [128, D], BF16, 2),
                        "QT": Ring(nc, pes, "r1QT", [128, D], BF16, 2), "KT": Ring(nc, pes, "r1KT", [128, D], BF16, 2),
                        "V": Ring(nc, pes, "r1V", [128, 2048], BF16, 2), "P": Ring(nc, pes, "r1P", [128, 512], BF16, 2),
                        "Kdec": Ring(nc, pes, "r1Kd", [128, D], BF16, 2), "o1": Ring(nc, pes, "r1o1", [128, 2048], F32, 2),
                    }
                    for cg in range(8):
                        dma_pool(wslot(cg).rearrange("p (k c) -> p k c", k=8),
                                 ret_w_in[j][:, cg * 512:(cg + 1) * 512].rearrange("(k p) c -> p k c", p=128), [], [Wb[cg]])
                    zero_state()
                    def stageA0a(n):
                        c = {"n": n}
                        row0, v = chunk_rows(n)
                        _, _, c["xh"], c["xhb"] = norm_a(R, src_ap, src_buf, row0)
                        return c

                    def stageA0b(c):
                        row0, v = chunk_rows(c["n"])
                        hT, hTb = R["hT"].next()
                        norm_b(c["xh"], c["xhb"], v, lambda k, hT=hT: hT[:, k * 128:(k + 1) * 128], hTb)
                        c.update(hT=hT, hTb=hTb)
                        return c

                    def stageA1(c, inject=None):
                        n = c["n"]
                        row0, v = chunk_rows(n)
                        hT, hTb = c["hT"], c["hTb"]
                        cs, csb = R["cs"].next()
                        dma_sp(cs[:, 0:128], ropec[row0:row0 + 128, :], [], [csb])
                        dma_sp(cs[:, 128:256], ropes[row0:row0 + 128, :], [], [csb])
                        cosb = cs[:, 0:128].unsqueeze(1).to_broadcast([128, 4, 128])
                        sinb = cs[:, 128:256].unsqueeze(1).to_broadcast([128, 4, 128])
                        Qr, Qrb = R["Qr"].next()
                        Kr, Krb = R["Kr"].next()
                        V, Vb = R["V"].next()
                        qf = None
                        inj = None
                        for ci, cg in enumerate((2, 3, 0, 1, 4, 5, 6, 7)):
                            pp, ppb = ps_proj[ci % 2], ps_proj_b[ci % 2]
                            wv = wslot(cg).rearrange("p (k c) -> p k c", k=8)
                            mm_group([(pp[:], hT[:, k * 128:(k + 1) * 128], wv[:, k, :], k == 0, k == 7) for k in range(8)],
                                     [hTb, Wb[cg]], [ppb])
                            if cg < 4:
                                if cg % 2 == 0:
                                    qf, qfb = R["qkf"].next()
                                copy("act", qf[:, (cg % 2) * 512:(cg % 2 + 1) * 512], pp[:], [ppb], [qfb])
                                if cg % 2 == 1:
                                    dst, dstb = (Qr, Qrb) if cg < 2 else (Kr, Krb)
                                    eng = "dve" if cg < 2 else "pool"
                                    rt, rtb = R["rt"].next()
                                    q4 = qf[:].rearrange("p (h e j) -> p h e j", h=4, e=2)
                                    te, to = q4[:, :, 0, :], q4[:, :, 1, :]
                                    r4 = rt[:].rearrange("p (a h j) -> p a h j", a=4, h=4)
                                    d4 = dst[:].rearrange("p (h e j) -> p h e j", h=4, e=2)
                                    tt(eng, r4[:, 0], te, cosb, ALU.mult, [qfb, csb], [rtb])
                                    tt(eng, r4[:, 1], to, sinb, ALU.mult, [qfb, csb], [rtb])
                                    tt(eng, r4[:, 2], te, sinb, ALU.mult, [qfb, csb], [rtb])
                                    tt(eng, r4[:, 3], to, cosb, ALU.mult, [qfb, csb], [rtb])
                                    tt(eng, d4[:, :, 0, :], r4[:, 0], r4[:, 1], ALU.subtract, [rtb], [dstb])
                                    tt(eng, d4[:, :, 1, :], r4[:, 2], r4[:, 3], ALU.add, [rtb], [dstb])
                            else:
                                copy("act", V[:, (cg - 4) * 512:(cg - 3) * 512], pp[:], [ppb], [Vb])
                            if ci == 3 and inject is not None:
                                inj = stageA0a(inject)
                        QT, QTb = R["QT"].next()
                        KT, KTb = R["KT"].next()
                        tr_group([(ps_tr[:, k * 128:(k + 1) * 128], Qr[:, k * 128:(k + 1) * 128]) for k in range(8)], [Qrb], [ps_tr_b])
                        copy("dve", QT[:], ps_tr[:], [ps_tr_b], [QTb])
                        tr_group([(ps_tr[:, k * 128:(k + 1) * 128], Kr[:, k * 128:(k + 1) * 128]) for k in range(8)], [Krb], [ps_tr_b])
                        copy("act", KT[:], ps_tr[:], [ps_tr_b], [KTb])
                        c.update(QT=QT, QTb=QTb, KT=KT, KTb=KTb, Kr=Kr, Krb=Krb, V=V, Vb=Vb)
                        if inj is not None:
                            inj = stageA0b(inj)
                        return c, inj

                    def stageB1(c):
                        n = c["n"]
                        o1, o1b = R["o1"].next()
                        scan_part(R, 0, n, c["QT"], c["QTb"], c["KT"], c["KTb"], c["Kr"], c["Krb"], c["V"], c["Vb"], o1, o1b, None, None, True)
                        dma_sp(QTd[n], c["QT"][:], [c["QTb"]], [Qb_d[n]])
                        dma_act(KTd[n], c["KT"][:], [c["KTb"]], [Kb_d[n]])
                        dma_pool(Krd[n], c["Kr"][:], [c["Krb"]], [Krb_d[n]])
                        dma_act(Vd[n], c["V"][:], [c["Vb"]], [Vb_d[n]])
                        dma_act(O1d[n], o1[:], [o1b], [O1b_d[n]])

                    c0 = stageA0b(stageA0a(0))
                    prev, nxt0 = stageA1(c0, inject=1 if NCHUNK > 1 else None)
                    for n in range(NCHUNK):
                        if n + 1 < NCHUNK:
                            nxt, nxt0 = stageA1(nxt0, inject=(n + 2) if n + 2 < NCHUNK else None)
                        else:
                            nxt = None
                        stageB1(prev)
                        prev = nxt
                    S.barrier()
                pes = contextlib.ExitStack()
                with pes:
                    R = {
                        "x": Ring(nc, pes, "r2x", [128, D], F32, 5), "junk": Ring(nc, pes, "r2j", [128, D], F32, 1),
                        "stat": Ring(nc, pes, "r2s", [128, 4], F32, 2), "xhat": Ring(nc, pes, "r2xh", [128, D], BF16, 2),
                        "hT": Ring(nc, pes, "r2hT", [128, D], BF16, 1),
                        "QT": Ring(nc, pes, "r2QT", [128, D], BF16, 2), "KT": Ring(nc, pes, "r2KT", [128, D], BF16, 2),
                        "Kr": Ring(nc, pes, "r2Kr", [128, D], BF16, 2),
                        "V": Ring(nc, pes, "r2V", [128, 2048], BF16, 2), "P": Ring(nc, pes, "r2P", [128, 512], BF16, 2),
                        "Kdec": Ring(nc, pes, "r2Kd", [128, D], BF16, 2), "o1": Ring(nc, pes, "r2o1", [128, 2048], F32, 2),
                        "SG": Ring(nc, pes, "r2SG", [128, 2048], BF16, 2), "Z": Ring(nc, pes, "r2Z", [128, 2048], BF16, 2),
                        "ZT": Ring(nc, pes, "r2ZT", [128, 2048], BF16, 1),
                        "xn": Ring(nc, pes, "r2xn", [128, D], F32, 1),
                        "bn": Ring(nc, pes, "r2bn", [128, 48], F32, 2),
                    }
                    R["xg"] = Ring(nc, pes, "r2xg", [128, 512], F32, 2)
                    R["wst"] = R["x"]
                    R["tmp"] = R["junk"]
                    R["stat"] = Ring(nc, pes, "r2s2", [128, 4], F32, 6)
                    if PAIR:
                        ex_cout, ex_cob = exchange_start(j, state[:], state_hb, 4096)
                    for cg in range(4):
                        dma_pool(wslot(cg).rearrange("p (k c) -> p k c", k=8),
                                 ret_w_in[j][:, (12 - 4 + cg) * 512:(12 - 3 + cg) * 512].rearrange("(k p) c -> p k c", p=128), [], [Wb[cg]])
                    Wo = W[:, 4 * 4096:8 * 4096].rearrange("p (k c) -> p k c", k=16)
                    for k in range(16):
                        wt_, wtb = R["wst"].next()
                        dma_sp(wt_[:], ret_w_out[j][k * 128:(k + 1) * 128, :], [], [wtb])
                        ts("dve", Wo[:, k, :], wt_[:], gn_t[:, j * 16 + k:j * 16 + k + 1], None, ALU.mult, None, [wtb, small_b], [Wb[4 + k // 4]])
                    zero_state()
                    order = list(range(NCC - 1, -1, -1)) + list(range(NCHUNK - 1, NCC - 1, -1))

                    def mk2(n):
                        c = {"n": n}
                        c["row0"], c["v"] = chunk_rows(n)
                        c["skip"] = last and c["v"] == 1 and not dbg
                        return c

                    def loads2(c):
                        n = c["n"]
                        c["Kr"], c["Krb"] = R["Kr"].next()
                        c["V"], c["Vb"] = R["V"].next()
                        dma_sp(c["Kr"][:], Krd[n], [Krb_d[n]], [c["Krb"]])
                        dma_sp(c["V"][:], Vd[n], [Vb_d[n]], [c["Vb"]])
                        if not c["skip"]:
                            c["QT"], c["QTb"] = R["QT"].next()
                            c["KT"], c["KTb"] = R["KT"].next()
                            dma_sp(c["QT"][:], QTd[n], [Qb_d[n]], [c["QTb"]])
                            dma_sp(c["KT"][:], KTd[n], [Kb_d[n]], [c["KTb"]])
                        return c

                    def stageB2(c):
                        n = c["n"]
                        if n == NCHUNK - 1 and PAIR:
                            exchange_finish(R, ex_cout, ex_cob, state, state_hb, 4096)
                            copy("pool", state_bf[:], state[:], state_hb, statebf_hb)
                        if c["skip"]:
                            scan_part(R, 1, n, None, None, None, None, c["Kr"], c["Krb"], c["V"], c["Vb"], None, None, None, None, False)
                            return
                        o1, o1b = c["o1"], c["o1b"]
                        scan_part(R, 1, n, c["QT"], c["QTb"], c["KT"], c["KTb"], c["Kr"], c["Krb"], c["V"], c["Vb"], o1, o1b, o1, o1b, True)

                    def c2a_norm(c):
                        if c["skip"]:
                            return
                        c["xt"], c["xb"], c["xh"], c["xhb"] = norm_a(R, src_ap, src_buf, c["row0"])

                    def c2a_pe(c):
                        if c["skip"]:
                            return
                        hT, hTb = R["hT"].next()
                        norm_b(c["xh"], c["xhb"], c["v"], lambda k, hT=hT: hT[:, k * 128:(k + 1) * 128], hTb)
                        SG, SGb = R["SG"].next()
                        for cg in range(4):
                            pp, ppb = ps_proj[cg % 2], ps_proj_b[cg % 2]
                            wv = wslot(cg).rearrange("p (k c) -> p k c", k=8)
                            mm_group([(pp[:], hT[:, k * 128:(k + 1) * 128], wv[:, k, :], k == 0, k == 7) for k in range(8)],
                                     [hTb, Wb[cg]], [ppb])
                            act(SG[:, cg * 512:(cg + 1) * 512], pp[:], AF.Silu, [ppb], [SGb])
                        c["SG"], c["SGb"] = SG, SGb

                    def c2b(c):
                        if c["skip"]:
                            return
                        o1, o1b = c["o1"], c["o1b"]
                        bn, bnb = R["bn"].next()
                        for h in range(4):
                            S.add("dve", "c", [lambda e, h=h, bn=bn, o1=o1: e.bn_stats(out=bn[:, h * 6:(h + 1) * 6], in_=o1[:, h * 512:(h + 1) * 512])], [o1b], [bnb])
                        for h in range(4):
                            S.add("dve", "c", [lambda e, h=h, bn=bn: e.bn_aggr(out=bn[:, 24 + h * 2:24 + h * 2 + 2], in_=bn[:, h * 6:(h + 1) * 6])], [bnb], [bnb])
                        mv = bn[:, 24:32].rearrange("p (h t) -> p h t", t=2)
                        act(bn[:, 32:36], mv[:, :, 1], AF.Sqrt, [bnb, cst_b], [bnb], scale=1.0, bias=eps_col)
                        S.add("dve", "c", [lambda e, bn=bn: e.reciprocal(out=bn[:, 36:40], in_=bn[:, 32:36])], [bnb], [bnb])
                        for h in range(4):
                            ts("dve", o1[:, h * 512:(h + 1) * 512], o1[:, h * 512:(h + 1) * 512], bn[:, 24 + h * 2:24 + h * 2 + 1],
                               bn[:, 36 + h:37 + h], ALU.subtract, ALU.mult, [o1b, bnb], [o1b])
                        Z, Zb = R["Z"].next()
                        tt("pool", Z[:], o1[:], c["SG"][:], ALU.mult, [o1b, c["SGb"]], [Zb])
                        c["Z"], c["Zb"] = Z, Zb

                    def c2c(c):
                        if c["skip"]:
                            return
                        n, row0, v = c["n"], c["row0"], c["v"]
                        Z, Zb = c["Z"], c["Zb"]
                        ZT, ZTb = R["ZT"].next()
                        for half in range(2):
                            tr_group([(ps_tr[:, k * 128:(k + 1) * 128], Z[:, (half * 8 + k) * 128:(half * 8 + k + 1) * 128]) for k in range(8)],
                                     [Zb], [ps_tr_b])
                            copy("act" if half == 0 else "dve", ZT[:, half * 1024:(half + 1) * 1024], ps_tr[:], [ps_tr_b], [ZTb])
                        for g in range(2):
                            mm_group([(ps_proj[g][:], ZT[:, k * 128:(k + 1) * 128], Wo[:, k, g * 512:(g + 1) * 512], k == 0, k == 15) for k in range(16)],
                                     [ZTb] + Wb[4:8], [ps_proj_b[g]])
                        xt, xb = c["xt"], c["xb"]
                        if last and v == 1:
                            post_chunk(R, xt, xb, v, dbg_ctx, dst_buf, row0)
                        elif last:
                            post_chunk(R, xt, xb, v, out, dst_buf, row0 - CTX)
                        else:
                            post_chunk(R, xt, xb, v, dst_ap, dst_buf, row0)

                    NO = len(order)
                    cs2 = {0: loads2(mk2(order[0]))}
                    c2a_norm(cs2[0])
                    c2a_pe(cs2[0])
                    if NO > 1:
                        cs2[1] = mk2(order[1])
                        c2a_norm(cs2[1])
                    for i, n in enumerate(order):
                        c = cs2[i]
                        if not c["skip"]:
                            c["o1"], c["o1b"] = R["o1"].next()
                            dma_sp(c["o1"][:], O1d[n], [O1b_d[n]], [c["o1b"]])
                        if i + 2 < NO:
                            cs2[i + 2] = mk2(order[i + 2])
                        if i + 1 < NO:
                            loads2(cs2[i + 1])
                            c2a_pe(cs2[i + 1])
                        stageB2(c)
                        if i + 2 < NO:
                            c2a_norm(cs2[i + 2])
                        if i >= 1:
                            c2c(cs2[i - 1])
                            del cs2[i - 1]
                        c2b(c)
                    c2c(cs2[NO - 1])
                    S.barrier()

        def lru_layer(l, src_ap, src_buf, dst_ap, dst_buf, last):
            j = l // 2
            tiles = [("c", 0, CTX)] + [("l", i * 512, 512) for i in range(NL // 512)]
            XR = {"c": XRc, "l": XRl}
            XC = {"c": XCc, "l": XCl}
            H1 = {"c": H1c, "l": H1l}
            SGD = {"c": SGc, "l": SGl}
            les = contextlib.ExitStack()
            with les:
                lsb = lambda n, s, d=F32: les.enter_context(nc.sbuf_tensor(f"{n}_L{l}", s, d))
                prep_layer(l, les)
                cl = lsb("cl", [128, 20])
                sp_t = lsb("sp_t", [128, 120])
                cl_b = Buf("cl")
                hprev = lsb("hprev", [128, 2 * NCH])
                hprev_b = Buf("hprev")
                lam_j = lam_t[:, j * 20:(j + 1) * 20]
                A0 = lambda i: sp_t[:, i * 20:(i + 1) * 20]
                ts("dve", A0(1), lam_j, -1.0, None, ALU.mult, None, [small_b], [cl_b])
                tt("dve", A0(0), lam_j, A0(1), ALU.min, [small_b, cl_b], [cl_b])
                act(A0(1), A0(0), AF.Exp, [cl_b], [cl_b])
                ts("dve", A0(2), A0(1), 2.0, None, ALU.add, None, [cl_b], [cl_b])
                S.add("dve", "c", [lambda e: e.reciprocal(out=A0(3), in_=A0(2))], [cl_b], [cl_b])
                tt("dve", A0(2), A0(1), A0(3), ALU.mult, [cl_b], [cl_b])
                tt("dve", A0(3), A0(2), A0(2), ALU.mult, [cl_b], [cl_b])
                ts("dve", A0(4), A0(3), 1.0 / 17.0, 1.0 / 15.0, ALU.mult, ALU.add, [cl_b], [cl_b])
                for cden in (13.0, 11.0, 9.0, 7.0, 5.0, 3.0, 1.0):
                    tt("dve", A0(4), A0(4), A0(3), ALU.mult, [cl_b], [cl_b])
                    ts("dve", A0(4), A0(4), 1.0 / cden, None, ALU.add, None, [cl_b], [cl_b])
                tt("dve", A0(4), A0(4), A0(2), ALU.mult, [cl_b], [cl_b])
                ts("dve", A0(5), lam_j, -1.0, 0.0, ALU.mult, ALU.max, [small_b, cl_b], [cl_b])
                stt(A0(5), A0(4), 2.0, A0(5), ALU.mult, ALU.add, [cl_b], [cl_b])
                ts("dve", cl[:], A0(5), -8.0, None, ALU.mult, None, [cl_b], [cl_b])

                XRb = {("c", 0): Buf("xrc")}
                XCb, H1b, SGb_d = {}, {}, {}
                for (sq, t0, nt) in tiles:
                    XRb[(sq, t0)] = Buf(f"xr{sq}{t0}")
                    XCb[(sq, t0)] = Buf(f"xc{sq}{t0}")
                    H1b[(sq, t0)] = Buf(f"h1{sq}{t0}")
                    SGb_d[(sq, t0)] = Buf(f"sg{sq}{t0}")
                pad_b = {("c", "L"): Buf("padcL"), ("c", "R"): Buf("padcR"), ("l", "L"): Buf("padlL"), ("l", "R"): Buf("padlR")}
                for s in range(5):
                    dma_pool(wslot(s).rearrange("p (k c) -> p k c", k=8),
                             lru_w_in[j][:, s * 512:(s + 1) * 512].rearrange("(k p) c -> p k c", p=128), [], [Wb[s]])
                GW = W[:, 5 * 4096:5 * 4096 + 40 * 128].rearrange("p (d g c j) -> p d g c j", d=2, g=2, c=NCH)
                for d in range(2):
                    dma_pool(GW[:, d, 0], lru_wa[j, d].rearrange("c i j -> i c j"), [], [Wb[5], Wb[6]])
                    dma_pool(GW[:, d, 1], lru_wx[j, d].rearrange("c i j -> i c j"), [], [Wb[5], Wb[6]])
                z3 = zeros_f[:, 0:20].rearrange("p (c t) -> p c t", t=2)
                for sq, ln in (("c", CTX), ("l", NL)):
                    dma_pool(XR[sq][:, :, 0:2].rearrange("c p t -> p c t"), z3, [cst_b], [pad_b[(sq, "L")]])
                    if sq == "c" or not PAIR:
                        dma_pool(XR[sq][:, :, ln + 2:ln + 4].rearrange("c p t -> p c t"), z3, [cst_b], [pad_b[(sq, "R")]])

                GS = 5
                hp1_cb = [Buf(f"hp1_{c}") for c in range(NCH)]
                hp2_cb = [Buf(f"hp2_{c}") for c in range(NCH)]

                def gates_p1(R, d, cc, nt, xc, xcb):
                    xb16, xb16b = R["xcb"].next()
                    copy("act", xb16[:, 0:nt], xc[:, 0:nt], [xcb], [xb16b])
                    pr, prb = ps_o[cc % 2], ps_o_b[cc % 2]
                    pg, pgb = ps_su[cc % 2], ps_su_b[cc % 2]
                    mm_group([(pr[:, 0:nt], GW[:, d, 0, cc, :], xb16[:, 0:nt], True, True)], [xb16b, Wb[5], Wb[6]], [prb])
                    mm_group([(pg[:, 0:nt], GW[:, d, 1, cc, :], xb16[:, 0:nt], True, True)], [xb16b, Wb[5], Wb[6]], [pgb])
                    r_, rb = R["r"].next()
                    gi, gib = R["gi"].next()
                    bcol = (j * 2 + d) * NCH + cc
                    act(r_[:, 0:nt], pr[:, 0:nt], AF.Sigmoid, [prb, small_b], [rb], bias=ba_t[:, bcol:bcol + 1])
                    act(gi[:, 0:nt], pg[:, 0:nt], AF.Sigmoid, [pgb, small_b], [gib], bias=bx_t[:, bcol:bcol + 1])
                    return dict(cc=cc, xc=xc, xcb=xcb, r=r_, rb=rb, gi=gi, gib=gib)

                def gates_rest(R, d, nt, cx):
                    for c_ in cx:
                        c_["a"], c_["ab"] = R["a"].next()
                        cc = c_["cc"]
                        act(c_["a"][:, 0:nt], c_["r"][:, 0:nt], AF.Exp, [c_["rb"], cl_b], [c_["ab"]], scale=cl[:, d * NCH + cc:d * NCH + cc + 1])
                    for c_ in cx:
                        c_["q"], c_["qb"] = R["q"].next()
                        tt("pool", c_["q"][:, 0:nt], c_["a"][:, 0:nt], c_["a"][:, 0:nt], ALU.mult, [c_["ab"]], [c_["qb"]])
                    for c_ in cx:
                        act(c_["q"][:, 0:nt], c_["q"][:, 0:nt], AF.Sqrt, [c_["qb"], cst_b], [c_["qb"]], scale=-1.0, bias=one_col)
                    for c_ in cx:
                        c_["u"], c_["ub"] = R["u"].next()
                        tt("pool", c_["u"][:, 0:nt], c_["q"][:, 0:nt], c_["gi"][:, 0:nt], ALU.mult, [c_["qb"], c_["gib"]], [c_["ub"]])
                        tt("dve", c_["u"][:, 0:nt], c_["u"][:, 0:nt], c_["xc"][:, 0:nt], ALU.mult, [c_["ub"], c_["xcb"]], [c_["ub"]])


                pes = contextlib.ExitStack()
                with pes:
                    R0 = {
                        "x": Ring(nc, pes, "l0x", [128, D], F32, 2), "junk": Ring(nc, pes, "l0j", [128, D], F32, 1),
                        "stat": Ring(nc, pes, "l0s", [128, 4], F32, 2), "xhat": Ring(nc, pes, "l0xh", [128, D], BF16, 2),
                        "hT": Ring(nc, pes, "l0hT", [128, 8 * 512], BF16, 2), "xr": Ring(nc, pes, "l0xr", [128, 512], F32, 2),
                        "sg": Ring(nc, pes, "l0sg", [128, 512], BF16, 2),
                        "halo": Ring(nc, pes, "l0halo", [128, 64], F32, 2),
                    }
                    R0["xg"] = R0["x"]

                    def L0_tile(sq, t0, nt):
                        R = R0
                        base = 0 if sq == "c" else CTX
                        v = 1 if sq == "c" else 0
                        hT, hTb = R["hT"].next()
                        hv = hT[:].rearrange("p (k t) -> p k t", k=8)
                        for c in range(nt // 128):
                            norm_chunk(R, src_ap, src_buf, base + t0 + c * 128, v, lambda k, hv=hv, c=c: hv[:, k, c * 128:(c + 1) * 128], hTb)
                        for cc in range(2 * NCH):
                            pp, ppb = ps_proj[cc % 2], ps_proj_b[cc % 2]
                            wv = wslot(cc // 4).rearrange("p (k c) -> p k c", k=8)
                            mm_group([(pp[:, 0:nt], wv[:, k, (cc % 4) * 128:(cc % 4 + 1) * 128], hv[:, k, 0:nt], k == 0, k == 7) for k in range(8)],
                                     [hTb, Wb[cc // 4]], [ppb])
                            if cc < NCH:
                                xr, xrb = R["xr"].next()
                                copy("act", xr[:, 0:nt], pp[:, 0:nt], [ppb], [xrb])
                                dma_act(XR[sq][cc, :, 2 + t0:2 + t0 + nt], xr[:, 0:nt], [xrb], [XRb[(sq, t0)]])
                            else:
                                sg, sgb = R["sg"].next()
                                act(sg[:, 0:nt], pp[:, 0:nt], AF.Silu, [ppb], [sgb])
                                dma_act(SGD[sq][cc - NCH, :, t0:t0 + nt], sg[:, 0:nt], [sgb], [SGb_d[(sq, t0)]])
                    def L0_halo():
                        R = R0
                        lastb = XRb[("l", NL - 512)]
                        hl, hlb = R["halo"].next()
                        dma_sp(hl[:, 0:20].rearrange("p (c t) -> p c t", t=2), XRl[:, :, NL:NL + 2].rearrange("c p t -> p c t"), [lastb], [hlb])
                        hr, hrb = R["halo"].next()
                        hc_, hcb_ = exchange_start(l, hl[:, 0:32], hlb, 32)
                        exchange_finish(R, hc_, hcb_, hr, hrb, 32)
                        h3 = hr[:, 0:20].rearrange("p (c t) -> p c t", t=2)
                        dma_pool(XRl[:, :, NL + 2:NL + 3].rearrange("c p t -> p c t"), h3[:, :, 1:2], [hrb], [pad_b[("l", "R")]], slow=True)
                        dma_pool(XRl[:, :, NL + 3:NL + 4].rearrange("c p t -> p c t"), h3[:, :, 0:1], [hrb], [pad_b[("l", "R")]], slow=True)

                    R = {
                        "win": Ring(nc, pes, "l1w", [128, 516], F32, 5), "xc": Ring(nc, pes, "l1xc", [128, 512], F32, 5),
                        "xcb": Ring(nc, pes, "l1xcb", [128, 512], BF16, 2), "r": Ring(nc, pes, "l1r", [128, 512], F32, 5),
                        "gi": Ring(nc, pes, "l1gi", [128, 512], F32, 5), "a": Ring(nc, pes, "l1a", [128, 512], F32, 5),
                        "q": Ring(nc, pes, "l1q", [128, 512], F32, 5), "u": Ring(nc, pes, "l1u", [128, 512], F32, 5),
                        "h": Ring(nc, pes, "l1h", [128, 512], F32, 3),
                    }
                    S.add("dve", "c", [lambda e: e.memset(hprev[:], 0.0)], [], hp1_cb + hp2_cb)
                    def L1_tile(sq, t0, nt):
                        ln = CTX if sq == "c" else NL
                        rd = [XRb[(sq, t0)]]
                        if t0 >= 512:
                            rd.append(XRb[(sq, t0 - 512)])
                        else:
                            rd.append(pad_b[(sq, "L")])
                        if t0 + nt < ln:
                            rd.append(XRb[(sq, t0 + nt)])
                        else:
                            rd.append(pad_b[(sq, "R")])
                        for g0 in range(0, NCH, GS):
                            cx = []
                            wins = {}
                            for cc in range(g0, g0 + GS):
                                win, winb = R["win"].next()
                                dma_sp(win[:, 0:nt + 4], XR[sq][cc, :, t0:t0 + nt + 4], rd, [winb])
                                wins[cc] = (win, winb)
                            for cc in range(g0, g0 + GS):
                                win, winb = wins[cc]
                                xc, xcb = R["xc"].next()
                                wc = lambda tap, cc=cc: cw_t[:, (j * NCH + cc) * 5 + tap:(j * NCH + cc) * 5 + tap + 1]
                                ts("dve", xc[:, 0:nt], win[:, 0:nt], wc(0), cb_t[:, j * NCH + cc:j * NCH + cc + 1], ALU.mult, ALU.add,
                                   [winb, small_b], [xcb])
                                for tap in range(1, 5):
                                    stt(xc[:, 0:nt], win[:, tap:tap + nt], wc(tap), xc[:, 0:nt], ALU.mult, ALU.add, [winb, small_b, xcb], [xcb])
                                dma_sp(XC[sq][cc, :, t0:t0 + nt], xc[:, 0:nt], [xcb], [XCb[(sq, t0)]])
                                cx.append(gates_p1(R, 0, cc, nt, xc, xcb))
                            gates_rest(R, 0, nt, cx)
                            for c_ in cx:
                                cc, a_, ab, u_, ub = c_["cc"], c_["a"], c_["ab"], c_["u"], c_["ub"]
                                h_, hb = R["h"].next()
                                S.add("dve", "c", [lambda e, h_=h_, a_=a_, u_=u_, cc=cc, nt=nt: e.tensor_tensor_scan(
                                    out=h_[:, 0:nt], data0=a_[:, 0:nt], data1=u_[:, 0:nt], initial=hprev[:, cc:cc + 1], op0=ALU.mult, op1=ALU.add)],
                                    [ab, ub, hp1_cb[cc]], [hb])
                                copy("act", hprev[:, cc:cc + 1], h_[:, nt - 1:nt], [hb], [hp1_cb[cc]])
                                dma_pool(H1[sq][cc, :, t0:t0 + nt], h_[:, 0:nt], [hb], [H1b[(sq, t0)]])

                    NT = len(tiles)
                    for ti in range(NT + 2):
                        if ti < NT:
                            L0_tile(*tiles[ti])
                            if ti == NT - 1 and PAIR:
                                L0_halo()
                        if 2 <= ti:
                            L1_tile(*tiles[ti - 2])
                    S.barrier()

                pes = contextlib.ExitStack()
                with pes:
                    R = {
                        "xc": Ring(nc, pes, "l2xc", [128, 512], F32, 5),
                        "xcb": Ring(nc, pes, "l2xcb", [128, 512], BF16, 2), "r": Ring(nc, pes, "l2r", [128, 512], F32, 5),
                        "gi": Ring(nc, pes, "l2gi", [128, 512], F32, 5), "a": Ring(nc, pes, "l2a", [128, 512], F32, 5),
                        "q": Ring(nc, pes, "l2q", [128, 512], F32, 5), "u": Ring(nc, pes, "l2u", [128, 512], F32, 5),
                        "h": Ring(nc, pes, "l2h", [128, 512], F32, 2), "h1": Ring(nc, pes, "l2h1", [128, 512], F32, 2),
                        "sg": Ring(nc, pes, "l2sg", [128, 512], BF16, 2), "Z": Ring(nc, pes, "l2Z", [128, NCH * 512], BF16, 1),
                        "x": Ring(nc, pes, "l2x", [128, D], F32, 2), "junk": Ring(nc, pes, "l2j", [128, D], F32, 1),
                        "stat": Ring(nc, pes, "l2s", [128, 4], F32, 2), "tmp": Ring(nc, pes, "l2tmp", [128, D], F32, 1),
                        "xn": Ring(nc, pes, "l2xn", [128, D], F32, 2),
                    }
                    R["xg"] = R["xn"]
                    Wo = W[:, 0:NCH * 1024].rearrange("p (k c) -> p k c", k=NCH)
                    dma_pool(Wo, lru_w_out[j].rearrange("(k p) c -> p k c", p=128), [], [Wb[0], Wb[1], Wb[2]])
                    hp2 = hprev[:, NCH:2 * NCH]
                    if PAIR:
                        pst_ = pes.enter_context(nc.sbuf_tensor(f"lpstate{l}", [128, 32], F32))
                        pst_b = Buf("lpstate")
                        hx, hxb = R["xg"].next()
                        copy("dve", hx[:, 0:NCH], hprev[:, 0:NCH], hp1_cb, [hxb])
                        sc_, scb_ = exchange_start(l - 1, hx[:, 0:32], hxb, 32)
                        exchange_finish(R, sc_, scb_, pst_, pst_b, 32)
                    order = [tiles[0]] + tiles[:0:-1]
                    for (sq, t0, nt) in order:
                        base = 0 if sq == "c" else CTX
                        v = 1 if sq == "c" else 0
                        if PAIR and sq == "l" and t0 == NL - 512:
                            copy("dve", hp2, pst_[:, 0:NCH], [pst_b] + hp2_cb, hp2_cb)
                        skip_out = last and sq == "c" and not dbg
                        Z, Zb = R["Z"].next()
                        zv = Z[:].rearrange("p (c t) -> p c t", c=NCH)
                        for g0 in range(0, NCH, GS):
                            cx = []
                            for cc in range(g0, g0 + GS):
                                xc, xcb = R["xc"].next()
                                dma_sp(xc[:, 0:nt], XC[sq][cc, :, t0:t0 + nt], [XCb[(sq, t0)]], [xcb])
                                cx.append(gates_p1(R, 1, cc, nt, xc, xcb))
                            gates_rest(R, 1, nt, cx)
                            for c_ in cx:
                                cc, a_, ab, u_, ub = c_["cc"], c_["a"], c_["ab"], c_["u"], c_["ub"]
                                h_, hb = R["h"].next()
                                S.add("dve", "c", [lambda e, h_=h_, a_=a_, u_=u_, cc=cc, nt=nt: e.tensor_tensor_scan(
                                    out=h_[:, 0:nt][:, ::-1], data0=a_[:, 0:nt][:, ::-1], data1=u_[:, 0:nt][:, ::-1],
                                    initial=hp2[:, cc:cc + 1], op0=ALU.mult, op1=ALU.add)], [ab, ub, hp2_cb[cc]], [hb])
                                copy("act", hp2[:, cc:cc + 1], h_[:, 0:1], [hb], [hp2_cb[cc]])
                                if skip_out:
                                    continue
                                h1, h1b = R["h1"].next()
                                dma_sp(h1[:, 0:nt], H1[sq][cc, :, t0:t0 + nt], [H1b[(sq, t0)]], [h1b])
                                sg, sgb = R["sg"].next()
                                dma_sp(sg[:, 0:nt], SGD[sq][cc, :, t0:t0 + nt], [SGb_d[(sq, t0)]], [sgb])
                                tt("pool", h1[:, 0:nt], h1[:, 0:nt], h_[:, 0:nt], ALU.add, [h1b, hb], [h1b])
                                tt("dve", zv[:, cc, 0:nt], h1[:, 0:nt], sg[:, 0:nt], ALU.mult, [h1b, sgb], [Zb])
                        if skip_out:
                            continue
                        for c in range(nt // 128):
                            row0 = base + t0 + c * 128
                            xt, xb = R["x"].next()
                            dma_sp(xt[:], src_ap[row0:row0 + 128, :], [src_buf], [xb])
                            for g in range(2):
                                mm_group([(ps_proj[g][:], zv[:, cc, c * 128:(c + 1) * 128], Wo[:, cc, g * 512:(g + 1) * 512], cc == 0, cc == NCH - 1)
                                          for cc in range(NCH)], [Zb, Wb[0], Wb[1], Wb[2]], [ps_proj_b[g]])
                            if last and v == 1:
                                post_chunk(R, xt, xb, v, dbg_ctx, dst_buf, row0)
                            elif last:
                                post_chunk(R, xt, xb, v, out, dst_buf, row0 - CTX)
                            else:
                                post_chunk(R, xt, xb, v, dst_ap, dst_buf, row0)
                    S.barrier()

        xin_b = Buf("xin")
        Xb = [Buf("Xs0"), Buf("Xs1")]
        out_b = Buf("out")
        src_ap, src_buf = xin, xin_b
        for l in range(depth):
            last = l == depth - 1
            dst_ap, dst_buf = (Xs[l % 2], Xb[l % 2])
            if last:
                dst_buf = out_b
            if l % 2 == 0:
                retention_layer(l, src_ap, src_buf, dst_ap, dst_buf, last)
            else:
                lru_layer(l, src_ap, src_buf, dst_ap, dst_buf, last)
            src_ap, src_buf = dst_ap, dst_buf
            if not last:
                S.rotate()
        S.final_wait("sp")

        block = es.enter_context(nc.Block())

        @block.tensor
        def _(e):
            S.emit("pe", e)

        @block.scalar
        def _(e):
            S.emit("act", e)

        @block.vector
        def _(e):
            S.emit("dve", e)

        @block.gpsimd
        def _(e):
            S.emit("pool", e)

        @block.sync
        def _(e):
            S.emit("sp", e)
    return nc


def _col(vec, nk):
    return np.ascontiguousarray(np.asarray(vec, np.float32).reshape(nk, 128).T)


def _rope_tables():
    n_rows = SEQ // GRID_W
    row = np.repeat(np.arange(n_rows, dtype=np.float32), GRID_W)
    col = np.tile(np.arange(GRID_W, dtype=np.float32), n_rows)
    n_freq = 64
    inv = (np.float32(10000.0) ** (-np.arange(n_freq, dtype=np.float32) / np.float32(n_freq))).astype(np.float32)
    ang = np.concatenate([row[:, None] * inv, col[:, None] * inv], axis=-1).astype(np.float32)
    return np.cos(ang).astype(np.float32), np.sin(ang).astype(np.float32)


def _core_inputs(core, inp, shared):
    if PAIR:
        b, half = core // 2, core % 2
    else:
        b, half = core, 0
    flip = half == 1
    dirs = (1, 0) if flip else (0, 1)
    x = inp["x"][b]
    ctx = inp["ctx"][b]
    cos, sin = shared["rope"]
    if PAIR:
        xs = x[half * NL:(half + 1) * NL]
        cs, sn = cos[half * NL:(half + 1) * NL], sin[half * NL:(half + 1) * NL]
    else:
        xs, cs, sn = x, cos, sin
    if flip:
        xs, cs, sn, ctx = xs[::-1], cs[::-1], sn[::-1], ctx[::-1]
    m = {}
    m["xin"] = np.ascontiguousarray(np.concatenate([ctx, xs], 0), dtype=np.float32)
    m["ropec"] = np.ascontiguousarray(np.concatenate([np.ones((CTX, 128), np.float32), cs], 0))
    m["ropes"] = np.ascontiguousarray(np.concatenate([np.zeros((CTX, 128), np.float32), sn], 0))
    cc = np.stack([_col(inp["c"][b], 8), _col(inp["c_ctx"], 8)], -1).reshape(128, 16)
    m["c_col"] = np.ascontiguousarray(cc)
    lg = np.asarray(inp["ret_log_decay"], np.float32)[:, list(dirs), :]
    m["ret_lg"] = np.ascontiguousarray(np.broadcast_to(lg.reshape(1, 16), (128, 16)))
    sel_d = list(dirs)
    m["lru_wa"] = np.ascontiguousarray(np.asarray(inp["lru_w_a"], np.float32)[:, sel_d])
    m["lru_wx"] = np.ascontiguousarray(np.asarray(inp["lru_w_x"], np.float32)[:, sel_d])
    def dcol(a):
        a = np.asarray(a, np.float32)[:, sel_d]
        return np.ascontiguousarray(np.concatenate([_col(a[j, d], NCH) for j in range(2) for d in range(2)], 1))
    m["lru_ba"] = dcol(inp["lru_b_a"])
    m["lru_bx"] = dcol(inp["lru_b_x"])
    m["lru_lam"] = dcol(inp["lru_lambda"])
    cw = np.asarray(inp["lru_conv_w"], np.float32)
    z = np.zeros_like(cw[:, :1])
    cw5 = np.concatenate([z, cw[:, ::-1]], 1) if flip else np.concatenate([cw, z], 1)
    cwc = np.stack([np.stack([_col(cw5[j, t], NCH) for t in range(5)], -1) for j in range(2)], 1)
    m["lru_cw"] = np.ascontiguousarray(cwc.reshape(128, 100))
    cst = shared["consts"].copy()
    if PAIR:
        cst[:, 649] = 1.0 if half == 1 else 0.0
        cst[:, 650] = 1.0 if half == 0 else 0.0
    m["consts"] = cst
    for k in ("mod_w", "mod_b_col", "npre_col", "npost_col", "ret_w_in", "ret_gn_col", "ret_w_out", "lru_w_in", "lru_cb", "lru_w_out"):
        m[k] = shared[k]
    return m


def _shared_inputs(inp):
    sh = {}
    sh["rope"] = _rope_tables()
    sh["mod_w"] = np.ascontiguousarray(inp["mod_w"], dtype=np.float32)
    sh["mod_b_col"] = np.ascontiguousarray(np.concatenate([_col(inp["mod_b"][l], 24) for l in range(DEPTH)], 1))
    sh["npre_col"] = np.ascontiguousarray(np.concatenate([_col(inp["norm_pre"][l], 8) for l in range(DEPTH)], 1))
    sh["npost_col"] = np.ascontiguousarray(np.concatenate([_col(inp["norm_post"][l], 8) for l in range(DEPTH)], 1))
    perm = np.arange(6144)
    for blk in range(2):
        for h in range(RET_H):
            base = blk * 1024 + h * 256
            perm[base:base + 256] = np.concatenate([base + np.arange(0, 256, 2), base + np.arange(1, 256, 2)])
    sh["ret_w_in"] = np.ascontiguousarray(np.asarray(inp["ret_w_in"], np.float32)[:, :, perm])
    sh["ret_gn_col"] = np.ascontiguousarray(np.concatenate([_col(inp["ret_gn"][j], 16) for j in range(2)], 1))
    sh["ret_w_out"] = np.ascontiguousarray(inp["ret_w_out"], dtype=np.float32)
    sh["lru_w_in"] = np.ascontiguousarray(inp["lru_w_in"], dtype=np.float32)
    sh["lru_cb"] = np.ascontiguousarray(np.concatenate([_col(inp["lru_conv_b"][j], NCH) for j in range(2)], 1))
    sh["lru_w_out"] = np.ascontiguousarray(inp["lru_w_out"], dtype=np.float32)
    cst = np.zeros((128, 128 * 5 + 16), np.float32)
    p = np.arange(128, dtype=np.float32)
    cst[:, 0:128] = np.eye(128, dtype=np.float32)
    cst[:, 128:256] = 1.0
    cst[:, 256:384] = (p[:, None] <= p[None, :])
    cst[:, 384:512] = (p[:, None] >= p[None, :])
    coefs = np.stack([127 - p, p + 1, -(p + 1), p, 128 - p, p - 128, np.full(128, 128.0, np.float32)], 1)
    cst[:, 640:647] = coefs
    cst[:, 647] = EPS
    cst[:, 648] = 1.0
    sh["consts"] = cst
    return sh


_NC_CACHE = {}


def kernel(**inputs):
    inp = {k: np.asarray(v) for k, v in inputs.items()}
    if "nc" not in _NC_CACHE:
        _NC_CACHE["nc"] = build_program()
    nc = _NC_CACHE["nc"]
    shared = _shared_inputs(inp)
    in_maps = [_core_inputs(c, inp, shared) for c in range(NCORES)]
    res = run_bass_kernel_spmd(nc, in_maps, core_ids=list(range(NCORES)))
    outp = np.empty((BATCH, SEQ, D), np.float32)
    for c in range(NCORES):
        o = np.asarray(res.results[c]["out"], np.float32)
        if PAIR:
            b, half = c // 2, c % 2
            outp[b, half * NL:(half + 1) * NL] = o[::-1] if half == 1 else o
        else:
            outp[c] = o
    return outp
```
